# Optimizing a Trainium2 kernel written in Bass

```python
import math
import jax
import jax.numpy as jnp
from jax import lax
import numpy as np

D_MODEL = 1024
BATCH = 16
SEQ = 2048
DEPTH = 2

GRID_W = 64
CTX_LEN = 256
N_EVEN = (DEPTH + 1) // 2
N_ODD = DEPTH // 2
N_MOD = 6
D_FF = 4 * D_MODEL
HEAD_DIM = D_MODEL // 16
MLA_HEADS = 8
MLA_NOPE = HEAD_DIM
MLA_ROPE = HEAD_DIM // 2
MLA_V = HEAD_DIM
MLA_Q_LORA = 3 * D_MODEL // 8
MLA_KV_LORA = D_MODEL // 4
GQA_HEADS = 8
GQA_KV_HEADS = 2
GQA_DIM = HEAD_DIM
ATTN_PARTS = (MLA_Q_LORA, MLA_KV_LORA, MLA_ROPE,
              GQA_HEADS * GQA_DIM, GQA_KV_HEADS * GQA_DIM, GQA_KV_HEADS * GQA_DIM)
ATTN_IN = sum(ATTN_PARTS)
ATTN_SPLITS = tuple(sum(ATTN_PARTS[:i + 1]) for i in range(len(ATTN_PARTS) - 1))
MIX_WIDTH = MLA_HEADS * MLA_V + GQA_HEADS * GQA_DIM
ROPE_THETA = 10000.0
Q_BLOCK = 128
HYENA_ORDER = 2
HYENA_SHORT = 3
HYENA_BANDS = 8
HYENA_EMB = 1 + 2 * HYENA_BANDS
HYENA_FILTER_HIDDEN = 64
HYENA_FAST_DECAY = 0.3
HYENA_SLOW_DECAY = 1.5
HYENA_TARGET = 1e-2
EPS = 1e-6

kernel_name = "hybrid_mla_gqa_hyena_dit"

F32 = jnp.float32


def rmsnorm(x, g):
    xf = x.astype(F32)
    y = xf * lax.rsqrt(jnp.mean(xf * xf, axis=-1, keepdims=True) + EPS)
    return (y * g.astype(F32)).astype(x.dtype)


def modulate(x, g, shift, scale):
    return rmsnorm(x, g) * (1 + scale) + shift


def _rotate(x, ang):
    n = ang.shape[-1]
    cos = jnp.cos(ang)[None, :, None, :]
    sin = jnp.sin(ang)[None, :, None, :]
    a = x[..., :n].astype(F32)
    b = x[..., n:].astype(F32)
    return jnp.concatenate([a * cos - b * sin, a * sin + b * cos], axis=-1).astype(x.dtype)


def rope_2d(x, row, col):
    half = x.shape[-1] // 2
    nf = half // 2
    inv = ROPE_THETA ** (-jnp.arange(nf, dtype=F32) / nf)
    ang_r = row.astype(F32)[:, None] * inv[None]
    ang_c = col.astype(F32)[:, None] * inv[None]
    return jnp.concatenate([_rotate(x[..., :half], ang_r), _rotate(x[..., half:], ang_c)], axis=-1)


def block_attention(q, k, v):
    B, S, Hq, dk = q.shape
    Hk, dv = k.shape[2], v.shape[-1]
    G = Hq // Hk
    nb = S // Q_BLOCK
    scale = 1.0 / math.sqrt(dk)
    qb = q.reshape(B, nb, Q_BLOCK, Hk, G, dk).transpose(1, 0, 2, 3, 4, 5)

    def one(qblk):
        s = jnp.einsum('bqhgd,bkhd->bhgqk', qblk, k, preferred_element_type=F32) * scale
        p = jax.nn.softmax(s, axis=-1).astype(v.dtype)
        return jnp.einsum('bhgqk,bkhe->bqhge', p, v)

    o = lax.map(one, qb)
    return o.transpose(1, 0, 2, 3, 4, 5).reshape(B, S, Hq, dv)


def attn_heads(z, q_a_g, w_q_b, kv_a_g, w_kv_b, q_norm_g, k_norm_g, pos):
    B, L, _ = z.shape
    cq, ckv, kr, gq, gk, gv = jnp.split(z, ATTN_SPLITS, axis=-1)
    q = (rmsnorm(cq, q_a_g) @ w_q_b).reshape(B, L, MLA_HEADS, MLA_NOPE + MLA_ROPE)
    q_nope, q_rope = q[..., :MLA_NOPE], q[..., MLA_NOPE:]
    kv = (rmsnorm(ckv, kv_a_g) @ w_kv_b).reshape(B, L, MLA_HEADS, MLA_NOPE + MLA_V)
    k_nope, v_mla = kv[..., :MLA_NOPE], kv[..., MLA_NOPE:]
    k_rope = kr[:, :, None, :]
    gq = rmsnorm(gq.reshape(B, L, GQA_HEADS, GQA_DIM), q_norm_g)
    gk = rmsnorm(gk.reshape(B, L, GQA_KV_HEADS, GQA_DIM), k_norm_g)
    gv = gv.reshape(B, L, GQA_KV_HEADS, GQA_DIM)
    if pos is not None:
        row, col = pos
        q_rope = rope_2d(q_rope, row, col)
        k_rope = rope_2d(k_rope, row, col)
        gq = rope_2d(gq, row, col)
        gk = rope_2d(gk, row, col)
    q_mla = jnp.concatenate([q_nope, q_rope], axis=-1)
    k_mla = jnp.concatenate([k_nope, jnp.broadcast_to(k_rope, (B, L, MLA_HEADS, MLA_ROPE))], axis=-1)
    return q_mla, k_mla, v_mla, gq, gk, gv


def attention_mixer(hl, hc, w_in, q_a_g, w_q_b, kv_a_g, w_kv_b, q_norm_g, k_norm_g, w_out,
                    row, col, need_ctx):
    B, S, _ = hl.shape
    lat = attn_heads(hl @ w_in, q_a_g, w_q_b, kv_a_g, w_kv_b, q_norm_g, k_norm_g, (row, col))
    cx = attn_heads(hc @ w_in, q_a_g, w_q_b, kv_a_g, w_kv_b, q_norm_g, k_norm_g, None)
    k_mla = jnp.concatenate([cx[1], lat[1]], axis=1)
    v_mla = jnp.concatenate([cx[2], lat[2]], axis=1)
    k_gqa = jnp.concatenate([cx[4], lat[4]], axis=1)
    v_gqa = jnp.concatenate([cx[5], lat[5]], axis=1)
    o_lat = jnp.concatenate([
        block_attention(lat[0], k_mla, v_mla).reshape(B, S, MLA_HEADS * MLA_V),
        block_attention(lat[3], k_gqa, v_gqa).reshape(B, S, GQA_HEADS * GQA_DIM)], axis=-1) @ w_out
    o_ctx = None
    if need_ctx:
        Lc = hc.shape[1]
        o_ctx = jnp.concatenate([
            block_attention(cx[0], cx[1], cx[2]).reshape(B, Lc, MLA_HEADS * MLA_V),
            block_attention(cx[3], cx[4], cx[5]).reshape(B, Lc, GQA_HEADS * GQA_DIM)], axis=-1) @ w_out
    return o_lat, o_ctx


def short_conv(z, w, b):
    L = z.shape[1]
    p = HYENA_SHORT // 2
    zp = jnp.pad(z, ((0, 0), (p, p), (0, 0)))
    out = b
    for j in range(HYENA_SHORT):
        out = out + zp[:, j:j + L] * w[j]
    return out


def hyena_filters(L, f_w1, f_b1, f_w2, f_b2, f_w3, f_b3, f_freq, f_w4):
    t = jnp.arange(L, dtype=F32)
    t_norm = t / L
    w = 2.0 * math.pi * t / L
    bands = jnp.linspace(1e-4, HYENA_BANDS - 1, HYENA_BANDS, dtype=F32)
    fw = w[:, None] * bands[None]
    feats = jnp.concatenate([t_norm[:, None], jnp.cos(fw), -jnp.sin(fw)], axis=-1)
    fr = f_freq.astype(F32)
    h = jnp.sin(fr[0] * (feats @ f_w1.astype(F32) + f_b1.astype(F32)))
    h = jnp.sin(fr[1] * (h @ f_w2.astype(F32) + f_b2.astype(F32)))
    h = jnp.sin(fr[2] * (h @ f_w3.astype(F32) + f_b3.astype(F32)))
    h = (h @ f_w4.astype(F32)).reshape(L, 2, HYENA_ORDER, D_MODEL)
    max_decay = math.log(HYENA_TARGET) / HYENA_FAST_DECAY
    min_decay = math.log(HYENA_TARGET) / HYENA_SLOW_DECAY
    deltas = jnp.abs(jnp.linspace(min_decay, max_decay, D_MODEL, dtype=F32))
    window = jnp.exp(-t_norm[:, None] * deltas[None])
    h = h * window[:, None, None, :]
    h = h * lax.rsqrt(jnp.sum(h * h, axis=(0, 1), keepdims=True) + EPS)
    return h[:, 0], h[:, 1]


def bidir_long_conv(u, hf, hb, skip):
    L = u.shape[1]
    kfull = jnp.concatenate([hf, jnp.zeros_like(hf[:1]), hb[:0:-1]], axis=0)
    K = jnp.fft.rfft(kfull, n=2 * L, axis=0)
    uf = u.astype(F32)
    U = jnp.fft.rfft(uf, n=2 * L, axis=1)
    y = jnp.fft.irfft(U * K[None], n=2 * L, axis=1)[:, :L]
    return (y + uf * skip.astype(F32)).astype(u.dtype)


def hyena_mixer(h, w_in, conv_w, conv_b, f_w1, f_b1, f_w2, f_b2, f_w3, f_b3, f_freq, f_w4,
                skip, w_out):
    L = h.shape[1]
    z = short_conv(h @ w_in, conv_w, conv_b)
    x1, x2, v = jnp.split(z, 3, axis=-1)
    hf, hb = hyena_filters(L, f_w1, f_b1, f_w2, f_b2, f_w3, f_b3, f_freq, f_w4)
    y = v
    for n, gate in enumerate((x1, x2)):
        y = gate * bidir_long_conv(y, hf[:, n], hb[:, n], skip[n])
    return y @ w_out


def sq_relu_mlp(h, w1, w2):
    return jnp.square(jax.nn.relu(h @ w1)) @ w2


def setup_inputs(seed: int = 0) -> dict:
    key = jax.random.key(seed)
    ks = list(jax.random.split(key, 40))

    def nrm(shape, scale):
        return jax.random.normal(ks.pop(), shape, F32) * scale

    def gain(shape):
        return 1.0 + nrm(shape, 0.02)

    D = D_MODEL
    return {
        'x': nrm((BATCH, SEQ, D), 1.0),
        'c': nrm((BATCH, D), 1.0),
        'ctx': nrm((BATCH, CTX_LEN, D), 1.0),
        'c_ctx': nrm((D,), 1.0),
        'w_mod': nrm((DEPTH, D, N_MOD * D), 0.5 * D ** -0.5),
        'b_mod': nrm((DEPTH, N_MOD * D), 0.02),
        'norm1_g': gain((DEPTH, D)),
        'norm2_g': gain((DEPTH, D)),
        'mlp_w1': nrm((DEPTH, D, D_FF), D ** -0.5),
        'mlp_w2': nrm((DEPTH, D_FF, D), D_FF ** -0.5),
        'a_w_in': nrm((N_EVEN, D, ATTN_IN), D ** -0.5),
        'a_q_a_g': gain((N_EVEN, MLA_Q_LORA)),
        'a_w_q_b': nrm((N_EVEN, MLA_Q_LORA, MLA_HEADS * (MLA_NOPE + MLA_ROPE)), MLA_Q_LORA ** -0.5),
        'a_kv_a_g': gain((N_EVEN, MLA_KV_LORA)),
        'a_w_kv_b': nrm((N_EVEN, MLA_KV_LORA, MLA_HEADS * (MLA_NOPE + MLA_V)), MLA_KV_LORA ** -0.5),
        'a_q_norm_g': gain((N_EVEN, GQA_DIM)),
        'a_k_norm_g': gain((N_EVEN, GQA_DIM)),
        'a_w_out': nrm((N_EVEN, MIX_WIDTH, D), MIX_WIDTH ** -0.5),
        'h_w_in': nrm((N_ODD, D, 3 * D), D ** -0.5),
        'h_conv_w': nrm((N_ODD, HYENA_SHORT, 3 * D), HYENA_SHORT ** -0.5),
        'h_conv_b': nrm((N_ODD, 3 * D), 0.02),
        'h_f_w1': nrm((N_ODD, HYENA_EMB, HYENA_FILTER_HIDDEN), HYENA_EMB ** -0.5),
        'h_f_b1': nrm((N_ODD, HYENA_FILTER_HIDDEN), 0.1),
        'h_f_w2': nrm((N_ODD, HYENA_FILTER_HIDDEN, HYENA_FILTER_HIDDEN), HYENA_FILTER_HIDDEN ** -0.5),
        'h_f_b2': nrm((N_ODD, HYENA_FILTER_HIDDEN), 0.1),
        'h_f_w3': nrm((N_ODD, HYENA_FILTER_HIDDEN, HYENA_FILTER_HIDDEN), HYENA_FILTER_HIDDEN ** -0.5),
        'h_f_b3': nrm((N_ODD, HYENA_FILTER_HIDDEN), 0.1),
        'h_f_freq': gain((N_ODD, 3, HYENA_FILTER_HIDDEN)),
        'h_f_w4': nrm((N_ODD, HYENA_FILTER_HIDDEN, 2 * HYENA_ORDER * D), HYENA_FILTER_HIDDEN ** -0.5),
        'h_skip': nrm((N_ODD, HYENA_ORDER, D), 0.3),
        'h_w_out': nrm((N_ODD, D, D), D ** -0.5),
        'final_g': gain((D,)),
    }


def reference(x, c, ctx, c_ctx, w_mod, b_mod, norm1_g, norm2_g, mlp_w1, mlp_w2,
              a_w_in, a_q_a_g, a_w_q_b, a_kv_a_g, a_w_kv_b, a_q_norm_g, a_k_norm_g, a_w_out,
              h_w_in, h_conv_w, h_conv_b, h_f_w1, h_f_b1, h_f_w2, h_f_b2, h_f_w3, h_f_b3,
              h_f_freq, h_f_w4, h_skip, h_w_out, final_g):
    S = x.shape[1]
    rows = S // GRID_W
    rr, cc = jnp.meshgrid(jnp.arange(rows, dtype=jnp.int32), jnp.arange(GRID_W, dtype=jnp.int32),
                          indexing='ij')
    row, col = rr.reshape(-1), cc.reshape(-1)
    silu_c = jax.nn.silu(c)
    silu_cc = jax.nn.silu(c_ctx)
    for i in range(DEPTH):
        ctx_live = any(j % 2 == 0 for j in range(i + 1, DEPTH))
        j = i // 2
        m_l = (silu_c @ w_mod[i] + b_mod[i])[:, None, :]
        m_c = (silu_cc @ w_mod[i] + b_mod[i])[None, None, :]
        sh1, sc1, g1, sh2, sc2, g2 = jnp.split(m_l, N_MOD, axis=-1)
        csh1, csc1, cg1, csh2, csc2, cg2 = jnp.split(m_c, N_MOD, axis=-1)
        hl = modulate(x, norm1_g[i], sh1, sc1)
        if i % 2 == 0:
            hc = modulate(ctx, norm1_g[i], csh1, csc1)
            ol, oc = attention_mixer(hl, hc, a_w_in[j], a_q_a_g[j], a_w_q_b[j], a_kv_a_g[j],
                                     a_w_kv_b[j], a_q_norm_g[j], a_k_norm_g[j], a_w_out[j],
                                     row, col, ctx_live)
        else:
            hy = (h_w_in[j], h_conv_w[j], h_conv_b[j], h_f_w1[j], h_f_b1[j], h_f_w2[j], h_f_b2[j],
                  h_f_w3[j], h_f_b3[j], h_f_freq[j], h_f_w4[j], h_skip[j], h_w_out[j])
            ol = hyena_mixer(hl, *hy)
            oc = None
            if ctx_live:
                oc = hyena_mixer(modulate(ctx, norm1_g[i], csh1, csc1), *hy)
        x = x + g1 * ol
        x = x + g2 * sq_relu_mlp(modulate(x, norm2_g[i], sh2, sc2), mlp_w1[i], mlp_w2[i])
        if ctx_live:
            ctx = ctx + cg1 * oc
            ctx = ctx + cg2 * sq_relu_mlp(modulate(ctx, norm2_g[i], csh2, csc2), mlp_w1[i], mlp_w2[i])
    return rmsnorm(x, final_g)
```

```python
import math
import numpy as np
import ml_dtypes
from contextlib import ExitStack
import concourse.bass as bass
import concourse.mybir as mybir
from concourse.bass_utils import run_bass_kernel_spmd

F32 = mybir.dt.float32
BF16 = mybir.dt.bfloat16
ALU = mybir.AluOpType
AF = mybir.ActivationFunctionType
NPBF = ml_dtypes.bfloat16

SAME_ENGINE_SYNC = True
D = 1024
NT = 4096
L = 2048
NK = 2304
EPS = 1e-6


class _Op:
    __slots__ = ("eng", "fn", "waits", "signal", "val", "key", "idx")


class Prog:
    ENGS = ("pe", "dve", "act", "pool", "sp")

    def __init__(self, nc, stack):
        self.nc = nc
        self.stack = stack
        self.streams = {e: [] for e in self.ENGS}
        self.sems = {}
        self.count = {}
        self.known = {e: {} for e in self.ENGS}
        self.res = {}
        self.base_idx = {e: 0 for e in self.ENGS}
        self.dma_n = {e: 0 for e in self.ENGS}
        for e in ("pe", "dve", "act", "pool"):
            self._sem(e)

    def _sem(self, tl):
        if tl not in self.sems:
            self.sems[tl] = self.stack.enter_context(self.nc.semaphore("s_" + str(tl)))
            self.count[tl] = 0
        return self.sems[tl]

    def _r(self, key):
        r = self.res.get(key)
        if r is None:
            r = [{}, {}]
            self.res[key] = r
        return r

    skip = False
    NPOOL = {"sp": 24, "pool": 8, "act": 4}

    def op(self, eng, fn, reads=(), writes=(), dma_key=None):
        if self.skip:
            return None
        o = _Op()
        o.eng = eng
        o.fn = fn
        o.signal = False
        o.val = None
        o.key = dma_key
        o.idx = self.base_idx[eng] + len(self.streams[eng])
        deps = {}

        def add(src):
            for tl, ev in src.items():
                if deps.get(tl, (-1,))[0] < ev[0]:
                    deps[tl] = ev
        for k in reads:
            add(self._r(k)[0])
        for k in writes:
            w, r = self._r(k)
            add(w)
            add(r)
        waits = []
        kn = self.known[eng]
        for tl, ev in deps.items():
            if tl == eng and (eng == "pe" or not SAME_ENGINE_SYNC):
                continue
            if kn.get(tl, -1) >= ev[0]:
                continue
            kn[tl] = ev[0]
            waits.append((tl, ev))
            if ev[1] is not None:
                ev[1].signal = True
        o.waits = waits
        if dma_key is not None:
            npool = self.NPOOL[eng]
            n = self.dma_n[eng]
            self.dma_n[eng] = n + 1
            mytl = "dq_%s_%d" % (eng, n % npool)
            self._sem(mytl)
            prev = self.count[mytl]
            if prev > 0 and kn.get(mytl, -1) < prev:
                kn[mytl] = prev
                waits.append((mytl, (prev, None)))
            self.count[mytl] = prev + 1
            myev = (prev + 1, None)
            o.key = mytl
        else:
            myev = (o.idx, o)
            mytl = eng
        for k in reads:
            self._r(k)[1][mytl] = myev
        for k in writes:
            r = self._r(k)
            r[0] = {mytl: myev}
            r[1] = {}
        self.streams[eng].append(o)
        return o

    def flush(self, barrier=True):
        nc = self.nc
        if barrier:
            self.wait_all_dma("sp", prefixes=("dq_sp_", "dq_act_"))
        for e in self.ENGS:
            for o in self.streams[e]:
                if o.key is None and o.signal:
                    self.count[e] += 1
                    o.val = self.count[e]
        streams = self.streams
        sems = self.sems

        def emit(eng_name, engine):
            for o in streams[eng_name]:
                for tl, ev in o.waits:
                    if ev[1] is None:
                        engine.wait_ge(sems[tl], 16 * ev[0])
                    else:
                        engine.wait_ge(sems[tl], ev[1].val)
                ins = o.fn(engine)
                if o.key is not None:
                    ins.then_inc(sems[o.key], 16)
                elif o.signal:
                    ins.then_inc(sems[eng_name], 1)

        with nc.Block() as block:
            if streams["pe"]:
                @block.tensor
                def _(t):
                    emit("pe", t)
            if streams["dve"]:
                @block.vector
                def _(v):
                    emit("dve", v)
            if streams["act"]:
                @block.scalar
                def _(s):
                    emit("act", s)
            if streams["pool"]:
                @block.gpsimd
                def _(g):
                    emit("pool", g)
            if streams["sp"]:
                @block.sync
                def _(s):
                    emit("sp", s)
        for e in self.ENGS:
            self.base_idx[e] += len(self.streams[e])
            self.streams[e] = []
        if barrier:
            nc.all_engine_barrier()
            for k, r in self.res.items():
                for d in (0, 1):
                    r[d] = {tl: ev for tl, ev in r[d].items() if ev[1] is None}
            for e in self.ENGS:
                self.known[e] = {tl: v for tl, v in self.known[e].items() if tl not in self.ENGS}

    def wait_all_dma(self, eng="sp", prefixes=("dq_",)):
        cnts = {tl: c for tl, c in self.count.items()
                if tl not in self.ENGS and c > 0 and str(tl).startswith(tuple(prefixes))}
        sems = self.sems

        def fn(engine, cnts=cnts):
            for tl, c in cnts.items():
                engine.wait_ge(sems[tl], 16 * c)
            return engine.nop()
        self.op(eng, fn)


def _fm(v, nch):
    return np.ascontiguousarray(np.asarray(v, np.float32).reshape(nch, 128).T)


def _rope_tables(hd):
    half = hd // 2
    nf = half // 2
    inv = (np.float32(10000.0) ** (-np.arange(nf, dtype=np.float32) / np.float32(nf))).astype(np.float32)
    pos = np.arange(L)
    row = (pos // 64).astype(np.float32)
    col = (pos % 64).astype(np.float32)
    cos = np.zeros((hd, L), np.float32)
    sin = np.zeros((hd, L), np.float32)
    partner = np.zeros(hd, np.int64)
    for j in range(hd):
        comp = row if j < half else col
        jj = j % half
        fi = jj % nf
        ang = (comp * inv[fi]).astype(np.float32)
        cos[j] = np.cos(ang)
        s = np.sin(ang)
        if jj < nf:
            sin[j] = -s
            partner[j] = j + nf
        else:
            sin[j] = s
            partner[j] = j - nf
    return cos, sin, partner


_CONST = None


def _constants():
    global _CONST
    if _CONST is not None:
        return _CONST
    c = {}
    cosG, sinG, partG = _rope_tables(64)
    cosM, sinM, partM = _rope_tables(32)
    c["ropeG_cos"] = np.ascontiguousarray(np.tile(cosG, (2, 1)))
    c["ropeG_sin"] = np.ascontiguousarray(np.tile(sinG, (2, 1)))
    c["ropeM_cos"] = cosM
    c["ropeM_sin"] = sinM
    c["_partG"] = partG
    c["_partM"] = partM
    c["ident_f"] = np.eye(128, dtype=np.float32)
    c["ident_b"] = np.eye(128, dtype=np.float32).astype(NPBF)
    bd = np.zeros((128, 128), np.float32)
    bd[:64, :64] = 1
    bd[64:, 64:] = 1
    c["bdones"] = bd.astype(NPBF)
    sel = np.zeros((32, 96), np.float32)
    sel[np.arange(32), np.arange(32)] = 1
    c["sel32"] = sel.astype(NPBF)
    t = np.arange(L, dtype=np.float32)
    t_norm = t / np.float32(L)
    w = (np.float32(2.0 * math.pi) * t / np.float32(L)).astype(np.float32)
    bands = np.linspace(1e-4, 7, 8, dtype=np.float32)
    fw = (w[:, None] * bands[None]).astype(np.float32)
    feats = np.concatenate([t_norm[:, None], np.cos(fw), -np.sin(fw)], axis=-1).astype(np.float32)
    c["featsT"] = np.ascontiguousarray(feats.T)
    max_decay = math.log(1e-2) / 0.3
    min_decay = math.log(1e-2) / 1.5
    deltas = np.abs(np.linspace(min_decay, max_decay, D, dtype=np.float32))
    c["window"] = np.exp(-t_norm[:, None] * deltas[None]).astype(np.float32)
    idx = np.arange(L, dtype=np.int64)
    ph = (np.outer(idx, idx) % 4096).astype(np.float64) * (2.0 * math.pi / 4096.0)
    FC = np.cos(ph)
    FS = np.sin(ph)
    FS[:, 0] = (-1.0) ** idx
    FST = FS.T.copy()

    def tile_t(M):
        return np.ascontiguousarray(M.reshape(16, 128, 16, 128).transpose(2, 1, 0, 3)).astype(NPBF)

    def tile_w(M):
        return np.ascontiguousarray(M.reshape(16, 128, 4, 512).transpose(2, 1, 0, 3)).astype(NPBF)
    c["FCt"] = tile_t(FC)
    c["FSt"] = tile_t(FS)
    c["FCw"] = tile_w(FC)
    c["FSTw"] = tile_w(FST)
    _CONST = c
    return c


def _prep(inp):
    c = _constants()
    partG, partM = c["_partG"], c["_partM"]
    sh = {k: v for k, v in c.items() if not k.startswith("_")}
    f32 = lambda a: np.ascontiguousarray(np.asarray(a, np.float32))
    sh["w_mod"] = f32(inp["w_mod"])
    sh["b_modT"] = np.ascontiguousarray(f32(inp["b_mod"]).reshape(2, 48, 128).transpose(2, 0, 1))
    sh["g_n1"] = np.ascontiguousarray(f32(inp["norm1_g"]).reshape(2, 8, 128).transpose(2, 0, 1))
    sh["g_n2"] = np.ascontiguousarray(f32(inp["norm2_g"]).reshape(2, 8, 128).transpose(2, 0, 1))
    sh["g_fin"] = _fm(inp["final_g"], 8)
    sh["mlp_w1"] = f32(inp["mlp_w1"])
    sh["mlp_w2"] = f32(inp["mlp_w2"])
    win = f32(inp["a_w_in"][0])
    cq, ckv, kr, gq, gk, gv = np.split(win, [384, 640, 672, 1184, 1312], axis=1)
    gq_sw = gq.reshape(D, 8, 64)[:, :, partG].reshape(D, 512)
    gk_sw = gk.reshape(D, 2, 64)[:, :, partG].reshape(D, 128)
    kr_sw = kr[:, partM]
    sh["a_w_in_x"] = np.ascontiguousarray(np.concatenate([cq, ckv, kr, kr_sw, gq, gq_sw, gk, gk_sw, gv], axis=1))
    wqb = f32(inp["a_w_q_b"][0]).reshape(384, 8, 96)
    nope, rope = wqb[:, :, :64], wqb[:, :, 64:]
    main = np.concatenate([rope, nope], axis=2)
    swp = rope[:, :, partM]
    sh["a_w_qb_x"] = np.ascontiguousarray(np.concatenate([main.reshape(384, 768), swp.reshape(384, 256)], axis=1))
    wkv = f32(inp["a_w_kv_b"][0]).reshape(256, 8, 128)
    knp = np.zeros((256, 8, 96), np.float32)
    knp[:, :, 32:] = wkv[:, :, :64]
    sh["a_w_kn"] = np.ascontiguousarray(knp.reshape(256, 768))
    sh["a_w_v"] = np.ascontiguousarray(wkv[:, :, 64:].reshape(256, 512))
    sh["a_w_out"] = f32(inp["a_w_out"][0])
    sh["g_qa"] = _fm(inp["a_q_a_g"][0], 3)
    sh["g_kva"] = _fm(inp["a_kv_a_g"][0], 2)
    qn = f32(inp["a_q_norm_g"][0])
    kn = f32(inp["a_k_norm_g"][0])
    sh["g_hn"] = np.ascontiguousarray(np.stack([np.tile(qn, 2), np.tile(qn[partG], 2),
                                                np.tile(kn, 2), np.tile(kn[partG], 2)], axis=1))
    sh["h_w_in"] = f32(inp["h_w_in"][0])
    sh["h_conv_wT"] = np.ascontiguousarray(f32(inp["h_conv_w"][0]).reshape(3, 24, 128).transpose(2, 1, 0))
    sh["h_conv_bT"] = _fm(inp["h_conv_b"][0], 24)
    sh["h_f_w1"] = f32(inp["h_f_w1"][0])
    sh["h_f_w2"] = f32(inp["h_f_w2"][0])
    sh["h_f_w3"] = f32(inp["h_f_w3"][0])
    sh["h_f_w4"] = f32(inp["h_f_w4"][0])
    sh["h_f_b"] = np.ascontiguousarray(np.stack([f32(inp["h_f_b1"][0]), f32(inp["h_f_b2"][0]), f32(inp["h_f_b3"][0])], axis=1))
    sh["h_f_fr"] = np.ascontiguousarray(f32(inp["h_f_freq"][0]).T)
    sh["h_skip_b"] = np.ascontiguousarray(np.broadcast_to(f32(inp["h_skip"][0]).reshape(1, 2048), (128, 2048)))
    sh["h_w_out"] = f32(inp["h_w_out"][0])
    x = f32(inp["x"])
    ctx = f32(inp["ctx"])
    cc = f32(inp["c"])
    cctx = f32(inp["c_ctx"])
    per = []
    for k in range(8):
        d = dict(sh)
        d["x"] = np.ascontiguousarray(x[2 * k:2 * k + 2].reshape(NT, D))
        d["ctx"] = np.ascontiguousarray(ctx[2 * k:2 * k + 2].reshape(512, D))
        cT = np.stack([cc[2 * k], cc[2 * k + 1], cctx], axis=1)
        d["cT"] = np.ascontiguousarray(cT.reshape(8, 128, 3).transpose(1, 0, 2))
        per.append(d)
    return per


def build(shapes, dbg=False, stage=2, phases=None):
    nc = bass.Bass("TRN2", target_bir_lowering=False)
    IN = {}
    for k, (shp, dt) in shapes.items():
        IN[k] = nc.dram_tensor(k, list(shp), BF16 if dt == "bf16" else F32, kind="ExternalInput").ap()
    OUT = nc.dram_tensor("out", [NT, D], F32, kind="ExternalOutput").ap()
    DBG = {}
    if dbg:
        for nm in ("xa0", "xm0", "xa1"):
            DBG[nm] = nc.dram_tensor("dbg_" + nm, [D, NT], F32, kind="ExternalOutput").ap()

    def scratch(name, shape, dt):
        return nc.dram_tensor(name, list(shape), dt).ap()
    XT = scratch("XT", [D, NT], F32)
    XTv = XT.rearrange("(c p) t -> p c t", p=128)
    W_in = scratch("W_in", [D, 2112], BF16)
    W_qb = scratch("W_qb", [384, 1024], BF16)
    W_kn = scratch("W_kn", [256, 768], BF16)
    W_v = scratch("W_v", [256, 512], BF16)
    W_ao = scratch("W_ao", [D, D], BF16)
    W_1 = [scratch(f"W_1_{i}", [D, 4096], BF16) for i in range(2)]
    W_2 = [scratch(f"W_2_{i}", [4096, D], BF16) for i in range(2)]
    W_hi = scratch("W_hi", [D, 3072], BF16)
    W_ho = scratch("W_ho", [D, D], BF16)
    QM = scratch("QM", [8, 96, NT], BF16)
    QG = scratch("QG", [8, 64, NT], BF16)
    KM = scratch("KM", [2, 8, 96, NK], BF16)
    KG = scratch("KG", [2, 2, 64, NK], BF16)
    VM = scratch("VM", [2, NK, 8, 128], BF16)
    VG = scratch("VG", [2, NK, 2, 128], BF16)
    X12T = scratch("X12T", [2048, NT], BF16)
    KA = scratch("KA", [2048, 2048], F32)
    KB = scratch("KB", [2048, 2048], F32)
    KD0 = scratch("KD0", [1, 2048], F32)

    with ExitStack() as top:
        P = Prog(nc, top)
        ps = [top.enter_context(nc.psum_tensor(f"ps{i}", [128, 512], F32)) for i in range(6)]
        psb = [top.enter_context(nc.psum_tensor(f"psb{i}", [128, 1024], BF16)) for i in range(2)]
        bank = [0]

        def nb():
            i = bank[0] % 6
            bank[0] += 1
            return ps[i], f"ps{i}"

        sbn = [0]

        def SB(st, name, shape, dt):
            sbn[0] += 1
            return st.enter_context(nc.sbuf_tensor("sb%d_%s" % (sbn[0], name), list(shape), dt))

        def dma(eng, out, in_, reads, writes, key):
            P.op(eng, lambda e, o=out, i=in_: e.dma_start(out=o, in_=i), reads=reads, writes=writes, dma_key=key)

        def mm(out, lhsT, rhs, start, stop, reads, writes):
            P.op("pe", lambda e, o=out, a=lhsT, b=rhs, s=start, t=stop: e.matmul(o, a, b, start=s, stop=t),
                 reads=reads, writes=writes)

        def tt(eng, out, a, b, op, reads, writes):
            P.op(eng, lambda e, o=out, x=a, y=b, p=op: e.tensor_tensor(o, x, y, p), reads=reads, writes=writes)

        def ts(eng, out, a, s1, s2, op0, op1, reads, writes):
            P.op(eng, lambda e, o=out, x=a, u=s1, v=s2, p=op0, q=op1: e.tensor_scalar(o, x, u, v, p, q),
                 reads=reads, writes=writes)

        def stt(eng, out, a, s, b, op0, op1, reads, writes):
            P.op(eng, lambda e, o=out, x=a, u=s, y=b, p=op0, q=op1: e.scalar_tensor_tensor(o, x, u, y, p, q),
                 reads=reads, writes=writes)

        def act(out, in_, func, reads, writes, bias=None, scale=None):
            kw = {}
            if bias is not None:
                kw["bias"] = bias
            if scale is not None:
                kw["scale"] = scale
            P.op("act", lambda e, o=out, i=in_, f=func, kw=kw: e.activation(o, i, f, **kw), reads=reads, writes=writes)

        def cp(eng, out, in_, reads, writes):
            if eng == "act":
                P.op("act", lambda e, o=out, i=in_: e.copy(o, i), reads=reads, writes=writes)
            else:
                P.op(eng, lambda e, o=out, i=in_: e.tensor_copy(o, i), reads=reads, writes=writes)

        def memset(eng, ap, val, writes):
            P.op(eng, lambda e, a=ap, v=val: e.memset(a, v), writes=writes)

        modT = SB(top, "modT", [128, 2, 48, 3], F32)
        A1 = SB(top, "A1", [128, 2, 8, 3], F32)
        A2 = SB(top, "A2", [128, 2, 8, 3], F32)
        gn1 = SB(top, "gn1", [128, 2, 8], F32)
        gn2 = SB(top, "gn2", [128, 2, 8], F32)
        gfin = SB(top, "gfin", [128, 8], F32)
        eps_t = SB(top, "eps_t", [128, 1], F32)
        ones_b = SB(top, "ones_b", [128, 128], BF16)
        ident_f = SB(top, "ident_f", [128, 128], F32)
        ident_b = SB(top, "ident_b", [128, 128], BF16)
        memset("dve", eps_t[:], EPS, ["eps_t"])
        memset("dve", ones_b[:], 1.0, ["ones_b"])
        dma("sp", ident_f[:], IN["ident_f"], [], ["ident_f"], "c0")
        dma("sp", ident_b[:], IN["ident_b"], [], ["ident_b"], "c0")
        dma("sp", gn1[:], IN["g_n1"], [], ["gn1"], "c0")
        dma("sp", gn2[:], IN["g_n2"], [], ["gn2"], "c0")
        dma("sp", gfin[:], IN["g_fin"], [], ["gfin"], "c0")

        def cast(dst, src, key, nsplit=1):
            n = src.shape[0]
            step = n // nsplit
            for i in range(nsplit):
                dma("pool", dst[i * step:(i + 1) * step, :], src[i * step:(i + 1) * step, :], [], [key], "cast_" + key)
        cast(W_in, IN["a_w_in_x"], "W_in", 2)
        cast(W_qb, IN["a_w_qb_x"], "W_qb")
        cast(W_kn, IN["a_w_kn"], "W_kn")
        cast(W_v, IN["a_w_v"], "W_v")
        cast(W_ao, IN["a_w_out"], "W_ao")
        cast(W_1[0], IN["mlp_w1"][0], "W_1_0", 4)
        cast(W_2[0], IN["mlp_w2"][0], "W_2_0", 4)
        cast(W_hi, IN["h_w_in"], "W_hi", 4)
        cast(W_ho, IN["h_w_out"], "W_ho")
        cast(W_1[1], IN["mlp_w1"][1], "W_1_1", 4)
        cast(W_2[1], IN["mlp_w2"][1], "W_2_1", 4)

        with ExitStack() as st:
            cT = SB(st, "cT", [128, 8, 3], F32)
            bm = SB(st, "bm", [128, 2, 48], F32)
            wm = [SB(st, f"wm{i}", [128, 8, 768], F32) for i in range(2)]
            dma("sp", cT[:], IN["cT"], [], ["cT"], "a0")
            dma("sp", bm[:], IN["b_modT"], [], ["bm"], "a0")
            act(cT[:], cT[:], AF.Silu, ["cT"], ["cT"])
            for i in range(2):
                wv = IN["w_mod"][i].rearrange("(kc p) n -> p kc n", p=128)
                for g in range(8):
                    s = (i * 8 + g) % 2
                    dma("sp", wm[s][:], wv[:, :, g * 768:(g + 1) * 768], [], [f"wm{s}"], f"wm{s}")
                    b_, bk = nb()
                    for m in range(6):
                        for kc in range(8):
                            mm(b_[:, m * 3:m * 3 + 3], wm[s][:, kc, m * 128:(m + 1) * 128], cT[:, kc, :],
                               kc == 0, kc == 7, [f"wm{s}", "cT"], [bk])
                    for m in range(6):
                        ch = g * 6 + m
                        ts("dve", modT[:, i, ch, :], b_[:, m * 3:m * 3 + 3], bm[:, i, ch:ch + 1], 0.0, ALU.add, ALU.add,
                           [bk, "bm"], ["modT"])
            for i in range(2):
                for (Ax, gx, which, nm) in ((A1, gn1, 1, "A1"), (A2, gn2, 4, "A2")):
                    for dc in range(8):
                        ts("dve", Ax[:, i, dc, :], modT[:, i, which * 8 + dc, :], 1.0, gx[:, i, dc:dc + 1], ALU.add, ALU.mult,
                           ["modT", "gn1", "gn2"], [nm])
            P.flush()

        def MODV(i, which, dc, j):
            return modT[:, i, which * 8 + dc, j:j + 1]

        def rstd_from(psap, pk, out, ok, n):
            act(out, psap, AF.Sqrt, [pk, "eps_t"], [ok], bias=eps_t[:], scale=1.0 / n)
            P.op("dve", lambda e, o=out: e.reciprocal(o, o), reads=[ok], writes=[ok])

        def load_xT(dst, dk, t0, key):
            dma("sp", dst[:], XTv[:, :, t0:t0 + 512], ["XT"], [dk], key)

        def norm_mod(xT, xk, hl, hk, sq, sqk, rs, rsk, tmp, tmpk, Ax, i, shw, j, hl_of=None):
            for dc in range(8):
                act(sq[:, dc, :], xT[:, dc, :], AF.Square, [xk], [sqk])
            b_, bk = nb()
            for dc in range(8):
                mm(b_[:], ones_b[:], sq[:, dc, :], dc == 0, dc == 7, ["ones_b", sqk], [bk])
            rstd_from(b_[:], bk, rs[:], rsk, 1024.0)
            for dc in range(8):
                s = dc % 2
                tt("dve", tmp[s][:], xT[:, dc, :], rs[:], ALU.mult, [xk, rsk], [tmpk + str(s)])
                act(hl_of(dc) if hl_of is not None else hl[:, dc, :], tmp[s][:], AF.Identity, [tmpk + str(s), "modT", "A1", "A2"], [hk],
                    bias=MODV(i, shw, dc, j), scale=Ax[:, i, dc, j:j + 1])

        want = (lambda ph: (phases is None and stage >= 1) or (phases is not None and ph in phases))
        P.skip = not want('C')
        with ExitStack() as st:
            win = SB(st, "win", [128, 8, 2112], BF16)
            wqb = SB(st, "wqb", [128, 3, 1024], BF16)
            wkn = SB(st, "wkn", [128, 2, 768], BF16)
            wv = SB(st, "wv", [128, 2, 512], BF16)
            sel32 = SB(st, "sel32", [32, 96], BF16)
            bdo = SB(st, "bdo", [128, 128], BF16)
            gqa = SB(st, "gqa", [128, 3], F32)
            gkva = SB(st, "gkva", [128, 2], F32)
            ghn = SB(st, "ghn", [128, 4], F32)
            tabG = SB(st, "tabG", [128, 4, L], F32)
            tabM = SB(st, "tabM", [32, 2, L], F32)
            xrow = [SB(st, f"xrow{i}", [128, D], F32) for i in range(2)]
            xT = SB(st, "xT", [128, 8, 512], F32)
            sq = SB(st, "sq", [128, 8, 512], BF16)
            hl = SB(st, "hl", [128, 8, 512], BF16)
            rs = SB(st, "rs", [128, 512], F32)
            rs2 = SB(st, "rs2", [128, 512], F32)
            tmp = [SB(st, f"tmp{i}", [128, 512], F32) for i in range(4)]
            c32 = SB(st, "c32", [128, 3, 512], F32)
            cqn = SB(st, "cqn", [128, 3, 512], BF16)
            ckvn = SB(st, "ckvn", [128, 2, 512], BF16)
            krr = SB(st, "krr", [32, 512], BF16)
            qo = [SB(st, f"qo{i}", [128, 512], BF16) for i in range(2)]
            vst = [SB(st, f"vst{i}", [128, 8, 128], BF16) for i in range(2)]
            vgs = [SB(st, f"vgs{i}", [128, 2, 128], BF16) for i in range(2)]
            dma("sp", win[:], W_in.rearrange("(c p) n -> p c n", p=128), ["W_in"], ["win"], "c1")
            dma("sp", wqb[:], W_qb.rearrange("(c p) n -> p c n", p=128), ["W_qb"], ["wqb"], "c1")
            dma("sp", wkn[:], W_kn.rearrange("(c p) n -> p c n", p=128), ["W_kn"], ["wkn"], "c1")
            dma("sp", wv[:], W_v.rearrange("(c p) n -> p c n", p=128), ["W_v"], ["wv"], "c1")
            dma("sp", sel32[:], IN["sel32"], [], ["sel32"], "c1")
            dma("sp", bdo[:], IN["bdones"], [], ["bdo"], "c1")
            dma("sp", gqa[:], IN["g_qa"], [], ["gqa"], "c1")
            dma("sp", gkva[:], IN["g_kva"], [], ["gkva"], "c1")
            dma("sp", ghn[:], IN["g_hn"], [], ["ghn"], "c1")
            dma("sp", tabG[:, 0, :], IN["ropeG_cos"], [], ["tabG"], "c1")
            dma("sp", tabG[:, 1, :], IN["ropeG_sin"], [], ["tabG"], "c1")
            dma("sp", tabG[:, 2, :], IN["ropeG_cos"], [], ["tabG"], "c1")
            dma("sp", tabG[:, 3, :], IN["ropeG_sin"], [], ["tabG"], "c1")
            dma("sp", tabM[:, 0, :], IN["ropeM_cos"], [], ["tabM"], "c1")
            dma("sp", tabM[:, 1, :], IN["ropeM_sin"], [], ["tabM"], "c1")
            for q in range(4):
                ts("pool" if q % 2 else "dve", tabG[:, q, :], tabG[:, q, :], ghn[:, q:q + 1], 0.0, ALU.mult, ALU.add,
                   ["tabG", "ghn"], ["tabG"])
            for i in range(2):
                memset("dve", vst[i][:], 1.0, [f"vst{i}"])
                memset("pool", vgs[i][:], 1.0, [f"vgs{i}"])

            def evac_copy(k, out, in_, reads, writes):
                cp("act" if k % 2 else "dve", out, in_, reads, writes)

            OFF_CKV, OFF_KR, OFF_KRS, OFF_GQ, OFF_GQS, OFF_GK, OFF_GKS, OFF_GV = 384, 640, 672, 704, 1216, 1728, 1856, 1984
            for blk in range(9):
                is_ctx = blk == 0
                src = IN["ctx"] if is_ctx else IN["x"]
                r0 = 0 if is_ctx else (blk - 1) * 512
                for dc in range(8):
                    pass
                for t4 in range(4):
                    s = t4 % 2
                    dma("sp", xrow[s][:], src[r0 + t4 * 128:r0 + (t4 + 1) * 128, :], [], [f"xrow{s}"], f"xrow{s}")
                    for dc in range(8):
                        b_, bk = nb()
                        P.op("pe", lambda e, o=b_[:, 0:128], i=xrow[s][:, dc * 128:(dc + 1) * 128]: e.transpose(o, i, ident_f[:]),
                             reads=[f"xrow{s}", "ident_f"], writes=[bk])
                        evac_copy(dc, xT[:, dc, t4 * 128:(t4 + 1) * 128], b_[:, 0:128], [bk], ["xT"])
                if not is_ctx:
                    dma("sp", XTv[:, :, r0:r0 + 512], xT[:], ["xT"], ["XT"], "xtst")
                if is_ctx:
                    norm_mod(xT, "xT", hl, "hl", sq, "sq", rs, "rs", tmp, "tmp", A1, 0, 0, 2)
                else:
                    lb = (blk - 1) // 4
                    norm_mod(xT, "xT", hl, "hl", sq, "sq", rs, "rs", tmp, "tmp", A1, 0, 0, lb)
                p0 = 0 if is_ctx else ((blk - 1) % 4) * 512

                def proj(col0, ncols, b_, bk, rows=128):
                    for dc in range(8):
                        mm(b_[0:ncols, :], win[:, dc, col0:col0 + ncols], hl[:, dc, :], dc == 0, dc == 7, ["win", "hl"], [bk])

                for (nm, col0, nch, gain, dst, dstk, do) in (("cq", 0, 3, gqa, cqn, "cqn", not is_ctx),
                                                             ("ckv", OFF_CKV, 2, gkva, ckvn, "ckvn", True)):
                    if not do:
                        continue
                    for m in range(nch):
                        b_, bk = nb()
                        proj(col0 + m * 128, 128, b_, bk)
                        cp("dve", c32[:, m, :], b_[:], [bk], ["c32"])
                        act(sq[:, m, :], c32[:, m, :], AF.Square, ["c32"], ["sq"])
                    b_, bk = nb()
                    for m in range(nch):
                        mm(b_[:], ones_b[:], sq[:, m, :], m == 0, m == nch - 1, ["ones_b", "sq"], [bk])
                    rstd_from(b_[:], bk, rs2[:], "rs2", float(nch * 128))
                    for m in range(nch):
                        stt("dve", dst[:, m, :], c32[:, m, :], gain[:, m:m + 1], rs2[:], ALU.mult, ALU.mult,
                            ["c32", "rs2", "gqa", "gkva"], [dstk])
                bm_, bmk = nb()
                proj(OFF_KR, 32, bm_, bmk)
                if is_ctx:
                    cp("dve", krr[:], bm_[0:32, :], [bmk], ["krr"])
                else:
                    bs_, bsk = nb()
                    proj(OFF_KRS, 32, bs_, bsk)
                    tt("dve", tmp[0][0:32, :], bm_[0:32, :], tabM[:, 0, p0:p0 + 512], ALU.mult, [bmk, "tabM"], ["tmp0"])
                    tt("dve", tmp[1][0:32, :], bs_[0:32, :], tabM[:, 1, p0:p0 + 512], ALU.mult, [bsk, "tabM"], ["tmp1"])
                    tt("pool", krr[:], tmp[0][0:32, :], tmp[1][0:32, :], ALU.add, ["tmp0", "tmp1"], ["krr"])
                if not is_ctx:
                    for h in range(8):
                        bm_, bmk = nb()
                        bs_, bsk = nb()
                        for kc in range(3):
                            mm(bm_[0:96, :], wqb[:, kc, h * 96:(h + 1) * 96], cqn[:, kc, :], kc == 0, kc == 2, ["wqb", "cqn"], [bmk])
                        for kc in range(3):
                            mm(bs_[0:32, :], wqb[:, kc, 768 + h * 32:768 + (h + 1) * 32], cqn[:, kc, :], kc == 0, kc == 2,
                               ["wqb", "cqn"], [bsk])
                        s = h % 2
                        tt("dve", tmp[0][0:32, :], bm_[0:32, :], tabM[:, 0, p0:p0 + 512], ALU.mult, [bmk, "tabM"], ["tmp0"])
                        tt("dve", tmp[1][0:32, :], bs_[0:32, :], tabM[:, 1, p0:p0 + 512], ALU.mult, [bsk, "tabM"], ["tmp1"])
                        tt("pool", qo[s][0:32, :], tmp[0][0:32, :], tmp[1][0:32, :], ALU.add, ["tmp0", "tmp1"], [f"qo{s}"])
                        cp("act", qo[s][32:64, :], bm_[32:64, :], [bmk], [f"qo{s}b", bmk])
                        cp("act", qo[s][64:96, :], bm_[64:96, :], [bmk], [f"qo{s}c", bmk])
                        dma("sp", QM[h, :, r0:r0 + 512], qo[s][0:96, :], [f"qo{s}", f"qo{s}b", f"qo{s}c"], ["QM"], f"qst{s}")
                for h in range(8):
                    b_, bk = nb()
                    for kc in range(2):
                        mm(b_[0:96, :], wkn[:, kc, h * 96:(h + 1) * 96], ckvn[:, kc, :], kc == 0, False, ["wkn", "ckvn"], [bk])
                    mm(b_[0:96, :], sel32[:], krr[:], False, True, ["sel32", "krr"], [bk])
                    s = h % 2
                    evac_copy(h, qo[s][0:96, :], b_[0:96, :], [bk], [f"qo{s}", f"qo{s}b", f"qo{s}c"])
                    if is_ctx:
                        for lb2 in range(2):
                            dma("sp", KM[lb2, h, :, 0:256], qo[s][0:96, lb2 * 256:(lb2 + 1) * 256], [f"qo{s}", f"qo{s}b"], ["KM"], f"qst{s}")
                    else:
                        dma("sp", KM[lb, h, :, 256 + p0:256 + p0 + 512], qo[s][0:96, :], [f"qo{s}", f"qo{s}b"], ["KM"], f"qst{s}")
                for t4 in range(4):
                    b_, bk = nb()
                    for kc in range(2):
                        mm(b_[:], ckvn[:, kc, t4 * 128:(t4 + 1) * 128], wv[:, kc, :], kc == 0, kc == 1, ["ckvn", "wv"], [bk])
                    s = t4 % 2
                    evac_copy(t4, vst[s][:, :, 0:64], b_[:].rearrange("p (h e) -> p h e", h=8), [bk], [f"vst{s}"])
                    if is_ctx:
                        lb2, kk = t4 // 2, (t4 % 2) * 128
                    else:
                        lb2, kk = lb, 256 + p0 + t4 * 128
                    dma("sp", VM[lb2, kk:kk + 128, :, :], vst[s][:], [f"vst{s}"], ["VM"], f"vstq{s}")
                for (cm, cs, tq, is_q) in [(OFF_GQ + m * 128, OFF_GQS + m * 128, 0, True) for m in range(4)] + [(OFF_GK, OFF_GKS, 2, False)]:
                    if is_q and is_ctx:
                        continue
                    bm_, bmk = nb()
                    proj(cm, 128, bm_, bmk)
                    act(sq[:, 0, :], bm_[:], AF.Square, [bmk], ["sq", bmk])
                    bq_, bqk = nb()
                    mm(bq_[:], bdo[:], sq[:, 0, :], True, True, ["bdo", "sq"], [bqk])
                    rstd_from(bq_[:], bqk, rs2[:], "rs2", 64.0)
                    m = (cm - OFF_GQ) // 128 if is_q else 0
                    s = m % 2
                    if is_ctx:
                        stt("dve", qo[s][:], bm_[:], ghn[:, 2:3], rs2[:], ALU.mult, ALU.mult, [bmk, "rs2", "ghn"], [f"qo{s}", f"qo{s}b", f"qo{s}c"])
                    else:
                        bs_, bsk = nb()
                        proj(cs, 128, bs_, bsk)
                        tt("dve", tmp[0][:], bm_[:], tabG[:, tq, p0:p0 + 512], ALU.mult, [bmk, "tabG"], ["tmp0"])
                        tt("dve", tmp[1][:], bs_[:], tabG[:, tq + 1, p0:p0 + 512], ALU.mult, [bsk, "tabG"], ["tmp1"])
                        tt("pool", tmp[2][:], tmp[0][:], tmp[1][:], ALU.add, ["tmp0", "tmp1"], ["tmp2"])
                        tt("dve", qo[s][:], tmp[2][:], rs2[:], ALU.mult, ["tmp2", "rs2"], [f"qo{s}", f"qo{s}b", f"qo{s}c"])
                    if is_q:
                        dma("sp", QG[2 * m:2 * m + 2, :, r0:r0 + 512].rearrange("h d t -> (h d) t"), qo[s][:],
                            [f"qo{s}", f"qo{s}b"], ["QG"], f"qst{s}")
                    elif is_ctx:
                        for lb2 in range(2):
                            dma("sp", KG[lb2, :, :, 0:256].rearrange("h d t -> (h d) t"), qo[s][:, lb2 * 256:(lb2 + 1) * 256],
                                [f"qo{s}", f"qo{s}b"], ["KG"], f"qst{s}")
                    else:
                        dma("sp", KG[lb, :, :, 256 + p0:256 + p0 + 512].rearrange("h d t -> (h d) t"), qo[s][:],
                            [f"qo{s}", f"qo{s}b"], ["KG"], f"qst{s}")
                for t4 in range(4):
                    b_, bk = nb()
                    for dc in range(8):
                        mm(b_[:, 0:128], hl[:, dc, t4 * 128:(t4 + 1) * 128], win[:, dc, OFF_GV:OFF_GV + 128], dc == 0, dc == 7,
                           ["hl", "win"], [bk])
                    s = t4 % 2
                    evac_copy(t4 + 1, vgs[s][:, :, 0:64], b_[:, 0:128].rearrange("p (h e) -> p h e", h=2), [bk], [f"vgs{s}"])
                    if is_ctx:
                        lb2, kk = t4 // 2, (t4 % 2) * 128
                    else:
                        lb2, kk = lb, 256 + p0 + t4 * 128
                    dma("sp", VG[lb2, kk:kk + 128, :, :], vgs[s][:], [f"vgs{s}"], ["VG"], f"vgsq{s}")
            P.flush()

        P.skip = not want('D')
        with ExitStack() as st:
            kms = SB(st, "kms", [96, 8, NK], BF16)
            kgs = SB(st, "kgs", [64, 2, NK], BF16)
            vms = SB(st, "vms", [128, 18, 8, 128], BF16)
            vgs2 = SB(st, "vgs2", [128, 18, 2, 128], BF16)
            wo = SB(st, "wo", [64, 16, D], BF16)
            qms = [SB(st, f"qms{i}", [96, 8, 512], BF16) for i in range(1)]
            qgs = [SB(st, f"qgs{i}", [64, 8, 512], BF16) for i in range(1)]
            pT = [SB(st, f"pT{i}", [128, 512], BF16) for i in range(3)]
            rec = [SB(st, f"rec{i}", [128, 512], F32) for i in range(2)]
            on = SB(st, "on", [64, 16, 512], BF16)
            xo = [SB(st, f"xo{i}", [128, 512], F32) for i in range(2)]
            dma("sp", wo[:], W_ao.rearrange("(h d) n -> d h n", d=64), ["W_ao"], ["wo"], "d0")
            psb_f = [ps[4], ps[5]]
            b4 = [0]

            def nb4():
                i = b4[0] % 4
                b4[0] += 1
                return ps[i], f"ps{i}"
            SC_M = 1.0 / math.sqrt(96.0)
            SC_G = 0.125
            pi = 0
            for lb in range(2):
                dma("sp", kms[:], KM[lb].rearrange("h d t -> d h t"), ["KM"], ["kms"], "d1")
                dma("sp", kgs[:], KG[lb].rearrange("h d t -> d h t"), ["KG"], ["kgs"], "d1")
                dma("sp", vms[:], VM[lb].rearrange("(c p) h e -> p c h e", p=128), ["VM"], ["vms"], "d1")
                dma("sp", vgs2[:], VG[lb].rearrange("(c p) h e -> p c h e", p=128), ["VG"], ["vgs2"], "d1")
                for qb in range(4):
                    t0 = lb * L + qb * 512
                    s = 0
                    dma("sp", qms[s][:], QM[:, :, t0:t0 + 512].rearrange("h d t -> d h t"), ["QM"], [f"qms{s}"], f"qld{s}")
                    dma("sp", qgs[s][:], QG[:, :, t0:t0 + 512].rearrange("h d t -> d h t"), ["QG"], [f"qgs{s}"], f"qld{s}")
                    for h in range(16):
                        oa, oak = psb_f[h % 2], f"ps{4 + h % 2}"
                        for kc in range(18):
                            b_, bk = nb4()
                            if h < 8:
                                mm(b_[:], kms[:, h, kc * 128:(kc + 1) * 128], qms[s][:, h, :], True, True, ["kms", f"qms{s}"], [bk])
                                sc = SC_M
                                va = vms[:, kc, h, :]
                                vk = "vms"
                            else:
                                g = (h - 8) // 4
                                mm(b_[:], kgs[:, g, kc * 128:(kc + 1) * 128], qgs[s][:, h - 8, :], True, True, ["kgs", f"qgs{s}"], [bk])
                                sc = SC_G
                                va = vgs2[:, kc, g, :]
                                vk = "vgs2"
                            pp = pi % 3
                            pi += 1
                            act(pT[pp][:], b_[:], AF.Exp, [bk], [f"pT{pp}"], scale=sc)
                            mm(oa[:], va, pT[pp][:], kc == 0, kc == 17, [vk, f"pT{pp}"], [oak])
                        rr = h % 2
                        P.op("dve", lambda e, o=rec[rr][64:128, :], i=oa[64:128, :]: e.reciprocal(o, i), reads=[oak], writes=[f"rec{rr}"])
                        tt("dve", on[:, h, :], oa[0:64, :], rec[rr][64:128, :], ALU.mult, [oak, f"rec{rr}"], ["on"])
                    for dc in range(8):
                        b_, bk = nb4()
                        for h in range(16):
                            mm(b_[:], wo[:, h, dc * 128:(dc + 1) * 128], on[:, h, :], h == 0, h == 15, ["wo", "on"], [bk])
                        xs = dc % 2
                        dma("sp", xo[xs][:], XT[dc * 128:(dc + 1) * 128, t0:t0 + 512], ["XT"], [f"xo{xs}"], f"xold{xs}")
                        stt("dve", xo[xs][:], b_[:], MODV(0, 2, dc, lb), xo[xs][:], ALU.mult, ALU.add, [bk, "modT", f"xo{xs}"], [f"xo{xs}"])
                        dma("sp", XT[dc * 128:(dc + 1) * 128, t0:t0 + 512], xo[xs][:], [f"xo{xs}"], ["XT"], f"xost{xs}")
            P.flush()
        P.skip = False
        if dbg:
            dma("sp", DBG["xa0"], XT, ["XT"], [], "dbg")

        def mlp_phase(i):
            with ExitStack() as st:
                xT = [SB(st, f"mxT{k}", [128, 8, 512], F32) for k in range(2)]
                sq = SB(st, "msq", [128, 8, 512], BF16)
                xn = SB(st, "mxn", [128, 8, 512], BF16)
                rs = SB(st, "mrs", [128, 512], F32)
                tmp = [SB(st, f"mtmp{k}", [128, 512], F32) for k in range(2)]
                aT = SB(st, "maT", [128, 32, 512], BF16)
                w1t = [SB(st, f"w1t{k}", [128, 8, 512], BF16) for k in range(3)]
                w2t = [SB(st, f"w2t{k}", [128, 4, 512], BF16) for k in range(3)]
                r1 = [SB(st, f"mr1{k}", [128, 512], F32) for k in range(2)]
                w1v = W_1[i].rearrange("(kc p) n -> p kc n", p=128)
                w2v = W_2[i].rearrange("(fc p) n -> p fc n", p=128)
                n1 = 0
                n2 = 0
                for tb in range(8):
                    lb = tb // 4
                    t0 = tb * 512
                    xs = tb % 2
                    load_xT(xT[xs], f"mxT{xs}", t0, f"mxld{xs}")
                    norm_mod(xT[xs], f"mxT{xs}", xn, "mxn", sq, "msq", rs, "mrs", tmp, "mtmp", A2, i, 3, lb)
                    for g in range(8):
                        ws = n1 % 3
                        n1 += 1
                        dma("sp", w1t[ws][:], w1v[:, :, g * 512:(g + 1) * 512], [f"W_1_{i}"], [f"w1t{ws}"], f"w1q{ws}")
                        for m in range(4):
                            b_, bk = nb()
                            for kc in range(8):
                                mm(b_[:], w1t[ws][:, kc, m * 128:(m + 1) * 128], xn[:, kc, :], kc == 0, kc == 7, [f"w1t{ws}", "mxn"], [bk])
                            fc = g * 4 + m
                            rk = fc % 2
                            act(r1[rk][:], b_[:], AF.Relu, [bk], [f"mr1{rk}"])
                            tt("dve" if fc % 4 else "pool", aT[:, fc, :], r1[rk][:], r1[rk][:], ALU.mult, [f"mr1{rk}"], ["maT"])
                    for half in range(2):
                        bks = [nb() for _ in range(4)]
                        for g in range(8):
                            ws = n2 % 3
                            n2 += 1
                            dma("sp", w2t[ws][:], w2v[:, g * 4:(g + 1) * 4, half * 512:(half + 1) * 512], [f"W_2_{i}"], [f"w2t{ws}"], f"w2q{ws}")
                            for f4 in range(4):
                                fc = g * 4 + f4
                                for m in range(4):
                                    mm(bks[m][0][:], w2t[ws][:, f4, m * 128:(m + 1) * 128], aT[:, fc, :], fc == 0, fc == 31,
                                       [f"w2t{ws}", "maT"], [bks[m][1]])
                        for m in range(4):
                            dc = half * 4 + m
                            stt("dve", xT[xs][:, dc, :], bks[m][0][:], MODV(i, 5, dc, lb), xT[xs][:, dc, :], ALU.mult, ALU.add,
                                [bks[m][1], "modT", f"mxT{xs}"], [f"mxT{xs}"])
                    dma("sp", XTv[:, :, t0:t0 + 512], xT[xs][:], [f"mxT{xs}"], ["XT"], f"mxst{xs}")
                P.flush()

        def hyena_phase():
            S2 = 2.0 / 4096.0
            MAGIC = 12582912.0
            TWO_PI = 2.0 * math.pi
            with ExitStack() as st:
                feats = SB(st, "feats", [17, L], F32)
                fw1 = SB(st, "fw1", [17, 64], F32)
                fw2 = SB(st, "fw2", [64, 64], F32)
                fw3 = SB(st, "fw3", [64, 64], F32)
                fw4 = SB(st, "fw4", [64, 4096], F32)
                fb = SB(st, "fb", [64, 3], F32)
                ffr = SB(st, "ffr", [64, 3], F32)
                fbf = SB(st, "fbf", [64, 3], F32)
                ones_f = SB(st, "ones_f", [128, 128], F32)
                hbuf = [SB(st, f"hbuf{k}", [64, L], F32) for k in range(2)]
                fa = SB(st, "fa", [64, L], F32)
                fk = SB(st, "fk", [64, L], F32)
                wint = [SB(st, f"wint{k}", [128, 1024], F32) for k in range(2)]
                hw = [SB(st, f"hw{k}", [128, 2048], F32) for k in range(2)]
                sqt = SB(st, "sqt", [128, 2048], F32)
                acc = SB(st, "acc", [128, 2048], F32)
                Gs = SB(st, "Gs", [128, 16, 1024], BF16)
                Hs = SB(st, "Hs", [128, 16, 1024], BF16)
                rs2 = SB(st, "frs2", [128, 1024], F32)
                sk2 = SB(st, "fsk2", [128, 1024], F32)
                ktmp = [SB(st, f"ktmp{k}", [128, 1024], F32) for k in range(3)]
                fct = [SB(st, f"fct{k}", [128, 16, 128], BF16) for k in range(2)]
                fst = [SB(st, f"fst{k}", [128, 16, 128], BF16) for k in range(2)]
                dma("sp", feats[:], IN["featsT"], [], ["feats"], "f")
                dma("sp", fw1[:], IN["h_f_w1"], [], ["fw1"], "f")
                dma("sp", fw2[:], IN["h_f_w2"], [], ["fw2"], "f")
                dma("sp", fw3[:], IN["h_f_w3"], [], ["fw3"], "f")
                dma("sp", fw4[:], IN["h_f_w4"], [], ["fw4"], "f")
                dma("sp", fb[:], IN["h_f_b"], [], ["fb"], "f")
                dma("sp", ffr[:], IN["h_f_fr"], [], ["ffr"], "f")
                memset("dve", ones_f[:], 1.0, ["ones_f"])
                tt("dve", fbf[:], fb[:], ffr[:], ALU.mult, ["fb", "ffr"], ["fbf"])
                src, srck, kdim = feats, "feats", 17
                wl = [(fw1, "fw1"), (fw2, "fw2"), (fw3, "fw3")]
                for l in range(3):
                    dst, dstk = hbuf[l % 2], f"hbuf{l % 2}"
                    for q in range(4):
                        b_, bk = nb()
                        mm(b_[0:64, :], wl[l][0][0:kdim, :], src[0:kdim, q * 512:(q + 1) * 512], True, True, [wl[l][1], srck], [bk])
                        act(fa[:, q * 512:(q + 1) * 512], b_[0:64, :], AF.Identity, [bk, "ffr", "fbf"], ["fa"],
                            bias=fbf[:, l:l + 1], scale=ffr[:, l:l + 1])
                    ts("dve", fk[:], fa[:], 1.0 / TWO_PI, 0.0, ALU.mult, ALU.add, ["fa"], ["fk"])
                    ts("dve", fk[:], fk[:], MAGIC, 0.0, ALU.add, ALU.add, ["fk"], ["fk"])
                    ts("dve", fk[:], fk[:], -MAGIC, 0.0, ALU.add, ALU.add, ["fk"], ["fk"])
                    stt("dve", fa[:], fk[:], -TWO_PI, fa[:], ALU.mult, ALU.add, ["fk", "fa"], ["fa"])
                    act(dst[:], fa[:], AF.Sin, ["fa"], [dstk], scale=0.9999995)
                    src, srck, kdim = dst, dstk, 64
                h3, h3k = src, srck
                for o in range(2):
                    memset("pool", acc[:], 0.0, ["acc"])
                    for tti in range(16):
                        ws = tti % 2
                        dma("sp", wint[ws][:], IN["window"][tti * 128:(tti + 1) * 128, :], [], [f"wint{ws}"], "f")
                        hs = tti % 2
                        for d_ in range(2):
                            for hf_ in range(2):
                                b_, bk = nb()
                                c0 = d_ * 2048 + o * 1024 + hf_ * 512
                                mm(b_[:], h3[:, tti * 128:(tti + 1) * 128], fw4[:, c0:c0 + 512], True, True, [h3k, "fw4"], [bk])
                                tt("dve", hw[hs][:, d_ * 1024 + hf_ * 512:d_ * 1024 + (hf_ + 1) * 512], b_[:],
                                   wint[ws][:, hf_ * 512:(hf_ + 1) * 512], ALU.mult, [bk, f"wint{ws}"], [f"hw{hs}"])
                        tt("pool", sqt[:], hw[hs][:], hw[hs][:], ALU.mult, [f"hw{hs}"], ["sqt"])
                        tt("pool", acc[:], acc[:], sqt[:], ALU.add, ["acc", "sqt"], ["acc"])
                        if tti == 0:
                            memset("dve", hw[hs][0:1, 1024:2048], 0.0, [f"hw{hs}"])
                        tt("pool", Gs[:, tti, :], hw[hs][:, 0:1024], hw[hs][:, 1024:2048], ALU.add, [f"hw{hs}"], ["Gs"])
                        tt("dve", Hs[:, tti, :], hw[hs][:, 1024:2048], hw[hs][:, 0:1024], ALU.subtract, [f"hw{hs}"], ["Hs"])
                    for hf_ in range(2):
                        b0, b0k = nb()
                        b1, b1k = nb()
                        mm(b0[:], ones_f[:], acc[:, hf_ * 512:(hf_ + 1) * 512], True, True, ["ones_f", "acc"], [b0k])
                        mm(b1[:], ones_f[:], acc[:, 1024 + hf_ * 512:1024 + (hf_ + 1) * 512], True, True, ["ones_f", "acc"], [b1k])
                        cp("dve", ktmp[0][:, 0:512], b0[:], [b0k], ["ktmp0"])
                        tt("dve", ktmp[0][:, 0:512], ktmp[0][:, 0:512], b1[:], ALU.add, ["ktmp0", b1k], ["ktmp0"])
                        act(rs2[:, hf_ * 512:(hf_ + 1) * 512], ktmp[0][:, 0:512], AF.Sqrt, ["ktmp0", "eps_t"], ["frs2"], bias=eps_t[:], scale=1.0)
                    P.op("dve", lambda e, o_=rs2[:]: e.reciprocal(o_, o_), reads=["frs2"], writes=["frs2"])
                    ts("dve", rs2[:], rs2[:], S2, 0.0, ALU.mult, ALU.add, ["frs2"], ["frs2"])
                    dma("sp", sk2[:], IN["h_skip_b"][:, o * 1024:(o + 1) * 1024], [], ["fsk2"], "f")
                    ts("dve", sk2[:], sk2[:], S2, 0.0, ALU.mult, ALU.add, ["fsk2"], ["fsk2"])
                    for fc in range(16):
                        s_ = fc % 2
                        dma("sp", fct[s_][:], IN["FCt"][fc], [], [f"fct{s_}"], "f")
                        dma("sp", fst[s_][:], IN["FSt"][fc], [], [f"fst{s_}"], "f")
                        jobs = [(fct[s_], f"fct{s_}", Gs, "Gs", 0), (fst[s_], f"fst{s_}", Hs, "Hs", 1)]
                        if fc == 0:
                            jobs.append((fst[s_], f"fst{s_}", Gs, "Gs", 2))
                        for (mt, mk, dat, dk_, kind) in jobs:
                            kt, ktk = ktmp[kind], f"ktmp{kind}"
                            for hf_ in range(2):
                                b_, bk = nb()
                                for tti in range(16):
                                    mm(b_[:], mt[:, tti, :], dat[:, tti, hf_ * 512:(hf_ + 1) * 512], tti == 0, tti == 15, [mk, dk_], [bk])
                                tt("dve", kt[:, hf_ * 512:(hf_ + 1) * 512], b_[:], rs2[:, hf_ * 512:(hf_ + 1) * 512], ALU.mult,
                                   [bk, "frs2"], [ktk])
                            if kind != 1:
                                tt("pool", kt[:], kt[:], sk2[:], ALU.add, [ktk, "fsk2"], [ktk])
                            if fc == 0:
                                if kind == 1:
                                    memset("dve", kt[0:1, :], 0.0, [ktk])
                                else:
                                    ts("dve", kt[0:1, :], kt[0:1, :], 0.5, 0.0, ALU.mult, ALU.add, [ktk], [ktk])
                            if kind == 0:
                                dma("sp", KA[fc * 128:(fc + 1) * 128, o * 1024:(o + 1) * 1024], kt[:], [ktk], ["KA"], "f")
                            elif kind == 1:
                                dma("sp", KB[fc * 128:(fc + 1) * 128, o * 1024:(o + 1) * 1024], kt[:], [ktk], ["KB"], "f")
                            else:
                                dma("sp", KD0[0:1, o * 1024:(o + 1) * 1024], kt[0:1, :], [ktk], ["KD0"], "f")
                P.flush()

            for lb in range(2):
                with ExitStack() as bst:
                    u = SB(bst, "u", [128, 16, 1024], BF16)
                    with ExitStack() as st:
                        hlT = SB(st, "hlT", [128, 8, L], BF16)
                        xT = SB(st, "hxT", [128, 8, 512], F32)
                        sq = SB(st, "hsq", [128, 8, 512], BF16)
                        rs = SB(st, "hrs", [128, 512], F32)
                        tmp = [SB(st, f"htmp{k}", [128, 512], F32) for k in range(2)]
                        zbuf = [SB(st, f"zbuf{k}", [128, L + 2], F32) for k in range(2)]
                        ztmp = SB(st, "ztmp", [128, L], F32)
                        zc = [SB(st, f"zc{k}", [128, L], BF16) for k in range(2)]
                        whi = [SB(st, f"whi{k}", [128, 8, 128], BF16) for k in range(2)]
                        cw = SB(st, "cw", [128, 24, 3], F32)
                        cb = SB(st, "cb", [128, 24], F32)
                        dma("sp", cw[:], IN["h_conv_wT"], [], ["cw"], "h")
                        dma("sp", cb[:], IN["h_conv_bT"], [], ["cb"], "h")
                        for k in range(2):
                            memset("pool", zbuf[k][:], 0.0, [f"zbuf{k}"])
                        for tb in range(4):
                            t0 = lb * L + tb * 512
                            load_xT(xT, "hxT", t0, "h")
                            norm_mod(xT, "hxT", None, "hlT", sq, "hsq", rs, "hrs", tmp, "htmp", A1, 1, 0, lb,
                                     hl_of=lambda dc, tb=tb: hlT[:, dc, tb * 512:(tb + 1) * 512])
                        whv = W_hi.rearrange("(dc p) n -> p dc n", p=128)
                        for cc in range(24):
                            s_ = cc % 2
                            dma("sp", whi[s_][:], whv[:, :, cc * 128:(cc + 1) * 128], ["W_hi"], [f"whi{s_}"], "h")
                            for tb in range(4):
                                b_, bk = nb()
                                for dc in range(8):
                                    mm(b_[:], whi[s_][:, dc, :], hlT[:, dc, tb * 512:(tb + 1) * 512], dc == 0, dc == 7, [f"whi{s_}", "hlT"], [bk])
                                cp("act", zbuf[s_][:, 1 + tb * 512:1 + (tb + 1) * 512], b_[:], [bk], [f"zbuf{s_}"])
                            ts("dve", ztmp[:], zbuf[s_][:, 0:L], cw[:, cc, 0:1], cb[:, cc:cc + 1], ALU.mult, ALU.add,
                               [f"zbuf{s_}", "cw", "cb"], ["ztmp"])
                            stt("dve", ztmp[:], zbuf[s_][:, 1:L + 1], cw[:, cc, 1:2], ztmp[:], ALU.mult, ALU.add,
                                [f"zbuf{s_}", "cw", "ztmp"], ["ztmp"])
                            stt("dve", zc[s_][:], zbuf[s_][:, 2:L + 2], cw[:, cc, 2:3], ztmp[:], ALU.mult, ALU.add,
                                [f"zbuf{s_}", "cw", "ztmp"], [f"zc{s_}"])
                            if cc < 16:
                                dma("sp", X12T[cc * 128:(cc + 1) * 128, lb * L:(lb + 1) * L], zc[s_][:], [f"zc{s_}"], ["X12T"], "h")
                            else:
                                vc = cc - 16
                                for g8 in range(2):
                                    pb, pbk = psb[g8], f"psb{g8}"
                                    for j in range(8):
                                        blk = g8 * 8 + j
                                        P.op("pe", lambda e, o_=pb[:, j * 128:(j + 1) * 128], i_=zc[s_][:, blk * 128:(blk + 1) * 128]:
                                             e.transpose(o_, i_, ident_b[:]), reads=[f"zc{s_}", "ident_b"], writes=[pbk])
                                    cp("act" if g8 else "dve", u[:, g8 * 8:(g8 + 1) * 8, vc * 128:(vc + 1) * 128],
                                       pb[:].rearrange("p (a b) -> p a b", a=8), [pbk], ["u"])
                        P.flush()
                    with ExitStack() as cst:
                        Yre = SB(cst, "Yre", [128, 16, 1024], BF16)
                        Yz = SB(cst, "Yz", [128, 16, 1024], BF16)
                        for o in range(2):
                            with ExitStack() as st:
                                fct = [SB(st, f"cfct{k}", [128, 16, 128], BF16) for k in range(2)]
                                fst = [SB(st, f"cfst{k}", [128, 16, 128], BF16) for k in range(2)]
                                kat = [SB(st, f"kat{k}", [128, 1024], F32) for k in range(2)]
                                kbt = [SB(st, f"kbt{k}", [128, 1024], F32) for k in range(2)]
                                kdt = SB(st, "kdt", [128, 1024], F32)
                                t1 = [SB(st, f"ct{k}", [128, 512], F32) for k in range(4)]
                                for fc in range(16):
                                    s_ = fc % 2
                                    dma("sp", fct[s_][:], IN["FCt"][fc], [], [f"cfct{s_}"], "h")
                                    dma("sp", fst[s_][:], IN["FSt"][fc], [], [f"cfst{s_}"], "h")
                                    dma("sp", kat[s_][:], KA[fc * 128:(fc + 1) * 128, o * 1024:(o + 1) * 1024], ["KA"], [f"kat{s_}"], "h")
                                    dma("sp", kbt[s_][:], KB[fc * 128:(fc + 1) * 128, o * 1024:(o + 1) * 1024], ["KB"], [f"kbt{s_}"], "h")
                                    if fc == 0:
                                        cp("pool", kdt[:], kat[s_][:], [f"kat{s_}"], ["kdt"])
                                        dma("sp", kdt[0:1, :], KD0[0:1, o * 1024:(o + 1) * 1024], ["KD0"], ["kdt"], "h")
                                        dd, ddk = kdt, "kdt"
                                    else:
                                        dd, ddk = kat[s_], f"kat{s_}"
                                    for hf_ in range(2):
                                        bc, bck = nb()
                                        bs, bsk = nb()
                                        cs = slice(hf_ * 512, (hf_ + 1) * 512)
                                        for tti in range(16):
                                            mm(bc[:], fct[s_][:, tti, :], u[:, tti, cs], tti == 0, tti == 15, [f"cfct{s_}", "u"], [bck])
                                        for tti in range(16):
                                            mm(bs[:], fst[s_][:, tti, :], u[:, tti, cs], tti == 0, tti == 15, [f"cfst{s_}", "u"], [bsk])
                                        tt("dve", t1[0][:], bc[:], kat[s_][:, cs], ALU.mult, [bck, f"kat{s_}"], ["ct0"])
                                        tt("dve", t1[1][:], bs[:], kbt[s_][:, cs], ALU.mult, [bsk, f"kbt{s_}"], ["ct1"])
                                        tt("pool", Yre[:, fc, cs], t1[0][:], t1[1][:], ALU.add, ["ct0", "ct1"], ["Yre"])
                                        tt("dve", t1[2][:], bs[:], dd[:, cs], ALU.mult, [bsk, ddk], ["ct2"])
                                        tt("dve", t1[3][:], bc[:], kbt[s_][:, cs], ALU.mult, [bck, f"kbt{s_}"], ["ct3"])
                                        tt("pool", Yz[:, fc, cs], t1[2][:], t1[3][:], ALU.subtract, ["ct2", "ct3"], ["Yz"])
                                P.flush()
                            with ExitStack() as st:
                                fcw = SB(st, "fcw", [128, 16, 512], BF16)
                                fsw = SB(st, "fsw", [128, 16, 512], BF16)
                                gate = SB(st, "gate", [128, 8, 512], BF16)
                                y2T = SB(st, "y2T", [128, 8, 512], BF16)
                                if o == 1:
                                    who = SB(st, "who", [128, 8, D], BF16)
                                    xo = [SB(st, f"hxo{k}", [128, 512], F32) for k in range(2)]
                                    dma("sp", who[:], W_ho.rearrange("(c p) n -> p c n", p=128), ["W_ho"], ["who"], "h")
                                gv_ = X12T[o * 1024:(o + 1) * 1024, :].rearrange("(cc p) t -> p cc t", p=128)
                                for nbk in range(4):
                                    t0 = lb * L + nbk * 512
                                    dma("sp", fcw[:], IN["FCw"][nbk], [], ["fcw"], "h")
                                    dma("sp", fsw[:], IN["FSTw"][nbk], [], ["fsw"], "h")
                                    dma("sp", gate[:], gv_[:, :, t0:t0 + 512], ["X12T"], ["gate"], "h")
                                    for cc in range(8):
                                        b_, bk = nb()
                                        for fc in range(16):
                                            mm(b_[:], Yre[:, fc, cc * 128:(cc + 1) * 128], fcw[:, fc, :], fc == 0, False, ["Yre", "fcw"], [bk])
                                        for fc in range(16):
                                            mm(b_[:], Yz[:, fc, cc * 128:(cc + 1) * 128], fsw[:, fc, :], False, fc == 15, ["Yz", "fsw"], [bk])
                                        tt("dve", y2T[:, cc, :], b_[:], gate[:, cc, :], ALU.mult, [bk, "gate"], ["y2T"])
                                    if o == 0:
                                        for j in range(4):
                                            pb, pbk = psb[j % 2], f"psb{j % 2}"
                                            for cc in range(8):
                                                P.op("pe", lambda e, o_=pb[:, cc * 128:(cc + 1) * 128], i_=y2T[:, cc, j * 128:(j + 1) * 128]:
                                                     e.transpose(o_, i_, ident_b[:]), reads=["y2T", "ident_b"], writes=[pbk])
                                            cp("act" if j % 2 else "dve", u[:, nbk * 4 + j, :], pb[:], [pbk], ["u"])
                                    else:
                                        for dc in range(8):
                                            b_, bk = nb()
                                            for cc in range(8):
                                                mm(b_[:], who[:, cc, dc * 128:(dc + 1) * 128], y2T[:, cc, :], cc == 0, cc == 7, ["who", "y2T"], [bk])
                                            xs = dc % 2
                                            dma("sp", xo[xs][:], XT[dc * 128:(dc + 1) * 128, t0:t0 + 512], ["XT"], [f"hxo{xs}"], "h")
                                            stt("dve", xo[xs][:], b_[:], MODV(1, 2, dc, lb), xo[xs][:], ALU.mult, ALU.add,
                                                [bk, "modT", f"hxo{xs}"], [f"hxo{xs}"])
                                            dma("sp", XT[dc * 128:(dc + 1) * 128, t0:t0 + 512], xo[xs][:], [f"hxo{xs}"], ["XT"], "h")
                                P.flush()

        P.skip = not want('M')
        mlp_phase(0)
        P.skip = False
        if dbg:
            dma("sp", DBG["xm0"], XT, ["XT"], [], "dbg")
        if (phases is None and stage >= 2) or (phases is not None and 'H' in phases):
            hyena_phase()
        if dbg:
            dma("sp", DBG["xa1"], XT, ["XT"], [], "dbg")
        if (phases is None and stage >= 2) or (phases is not None and 'N' in phases):
            mlp_phase(1)

        with ExitStack() as st:
            xT = [SB(st, f"fxT{k}", [128, 8, 512], F32) for k in range(2)]
            sq = SB(st, "fsq", [128, 8, 512], BF16)
            rs = SB(st, "frs", [128, 512], F32)
            yT = SB(st, "fyT", [128, 8, 512], F32)
            orow = [SB(st, f"forow{k}", [128, D], F32) for k in range(2)]
            for tb in range(8):
                t0 = tb * 512
                xs = tb % 2
                load_xT(xT[xs], f"fxT{xs}", t0, f"fxld{xs}")
                for dc in range(8):
                    act(sq[:, dc, :], xT[xs][:, dc, :], AF.Square, [f"fxT{xs}"], ["fsq"])
                b_, bk = nb()
                for dc in range(8):
                    mm(b_[:], ones_b[:], sq[:, dc, :], dc == 0, dc == 7, ["ones_b", "fsq"], [bk])
                rstd_from(b_[:], bk, rs[:], "frs", 1024.0)
                for dc in range(8):
                    stt("dve", yT[:, dc, :], xT[xs][:, dc, :], gfin[:, dc:dc + 1], rs[:], ALU.mult, ALU.mult,
                        [f"fxT{xs}", "gfin", "frs"], ["fyT"])
                for t4 in range(4):
                    os_ = t4 % 2
                    for dc in range(8):
                        b_, bk = nb()
                        P.op("pe", lambda e, o=b_[:, 0:128], i=yT[:, dc, t4 * 128:(t4 + 1) * 128]: e.transpose(o, i, ident_f[:]),
                             reads=["fyT", "ident_f"], writes=[bk])
                        cp("act" if dc % 2 else "dve", orow[os_][:, dc * 128:(dc + 1) * 128], b_[:, 0:128], [bk], [f"forow{os_}"])
                    dma("sp", OUT[t0 + t4 * 128:t0 + (t4 + 1) * 128, :], orow[os_][:], [f"forow{os_}"], [], f"fost{os_}")
            P.flush()
        P.wait_all_dma("sp")
        P.flush(barrier=False)
    return nc


_NC = {}


def _run(inputs, dbg=False, stage=2, phases=None):
    per = _prep(inputs)
    shapes = {k: (v.shape, "bf16" if v.dtype == NPBF else "f32") for k, v in per[0].items()}
    key = (dbg, stage, phases)
    if key not in _NC:
        _NC[key] = build(shapes, dbg=dbg, stage=stage, phases=phases)
    res = run_bass_kernel_spmd(_NC[key], per, core_ids=list(range(8)))
    return res


def kernel(**inputs):
    res = _run(inputs)
    out = np.concatenate([np.asarray(r["out"], np.float32).reshape(2, L, D) for r in res.results], axis=0)
    return out
```

```python
import math
import numpy as np
import ml_dtypes
from contextlib import ExitStack
import concourse.bass as bass
import concourse.mybir as mybir
from concourse.bass_utils import run_bass_kernel_spmd

F32 = mybir.dt.float32
BF16 = mybir.dt.bfloat16
ALU = mybir.AluOpType
AF = mybir.ActivationFunctionType
NPBF = ml_dtypes.bfloat16

SAME_ENGINE_SYNC = True
D = 1024
NT = 4096
L = 2048
NK = 2304
EPS = 1e-6


class _Op:
    __slots__ = ("eng", "fn", "waits", "signal", "val", "key", "idx")


class Prog:
    ENGS = ("pe", "dve", "act", "pool", "sp")

    def __init__(self, nc, stack):
        self.nc = nc
        self.stack = stack
        self.streams = {e: [] for e in self.ENGS}
        self.sems = {}
        self.count = {}
        self.known = {e: {} for e in self.ENGS}
        self.res = {}
        self.base_idx = {e: 0 for e in self.ENGS}
        self.dma_n = {e: 0 for e in self.ENGS}
        for e in ("pe", "dve", "act", "pool"):
            self._sem(e)

    def _sem(self, tl):
        if tl not in self.sems:
            self.sems[tl] = self.stack.enter_context(self.nc.semaphore("s_" + str(tl)))
            self.count[tl] = 0
        return self.sems[tl]

    def _r(self, key):
        r = self.res.get(key)
        if r is None:
            r = [{}, {}]
            self.res[key] = r
        return r

    skip = False
    NPOOL = {"sp": 24, "pool": 8, "act": 4}

    def op(self, eng, fn, reads=(), writes=(), dma_key=None):
        if self.skip:
            return None
        o = _Op()
        o.eng = eng
        o.fn = fn
        o.signal = False
        o.val = None
        o.key = dma_key
        o.idx = self.base_idx[eng] + len(self.streams[eng])
        deps = {}

        def add(src):
            for tl, ev in src.items():
                if deps.get(tl, (-1,))[0] < ev[0]:
                    deps[tl] = ev
        for k in reads:
            add(self._r(k)[0])
        for k in writes:
            w, r = self._r(k)
            add(w)
            add(r)
        waits = []
        kn = self.known[eng]
        for tl, ev in deps.items():
            if tl == eng and (eng == "pe" or not SAME_ENGINE_SYNC):
                continue
            if kn.get(tl, -1) >= ev[0]:
                continue
            kn[tl] = ev[0]
            waits.append((tl, ev))
            if ev[1] is not None:
                ev[1].signal = True
        o.waits = waits
        if dma_key is not None:
            npool = self.NPOOL[eng]
            n = self.dma_n[eng]
            self.dma_n[eng] = n + 1
            mytl = "dq_%s_%d" % (eng, n % npool)
            self._sem(mytl)
            prev = self.count[mytl]
            if prev > 0 and kn.get(mytl, -1) < prev:
                kn[mytl] = prev
                waits.append((mytl, (prev, None)))
            self.count[mytl] = prev + 1
            myev = (prev + 1, None)
            o.key = mytl
        else:
            myev = (o.idx, o)
            mytl = eng
        for k in reads:
            self._r(k)[1][mytl] = myev
        for k in writes:
            r = self._r(k)
            r[0] = {mytl: myev}
            r[1] = {}
        self.streams[eng].append(o)
        return o

    def flush(self, barrier=True):
        nc = self.nc
        if barrier:
            self.wait_all_dma("sp", prefixes=("dq_sp_", "dq_act_"))
        for e in self.ENGS:
            for o in self.streams[e]:
                if o.key is None and o.signal:
                    self.count[e] += 1
                    o.val = self.count[e]
        streams = self.streams
        sems = self.sems

        def emit(eng_name, engine):
            for o in streams[eng_name]:
                for tl, ev in o.waits:
                    if ev[1] is None:
                        engine.wait_ge(sems[tl], 16 * ev[0])
                    else:
                        engine.wait_ge(sems[tl], ev[1].val)
                ins = o.fn(engine)
                if o.key is not None:
                    ins.then_inc(sems[o.key], 16)
                elif o.signal:
                    ins.then_inc(sems[eng_name], 1)

        with nc.Block() as block:
            if streams["pe"]:
                @block.tensor
                def _(t):
                    emit("pe", t)
            if streams["dve"]:
                @block.vector
                def _(v):
                    emit("dve", v)
            if streams["act"]:
                @block.scalar
                def _(s):
                    emit("act", s)
            if streams["pool"]:
                @block.gpsimd
                def _(g):
                    emit("pool", g)
            if streams["sp"]:
                @block.sync
                def _(s):
                    emit("sp", s)
        for e in self.ENGS:
            self.base_idx[e] += len(self.streams[e])
            self.streams[e] = []
        if barrier:
            nc.all_engine_barrier()
            for k, r in self.res.items():
                for d in (0, 1):
                    r[d] = {tl: ev for tl, ev in r[d].items() if ev[1] is None}
            for e in self.ENGS:
                self.known[e] = {tl: v for tl, v in self.known[e].items() if tl not in self.ENGS}

    def wait_all_dma(self, eng="sp", prefixes=("dq_",)):
        cnts = {tl: c for tl, c in self.count.items()
                if tl not in self.ENGS and c > 0 and str(tl).startswith(tuple(prefixes))}
        sems = self.sems

        def fn(engine, cnts=cnts):
            for tl, c in cnts.items():
                engine.wait_ge(sems[tl], 16 * c)
            return engine.nop()
        self.op(eng, fn)


def _fm(v, nch):
    return np.ascontiguousarray(np.asarray(v, np.float32).reshape(nch, 128).T)


def _rope_tables(hd):
    half = hd // 2
    nf = half // 2
    inv = (np.float32(10000.0) ** (-np.arange(nf, dtype=np.float32) / np.float32(nf))).astype(np.float32)
    pos = np.arange(L)
    row = (pos // 64).astype(np.float32)
    col = (pos % 64).astype(np.float32)
    cos = np.zeros((hd, L), np.float32)
    sin = np.zeros((hd, L), np.float32)
    partner = np.zeros(hd, np.int64)
    for j in range(hd):
        comp = row if j < half else col
        jj = j % half
        fi = jj % nf
        ang = (comp * inv[fi]).astype(np.float32)
        cos[j] = np.cos(ang)
        s = np.sin(ang)
        if jj < nf:
            sin[j] = -s
            partner[j] = j + nf
        else:
            sin[j] = s
            partner[j] = j - nf
    return cos, sin, partner


_CONST = None


def _constants():
    global _CONST
    if _CONST is not None:
        return _CONST
    c = {}
    cosG, sinG, partG = _rope_tables(64)
    cosM, sinM, partM = _rope_tables(32)
    c["ropeG_cos"] = np.ascontiguousarray(np.tile(cosG, (2, 1)))
    c["ropeG_sin"] = np.ascontiguousarray(np.tile(sinG, (2, 1)))
    c["ropeM_cos"] = cosM
    c["ropeM_sin"] = sinM
    c["_partG"] = partG
    c["_partM"] = partM
    c["ident_f"] = np.eye(128, dtype=np.float32)
    c["ident_b"] = np.eye(128, dtype=np.float32).astype(NPBF)
    bd = np.zeros((128, 128), np.float32)
    bd[:64, :64] = 1
    bd[64:, 64:] = 1
    c["bdones"] = bd.astype(NPBF)
    sel = np.zeros((32, 96), np.float32)
    sel[np.arange(32), np.arange(32)] = 1
    c["sel32"] = sel.astype(NPBF)
    t = np.arange(L, dtype=np.float32)
    t_norm = t / np.float32(L)
    w = (np.float32(2.0 * math.pi) * t / np.float32(L)).astype(np.float32)
    bands = np.linspace(1e-4, 7, 8, dtype=np.float32)
    fw = (w[:, None] * bands[None]).astype(np.float32)
    feats = np.concatenate([t_norm[:, None], np.cos(fw), -np.sin(fw)], axis=-1).astype(np.float32)
    c["featsT"] = np.ascontiguousarray(feats.T)
    max_decay = math.log(1e-2) / 0.3
    min_decay = math.log(1e-2) / 1.5
    deltas = np.abs(np.linspace(min_decay, max_decay, D, dtype=np.float32))
    c["window"] = np.exp(-t_norm[:, None] * deltas[None]).astype(np.float32)
    idx = np.arange(L, dtype=np.int64)
    ph = (np.outer(idx, idx) % 4096).astype(np.float64) * (2.0 * math.pi / 4096.0)
    FC = np.cos(ph)
    FS = np.sin(ph)
    FS[:, 0] = (-1.0) ** idx
    FST = FS.T.copy()

    def tile_t(M):
        return np.ascontiguousarray(M.reshape(16, 128, 16, 128).transpose(2, 1, 0, 3)).astype(NPBF)

    def tile_w(M):
        return np.ascontiguousarray(M.reshape(16, 128, 4, 512).transpose(2, 1, 0, 3)).astype(NPBF)
    c["FCt"] = tile_t(FC)
    c["FSt"] = tile_t(FS)
    c["FCw"] = tile_w(FC)
    c["FSTw"] = tile_w(FST)
    _CONST = c
    return c


def _prep(inp):
    c = _constants()
    partG, partM = c["_partG"], c["_partM"]
    sh = {k: v for k, v in c.items() if not k.startswith("_")}
    f32 = lambda a: np.ascontiguousarray(np.asarray(a, np.float32))
    sh["w_mod"] = f32(inp["w_mod"])
    sh["b_modT"] = np.ascontiguousarray(f32(inp["b_mod"]).reshape(2, 48, 128).transpose(2, 0, 1))
    sh["g_n1"] = np.ascontiguousarray(f32(inp["norm1_g"]).reshape(2, 8, 128).transpose(2, 0, 1))
    sh["g_n2"] = np.ascontiguousarray(f32(inp["norm2_g"]).reshape(2, 8, 128).transpose(2, 0, 1))
    sh["g_fin"] = _fm(inp["final_g"], 8)
    sh["mlp_w1"] = f32(inp["mlp_w1"])
    sh["mlp_w2"] = f32(inp["mlp_w2"])
    win = f32(inp["a_w_in"][0])
    cq, ckv, kr, gq, gk, gv = np.split(win, [384, 640, 672, 1184, 1312], axis=1)
    gq_sw = gq.reshape(D, 8, 64)[:, :, partG].reshape(D, 512)
    gk_sw = gk.reshape(D, 2, 64)[:, :, partG].reshape(D, 128)
    kr_sw = kr[:, partM]
    sh["a_w_in_x"] = np.ascontiguousarray(np.concatenate([cq, ckv, kr, kr_sw, gq, gq_sw, gk, gk_sw, gv], axis=1))
    wqb = f32(inp["a_w_q_b"][0]).reshape(384, 8, 96)
    nope, rope = wqb[:, :, :64], wqb[:, :, 64:]
    main = np.concatenate([rope, nope], axis=2)
    swp = rope[:, :, partM]
    sh["a_w_qb_x"] = np.ascontiguousarray(np.concatenate([main.reshape(384, 768), swp.reshape(384, 256)], axis=1))
    wkv = f32(inp["a_w_kv_b"][0]).reshape(256, 8, 128)
    knp = np.zeros((256, 8, 96), np.float32)
    knp[:, :, 32:] = wkv[:, :, :64]
    sh["a_w_kn"] = np.ascontiguousarray(knp.reshape(256, 768))
    sh["a_w_v"] = np.ascontiguousarray(wkv[:, :, 64:].reshape(256, 512))
    sh["a_w_out"] = f32(inp["a_w_out"][0])
    sh["g_qa"] = _fm(inp["a_q_a_g"][0], 3)
    sh["g_kva"] = _fm(inp["a_kv_a_g"][0], 2)
    qn = f32(inp["a_q_norm_g"][0])
    kn = f32(inp["a_k_norm_g"][0])
    sh["g_hn"] = np.ascontiguousarray(np.stack([np.tile(qn, 2), np.tile(qn[partG], 2),
                                                np.tile(kn, 2), np.tile(kn[partG], 2)], axis=1))
    sh["h_w_in"] = f32(inp["h_w_in"][0])
    sh["h_conv_wT"] = np.ascontiguousarray(f32(inp["h_conv_w"][0]).reshape(3, 24, 128).transpose(2, 1, 0))
    sh["h_conv_bT"] = _fm(inp["h_conv_b"][0], 24)
    sh["h_f_w1"] = f32(inp["h_f_w1"][0])
    sh["h_f_w2"] = f32(inp["h_f_w2"][0])
    sh["h_f_w3"] = f32(inp["h_f_w3"][0])
    sh["h_f_w4"] = f32(inp["h_f_w4"][0])
    sh["h_f_b"] = np.ascontiguousarray(np.stack([f32(inp["h_f_b1"][0]), f32(inp["h_f_b2"][0]), f32(inp["h_f_b3"][0])], axis=1))
    sh["h_f_fr"] = np.ascontiguousarray(f32(inp["h_f_freq"][0]).T)
    sh["h_skip_b"] = np.ascontiguousarray(np.broadcast_to(f32(inp["h_skip"][0]).reshape(1, 2048), (128, 2048)))
    sh["h_w_out"] = f32(inp["h_w_out"][0])
    x = f32(inp["x"])
    ctx = f32(inp["ctx"])
    cc = f32(inp["c"])
    cctx = f32(inp["c_ctx"])
    per = []
    for k in range(8):
        d = dict(sh)
        d["x"] = np.ascontiguousarray(x[2 * k:2 * k + 2].reshape(NT, D))
        d["ctx"] = np.ascontiguousarray(ctx[2 * k:2 * k + 2].reshape(512, D))
        cT = np.stack([cc[2 * k], cc[2 * k + 1], cctx], axis=1)
        d["cT"] = np.ascontiguousarray(cT.reshape(8, 128, 3).transpose(1, 0, 2))
        per.append(d)
    return per


def build(shapes, dbg=False, stage=2, phases=None):
    nc = bass.Bass("TRN2", target_bir_lowering=False)
    IN = {}
    for k, (shp, dt) in shapes.items():
        IN[k] = nc.dram_tensor(k, list(shp), BF16 if dt == "bf16" else F32, kind="ExternalInput").ap()
    OUT = nc.dram_tensor("out", [NT, D], F32, kind="ExternalOutput").ap()
    DBG = {}
    if dbg:
        for nm in ("xa0", "xm0", "xa1"):
            DBG[nm] = nc.dram_tensor("dbg_" + nm, [D, NT], F32, kind="ExternalOutput").ap()

    def scratch(name, shape, dt):
        return nc.dram_tensor(name, list(shape), dt).ap()
    XT = scratch("XT", [D, NT], F32)
    XTv = XT.rearrange("(c p) t -> p c t", p=128)
    W_in = scratch("W_in", [D, 2112], BF16)
    W_qb = scratch("W_qb", [384, 1024], BF16)
    W_kn = scratch("W_kn", [256, 768], BF16)
    W_v = scratch("W_v", [256, 512], BF16)
    W_ao = scratch("W_ao", [D, D], BF16)
    W_1 = [scratch(f"W_1_{i}", [D, 4096], BF16) for i in range(2)]
    W_2 = [scratch(f"W_2_{i}", [4096, D], BF16) for i in range(2)]
    W_hi = scratch("W_hi", [D, 3072], BF16)
    W_ho = scratch("W_ho", [D, D], BF16)
    QM = scratch("QM", [8, 96, NT], BF16)
    QG = scratch("QG", [8, 64, NT], BF16)
    KM = scratch("KM", [2, 8, 96, NK], BF16)
    KG = scratch("KG", [2, 2, 64, NK], BF16)
    VM = scratch("VM", [2, NK, 8, 128], BF16)
    VG = scratch("VG", [2, NK, 2, 128], BF16)
    X12T = scratch("X12T", [2048, NT], BF16)
    KA = scratch("KA", [2048, 2048], F32)
    KB = scratch("KB", [2048, 2048], F32)
    KD0 = scratch("KD0", [1, 2048], F32)

    with ExitStack() as top:
        P = Prog(nc, top)
        ps = [top.enter_context(nc.psum_tensor(f"ps{i}", [128, 512], F32)) for i in range(6)]
        psb = [top.enter_context(nc.psum_tensor(f"psb{i}", [128, 1024], BF16)) for i in range(2)]
        bank = [0]

        def nb():
            i = bank[0] % 6
            bank[0] += 1
            return ps[i], f"ps{i}"

        sbn = [0]

        def SB(st, name, shape, dt):
            sbn[0] += 1
            return st.enter_context(nc.sbuf_tensor("sb%d_%s" % (sbn[0], name), list(shape), dt))

        def dma(eng, out, in_, reads, writes, key):
            P.op(eng, lambda e, o=out, i=in_: e.dma_start(out=o, in_=i), reads=reads, writes=writes, dma_key=key)

        def mm(out, lhsT, rhs, start, stop, reads, writes):
            P.op("pe", lambda e, o=out, a=lhsT, b=rhs, s=start, t=stop: e.matmul(o, a, b, start=s, stop=t),
                 reads=reads, writes=writes)

        def tt(eng, out, a, b, op, reads, writes):
            P.op(eng, lambda e, o=out, x=a, y=b, p=op: e.tensor_tensor(o, x, y, p), reads=reads, writes=writes)

        def ts(eng, out, a, s1, s2, op0, op1, reads, writes):
            P.op(eng, lambda e, o=out, x=a, u=s1, v=s2, p=op0, q=op1: e.tensor_scalar(o, x, u, v, p, q),
                 reads=reads, writes=writes)

        def stt(eng, out, a, s, b, op0, op1, reads, writes):
            P.op(eng, lambda e, o=out, x=a, u=s, y=b, p=op0, q=op1: e.scalar_tensor_tensor(o, x, u, y, p, q),
                 reads=reads, writes=writes)

        def act(out, in_, func, reads, writes, bias=None, scale=None):
            kw = {}
            if bias is not None:
                kw["bias"] = bias
            if scale is not None:
                kw["scale"] = scale
            P.op("act", lambda e, o=out, i=in_, f=func, kw=kw: e.activation(o, i, f, **kw), reads=reads, writes=writes)

        def cp(eng, out, in_, reads, writes):
            if eng == "act":
                P.op("act", lambda e, o=out, i=in_: e.copy(o, i), reads=reads, writes=writes)
            else:
                P.op(eng, lambda e, o=out, i=in_: e.tensor_copy(o, i), reads=reads, writes=writes)

        def memset(eng, ap, val, writes):
            P.op(eng, lambda e, a=ap, v=val: e.memset(a, v), writes=writes)

        modT = SB(top, "modT", [128, 2, 48, 3], F32)
        A1 = SB(top, "A1", [128, 2, 8, 3], F32)
        A2 = SB(top, "A2", [128, 2, 8, 3], F32)
        gn1 = SB(top, "gn1", [128, 2, 8], F32)
        gn2 = SB(top, "gn2", [128, 2, 8], F32)
        gfin = SB(top, "gfin", [128, 8], F32)
        eps_t = SB(top, "eps_t", [128, 1], F32)
        ones_b = SB(top, "ones_b", [128, 128], BF16)
        ident_f = SB(top, "ident_f", [128, 128], F32)
        ident_b = SB(top, "ident_b", [128, 128], BF16)
        memset("dve", eps_t[:], EPS, ["eps_t"])
        memset("dve", ones_b[:], 1.0, ["ones_b"])
        dma("sp", ident_f[:], IN["ident_f"], [], ["ident_f"], "c0")
        dma("sp", ident_b[:], IN["ident_b"], [], ["ident_b"], "c0")
        dma("sp", gn1[:], IN["g_n1"], [], ["gn1"], "c0")
        dma("sp", gn2[:], IN["g_n2"], [], ["gn2"], "c0")
        dma("sp", gfin[:], IN["g_fin"], [], ["gfin"], "c0")

        def cast(dst, src, key, nsplit=1):
            n = src.shape[0]
            step = n // nsplit
            for i in range(nsplit):
                dma("pool", dst[i * step:(i + 1) * step, :], src[i * step:(i + 1) * step, :], [], [key], "cast_" + key)
        cast(W_in, IN["a_w_in_x"], "W_in", 2)
        cast(W_qb, IN["a_w_qb_x"], "W_qb")
        cast(W_kn, IN["a_w_kn"], "W_kn")
        cast(W_v, IN["a_w_v"], "W_v")
        cast(W_ao, IN["a_w_out"], "W_ao")
        cast(W_1[0], IN["mlp_w1"][0], "W_1_0", 4)
        cast(W_2[0], IN["mlp_w2"][0], "W_2_0", 4)
        cast(W_hi, IN["h_w_in"], "W_hi", 4)
        cast(W_ho, IN["h_w_out"], "W_ho")
        cast(W_1[1], IN["mlp_w1"][1], "W_1_1", 4)
        cast(W_2[1], IN["mlp_w2"][1], "W_2_1", 4)

        with ExitStack() as st:
            cT = SB(st, "cT", [128, 8, 3], F32)
            bm = SB(st, "bm", [128, 2, 48], F32)
            wm = [SB(st, f"wm{i}", [128, 8, 768], F32) for i in range(2)]
            dma("sp", cT[:], IN["cT"], [], ["cT"], "a0")
            dma("sp", bm[:], IN["b_modT"], [], ["bm"], "a0")
            act(cT[:], cT[:], AF.Silu, ["cT"], ["cT"])
            for i in range(2):
                wv = IN["w_mod"][i].rearrange("(kc p) n -> p kc n", p=128)
                for g in range(8):
                    s = (i * 8 + g) % 2
                    dma("sp", wm[s][:], wv[:, :, g * 768:(g + 1) * 768], [], [f"wm{s}"], f"wm{s}")
                    b_, bk = nb()
                    for m in range(6):
                        for kc in range(8):
                            mm(b_[:, m * 3:m * 3 + 3], wm[s][:, kc, m * 128:(m + 1) * 128], cT[:, kc, :],
                               kc == 0, kc == 7, [f"wm{s}", "cT"], [bk])
                    for m in range(6):
                        ch = g * 6 + m
                        ts("dve", modT[:, i, ch, :], b_[:, m * 3:m * 3 + 3], bm[:, i, ch:ch + 1], 0.0, ALU.add, ALU.add,
                           [bk, "bm"], ["modT"])
            for i in range(2):
                for (Ax, gx, which, nm) in ((A1, gn1, 1, "A1"), (A2, gn2, 4, "A2")):
                    for dc in range(8):
                        ts("dve", Ax[:, i, dc, :], modT[:, i, which * 8 + dc, :], 1.0, gx[:, i, dc:dc + 1], ALU.add, ALU.mult,
                           ["modT", "gn1", "gn2"], [nm])
            P.flush()

        def MODV(i, which, dc, j):
            return modT[:, i, which * 8 + dc, j:j + 1]

        def rstd_from(psap, pk, out, ok, n):
            act(out, psap, AF.Sqrt, [pk, "eps_t"], [ok], bias=eps_t[:], scale=1.0 / n)
            P.op("dve", lambda e, o=out: e.reciprocal(o, o), reads=[ok], writes=[ok])

        def load_xT(dst, dk, t0, key):
            dma("sp", dst[:], XTv[:, :, t0:t0 + 512], ["XT"], [dk], key)

        def norm_mod(xT, xk, hl, hk, sq, sqk, rs, rsk, tmp, tmpk, Ax, i, shw, j, hl_of=None):
            for dc in range(8):
                act(sq[:, dc, :], xT[:, dc, :], AF.Square, [xk], [sqk])
            b_, bk = nb()
            for dc in range(8):
                mm(b_[:], ones_b[:], sq[:, dc, :], dc == 0, dc == 7, ["ones_b", sqk], [bk])
            rstd_from(b_[:], bk, rs[:], rsk, 1024.0)
            for dc in range(8):
                s = dc % 2
                tt("dve", tmp[s][:], xT[:, dc, :], rs[:], ALU.mult, [xk, rsk], [tmpk + str(s)])
                act(hl_of(dc) if hl_of is not None else hl[:, dc, :], tmp[s][:], AF.Identity, [tmpk + str(s), "modT", "A1", "A2"], [hk],
                    bias=MODV(i, shw, dc, j), scale=Ax[:, i, dc, j:j + 1])

        want = (lambda ph: (phases is None and stage >= 1) or (phases is not None and ph in phases))
        P.skip = not want('C')
        with ExitStack() as st:
            win = SB(st, "win", [128, 8, 2112], BF16)
            wqb = SB(st, "wqb", [128, 3, 1024], BF16)
            wkn = SB(st, "wkn", [128, 2, 768], BF16)
            wv = SB(st, "wv", [128, 2, 512], BF16)
            sel32 = SB(st, "sel32", [32, 96], BF16)
            bdo = SB(st, "bdo", [128, 128], BF16)
            gqa = SB(st, "gqa", [128, 3], F32)
            gkva = SB(st, "gkva", [128, 2], F32)
            ghn = SB(st, "ghn", [128, 4], F32)
            tabG = SB(st, "tabG", [128, 4, L], F32)
            tabM = SB(st, "tabM", [32, 2, L], F32)
            xrow = [SB(st, f"xrow{i}", [128, D], F32) for i in range(2)]
            xT = SB(st, "xT", [128, 8, 512], F32)
            sq = SB(st, "sq", [128, 8, 512], BF16)
            hl = SB(st, "hl", [128, 8, 512], BF16)
            rs = SB(st, "rs", [128, 512], F32)
            rs2 = SB(st, "rs2", [128, 512], F32)
            tmp = [SB(st, f"tmp{i}", [128, 512], F32) for i in range(4)]
            c32 = SB(st, "c32", [128, 3, 512], F32)
            cqn = SB(st, "cqn", [128, 3, 512], BF16)
            ckvn = SB(st, "ckvn", [128, 2, 512], BF16)
            krr = SB(st, "krr", [32, 512], BF16)
            qo = [SB(st, f"qo{i}", [128, 512], BF16) for i in range(2)]
            vst = [SB(st, f"vst{i}", [128, 8, 128], BF16) for i in range(2)]
            vgs = [SB(st, f"vgs{i}", [128, 2, 128], BF16) for i in range(2)]
            dma("sp", win[:], W_in.rearrange("(c p) n -> p c n", p=128), ["W_in"], ["win"], "c1")
            dma("sp", wqb[:], W_qb.rearrange("(c p) n -> p c n", p=128), ["W_qb"], ["wqb"], "c1")
            dma("sp", wkn[:], W_kn.rearrange("(c p) n -> p c n", p=128), ["W_kn"], ["wkn"], "c1")
            dma("sp", wv[:], W_v.rearrange("(c p) n -> p c n", p=128), ["W_v"], ["wv"], "c1")
            dma("sp", sel32[:], IN["sel32"], [], ["sel32"], "c1")
            dma("sp", bdo[:], IN["bdones"], [], ["bdo"], "c1")
            dma("sp", gqa[:], IN["g_qa"], [], ["gqa"], "c1")
            dma("sp", gkva[:], IN["g_kva"], [], ["gkva"], "c1")
            dma("sp", ghn[:], IN["g_hn"], [], ["ghn"], "c1")
            dma("sp", tabG[:, 0, :], IN["ropeG_cos"], [], ["tabG"], "c1")
            dma("sp", tabG[:, 1, :], IN["ropeG_sin"], [], ["tabG"], "c1")
            dma("sp", tabG[:, 2, :], IN["ropeG_cos"], [], ["tabG"], "c1")
            dma("sp", tabG[:, 3, :], IN["ropeG_sin"], [], ["tabG"], "c1")
            dma("sp", tabM[:, 0, :], IN["ropeM_cos"], [], ["tabM"], "c1")
            dma("sp", tabM[:, 1, :], IN["ropeM_sin"], [], ["tabM"], "c1")
            for q in range(4):
                ts("pool" if q % 2 else "dve", tabG[:, q, :], tabG[:, q, :], ghn[:, q:q + 1], 0.0, ALU.mult, ALU.add,
                   ["tabG", "ghn"], ["tabG"])
            for i in range(2):
                memset("dve", vst[i][:], 1.0, [f"vst{i}"])
                memset("pool", vgs[i][:], 1.0, [f"vgs{i}"])

            def evac_copy(k, out, in_, reads, writes):
                cp("act" if k % 2 else "dve", out, in_, reads, writes)

            OFF_CKV, OFF_KR, OFF_KRS, OFF_GQ, OFF_GQS, OFF_GK, OFF_GKS, OFF_GV = 384, 640, 672, 704, 1216, 1728, 1856, 1984
            for blk in range(9):
                is_ctx = blk == 0
                src = IN["ctx"] if is_ctx else IN["x"]
                r0 = 0 if is_ctx else (blk - 1) * 512
                for dc in range(8):
                    pass
                for t4 in range(4):
                    s = t4 % 2
                    dma("sp", xrow[s][:], src[r0 + t4 * 128:r0 + (t4 + 1) * 128, :], [], [f"xrow{s}"], f"xrow{s}")
                    for dc in range(8):
                        b_, bk = nb()
                        P.op("pe", lambda e, o=b_[:, 0:128], i=xrow[s][:, dc * 128:(dc + 1) * 128]: e.transpose(o, i, ident_f[:]),
                             reads=[f"xrow{s}", "ident_f"], writes=[bk])
                        evac_copy(dc, xT[:, dc, t4 * 128:(t4 + 1) * 128], b_[:, 0:128], [bk], ["xT"])
                if not is_ctx:
                    dma("sp", XTv[:, :, r0:r0 + 512], xT[:], ["xT"], ["XT"], "xtst")
                if is_ctx:
                    norm_mod(xT, "xT", hl, "hl", sq, "sq", rs, "rs", tmp, "tmp", A1, 0, 0, 2)
                else:
                    lb = (blk - 1) // 4
                    norm_mod(xT, "xT", hl, "hl", sq, "sq", rs, "rs", tmp, "tmp", A1, 0, 0, lb)
                p0 = 0 if is_ctx else ((blk - 1) % 4) * 512

                def proj(col0, ncols, b_, bk, rows=128):
                    for dc in range(8):
                        mm(b_[0:ncols, :], win[:, dc, col0:col0 + ncols], hl[:, dc, :], dc == 0, dc == 7, ["win", "hl"], [bk])

                for (nm, col0, nch, gain, dst, dstk, do) in (("cq", 0, 3, gqa, cqn, "cqn", not is_ctx),
                                                             ("ckv", OFF_CKV, 2, gkva, ckvn, "ckvn", True)):
                    if not do:
                        continue
                    for m in range(nch):
                        b_, bk = nb()
                        proj(col0 + m * 128, 128, b_, bk)
                        cp("dve", c32[:, m, :], b_[:], [bk], ["c32"])
                        act(sq[:, m, :], c32[:, m, :], AF.Square, ["c32"], ["sq"])
                    b_, bk = nb()
                    for m in range(nch):
                        mm(b_[:], ones_b[:], sq[:, m, :], m == 0, m == nch - 1, ["ones_b", "sq"], [bk])
                    rstd_from(b_[:], bk, rs2[:], "rs2", float(nch * 128))
                    for m in range(nch):
                        stt("dve", dst[:, m, :], c32[:, m, :], gain[:, m:m + 1], rs2[:], ALU.mult, ALU.mult,
                            ["c32", "rs2", "gqa", "gkva"], [dstk])
                bm_, bmk = nb()
                proj(OFF_KR, 32, bm_, bmk)
                if is_ctx:
                    cp("dve", krr[:], bm_[0:32, :], [bmk], ["krr"])
                else:
                    bs_, bsk = nb()
                    proj(OFF_KRS, 32, bs_, bsk)
                    tt("dve", tmp[0][0:32, :], bm_[0:32, :], tabM[:, 0, p0:p0 + 512], ALU.mult, [bmk, "tabM"], ["tmp0"])
                    tt("dve", tmp[1][0:32, :], bs_[0:32, :], tabM[:, 1, p0:p0 + 512], ALU.mult, [bsk, "tabM"], ["tmp1"])
                    tt("pool", krr[:], tmp[0][0:32, :], tmp[1][0:32, :], ALU.add, ["tmp0", "tmp1"], ["krr"])
                if not is_ctx:
                    for h in range(8):
                        bm_, bmk = nb()
                        bs_, bsk = nb()
                        for kc in range(3):
                            mm(bm_[0:96, :], wqb[:, kc, h * 96:(h + 1) * 96], cqn[:, kc, :], kc == 0, kc == 2, ["wqb", "cqn"], [bmk])
                        for kc in range(3):
                            mm(bs_[0:32, :], wqb[:, kc, 768 + h * 32:768 + (h + 1) * 32], cqn[:, kc, :], kc == 0, kc == 2,
                               ["wqb", "cqn"], [bsk])
                        s = h % 2
                        tt("dve", tmp[0][0:32, :], bm_[0:32, :], tabM[:, 0, p0:p0 + 512], ALU.mult, [bmk, "tabM"], ["tmp0"])
                        tt("dve", tmp[1][0:32, :], bs_[0:32, :], tabM[:, 1, p0:p0 + 512], ALU.mult, [bsk, "tabM"], ["tmp1"])
                        tt("pool", qo[s][0:32, :], tmp[0][0:32, :], tmp[1][0:32, :], ALU.add, ["tmp0", "tmp1"], [f"qo{s}"])
                        cp("act", qo[s][32:64, :], bm_[32:64, :], [bmk], [f"qo{s}b", bmk])
                        cp("act", qo[s][64:96, :], bm_[64:96, :], [bmk], [f"qo{s}c", bmk])
                        dma("sp", QM[h, :, r0:r0 + 512], qo[s][0:96, :], [f"qo{s}", f"qo{s}b", f"qo{s}c"], ["QM"], f"qst{s}")
                for h in range(8):
                    b_, bk = nb()
                    for kc in range(2):
                        mm(b_[0:96, :], wkn[:, kc, h * 96:(h + 1) * 96], ckvn[:, kc, :], kc == 0, False, ["wkn", "ckvn"], [bk])
                    mm(b_[0:96, :], sel32[:], krr[:], False, True, ["sel32", "krr"], [bk])
                    s = h % 2
                    evac_copy(h, qo[s][0:96, :], b_[0:96, :], [bk], [f"qo{s}", f"qo{s}b", f"qo{s}c"])
                    if is_ctx:
                        for lb2 in range(2):
                            dma("sp", KM[lb2, h, :, 0:256], qo[s][0:96, lb2 * 256:(lb2 + 1) * 256], [f"qo{s}", f"qo{s}b"], ["KM"], f"qst{s}")
                    else:
                        dma("sp", KM[lb, h, :, 256 + p0:256 + p0 + 512], qo[s][0:96, :], [f"qo{s}", f"qo{s}b"], ["KM"], f"qst{s}")
                for t4 in range(4):
                    b_, bk = nb()
                    for kc in range(2):
                        mm(b_[:], ckvn[:, kc, t4 * 128:(t4 + 1) * 128], wv[:, kc, :], kc == 0, kc == 1, ["ckvn", "wv"], [bk])
                    s = t4 % 2
                    evac_copy(t4, vst[s][:, :, 0:64], b_[:].rearrange("p (h e) -> p h e", h=8), [bk], [f"vst{s}"])
                    if is_ctx:
                        lb2, kk = t4 // 2, (t4 % 2) * 128
                    else:
                        lb2, kk = lb, 256 + p0 + t4 * 128
                    dma("sp", VM[lb2, kk:kk + 128, :, :], vst[s][:], [f"vst{s}"], ["VM"], f"vstq{s}")
                for (cm, cs, tq, is_q) in [(OFF_GQ + m * 128, OFF_GQS + m * 128, 0, True) for m in range(4)] + [(OFF_GK, OFF_GKS, 2, False)]:
                    if is_q and is_ctx:
                        continue
                    bm_, bmk = nb()
                    proj(cm, 128, bm_, bmk)
                    act(sq[:, 0, :], bm_[:], AF.Square, [bmk], ["sq", bmk])
                    bq_, bqk = nb()
                    mm(bq_[:], bdo[:], sq[:, 0, :], True, True, ["bdo", "sq"], [bqk])
                    rstd_from(bq_[:], bqk, rs2[:], "rs2", 64.0)
                    m = (cm - OFF_GQ) // 128 if is_q else 0
                    s = m % 2
                    if is_ctx:
                        stt("dve", qo[s][:], bm_[:], ghn[:, 2:3], rs2[:], ALU.mult, ALU.mult, [bmk, "rs2", "ghn"], [f"qo{s}", f"qo{s}b", f"qo{s}c"])
                    else:
                        bs_, bsk = nb()
                        proj(cs, 128, bs_, bsk)
                        tt("dve", tmp[0][:], bm_[:], tabG[:, tq, p0:p0 + 512], ALU.mult, [bmk, "tabG"], ["tmp0"])
                        tt("dve", tmp[1][:], bs_[:], tabG[:, tq + 1, p0:p0 + 512], ALU.mult, [bsk, "tabG"], ["tmp1"])
                        tt("pool", tmp[2][:], tmp[0][:], tmp[1][:], ALU.add, ["tmp0", "tmp1"], ["tmp2"])
                        tt("dve", qo[s][:], tmp[2][:], rs2[:], ALU.mult, ["tmp2", "rs2"], [f"qo{s}", f"qo{s}b", f"qo{s}c"])
                    if is_q:
                        dma("sp", QG[2 * m:2 * m + 2, :, r0:r0 + 512].rearrange("h d t -> (h d) t"), qo[s][:],
                            [f"qo{s}", f"qo{s}b"], ["QG"], f"qst{s}")
                    elif is_ctx:
                        for lb2 in range(2):
                            dma("sp", KG[lb2, :, :, 0:256].rearrange("h d t -> (h d) t"), qo[s][:, lb2 * 256:(lb2 + 1) * 256],
                                [f"qo{s}", f"qo{s}b"], ["KG"], f"qst{s}")
                    else:
                        dma("sp", KG[lb, :, :, 256 + p0:256 + p0 + 512].rearrange("h d t -> (h d) t"), qo[s][:],
                            [f"qo{s}", f"qo{s}b"], ["KG"], f"qst{s}")
                for t4 in range(4):
                    b_, bk = nb()
                    for dc in range(8):
                        mm(b_[:, 0:128], hl[:, dc, t4 * 128:(t4 + 1) * 128], win[:, dc, OFF_GV:OFF_GV + 128], dc == 0, dc == 7,
                           ["hl", "win"], [bk])
                    s = t4 % 2
                    evac_copy(t4 + 1, vgs[s][:, :, 0:64], b_[:, 0:128].rearrange("p (h e) -> p h e", h=2), [bk], [f"vgs{s}"])
                    if is_ctx:
                        lb2, kk = t4 // 2, (t4 % 2) * 128
                    else:
                        lb2, kk = lb, 256 + p0 + t4 * 128
                    dma("sp", VG[lb2, kk:kk + 128, :, :], vgs[s][:], [f"vgs{s}"], ["VG"], f"vgsq{s}")
            P.flush()

        P.skip = not want('D')
        with ExitStack() as st:
            kms = SB(st, "kms", [96, 8, NK], BF16)
            kgs = SB(st, "kgs", [64, 2, NK], BF16)
            vms = SB(st, "vms", [128, 18, 8, 128], BF16)
            vgs2 = SB(st, "vgs2", [128, 18, 2, 128], BF16)
            wo = SB(st, "wo", [64, 16, D], BF16)
            qms = [SB(st, f"qms{i}", [96, 8, 512], BF16) for i in range(2)]
            qgs = [SB(st, f"qgs{i}", [64, 8, 512], BF16) for i in range(2)]
            pT = [SB(st, f"pT{i}", [128, 512], BF16) for i in range(4)]
            pcount = [0]
            rec = [SB(st, f"rec{i}", [128, 512], F32) for i in range(2)]
            on = SB(st, "on", [64, 16, 512], BF16)
            xo = [SB(st, f"xo{i}", [128, 512], F32) for i in range(2)]
            dma("sp", wo[:], W_ao.rearrange("(h d) n -> d h n", d=64), ["W_ao"], ["wo"], "d0")
            psb_f = [ps[4], ps[5]]
            b4 = [0]

            def nb4():
                i = b4[0] % 4
                b4[0] += 1
                return ps[i], f"ps{i}"
            SC_M = 1.0 / math.sqrt(96.0)
            SC_G = 0.125
            pi = 0
            for lb in range(2):
                dma("sp", kms[:], KM[lb].rearrange("h d t -> d h t"), ["KM"], ["kms"], "d1")
                dma("sp", kgs[:], KG[lb].rearrange("h d t -> d h t"), ["KG"], ["kgs"], "d1")
                dma("sp", vms[:], VM[lb].rearrange("(c p) h e -> p c h e", p=128), ["VM"], ["vms"], "d1")
                dma("sp", vgs2[:], VG[lb].rearrange("(c p) h e -> p c h e", p=128), ["VG"], ["vgs2"], "d1")
                for qb in range(4):
                    t0 = lb * L + qb * 512
                    s = qb % 2
                    dma("sp", qms[s][:], QM[:, :, t0:t0 + 512].rearrange("h d t -> d h t"), ["QM"], [f"qms{s}"], f"qld{s}")
                    dma("sp", qgs[s][:], QG[:, :, t0:t0 + 512].rearrange("h d t -> d h t"), ["QG"], [f"qgs{s}"], f"qld{s}")
                    items = [(h, kc) for h in range(16) for kc in range(18)]
                    sbank = {}

                    def S_E(i):
                        h, kc = items[i]
                        b_, bk = nb4()
                        if h < 8:
                            mm(b_[:], kms[:, h, kc * 128:(kc + 1) * 128], qms[s][:, h, :], True, True, ["kms", f"qms{s}"], [bk])
                            sc = SC_M
                        else:
                            g = (h - 8) // 4
                            mm(b_[:], kgs[:, g, kc * 128:(kc + 1) * 128], qgs[s][:, h - 8, :], True, True, ["kgs", f"qgs{s}"], [bk])
                            sc = SC_G
                        pp = pcount[0] % 4
                        pcount[0] += 1
                        act(pT[pp][:], b_[:], AF.Exp, [bk], [f"pT{pp}"], scale=sc)
                        sbank[i] = pp

                    def PV(i):
                        h, kc = items[i]
                        oa, oak = psb_f[h % 2], f"ps{4 + h % 2}"
                        pp = sbank.pop(i)
                        if h < 8:
                            va, vk = vms[:, kc, h, :], "vms"
                        else:
                            va, vk = vgs2[:, kc, (h - 8) // 4, :], "vgs2"
                        mm(oa[:], va, pT[pp][:], kc == 0, kc == 17, [vk, f"pT{pp}"], [oak])
                        if kc == 17:
                            rr = h % 2
                            P.op("dve", lambda e, o=rec[rr][64:128, :], i_=oa[64:128, :]: e.reciprocal(o, i_), reads=[oak], writes=[f"rec{rr}"])
                            tt("dve", on[:, h, :], oa[0:64, :], rec[rr][64:128, :], ALU.mult, [oak, f"rec{rr}"], ["on"])
                    LA = 2
                    for i in range(LA):
                        S_E(i)
                    for i in range(len(items)):
                        if i + LA < len(items):
                            S_E(i + LA)
                        PV(i)
                    for dc in range(8):
                        b_, bk = nb4()
                        for h in range(16):
                            mm(b_[:], wo[:, h, dc * 128:(dc + 1) * 128], on[:, h, :], h == 0, h == 15, ["wo", "on"], [bk])
                        xs = dc % 2
                        dma("sp", xo[xs][:], XT[dc * 128:(dc + 1) * 128, t0:t0 + 512], ["XT"], [f"xo{xs}"], f"xold{xs}")
                        stt("dve", xo[xs][:], b_[:], MODV(0, 2, dc, lb), xo[xs][:], ALU.mult, ALU.add, [bk, "modT", f"xo{xs}"], [f"xo{xs}"])
                        dma("sp", XT[dc * 128:(dc + 1) * 128, t0:t0 + 512], xo[xs][:], [f"xo{xs}"], ["XT"], f"xost{xs}")
            P.flush()
        P.skip = False
        if dbg:
            dma("sp", DBG["xa0"], XT, ["XT"], [], "dbg")

        def mlp_phase(i):
            with ExitStack() as st:
                xT = [SB(st, f"mxT{k}", [128, 8, 512], F32) for k in range(2)]
                sq = SB(st, "msq", [128, 8, 512], BF16)
                xn = SB(st, "mxn", [128, 8, 512], BF16)
                rs = SB(st, "mrs", [128, 512], F32)
                tmp = [SB(st, f"mtmp{k}", [128, 512], F32) for k in range(2)]
                aT = SB(st, "maT", [128, 32, 512], BF16)
                w1t = [SB(st, f"w1t{k}", [128, 8, 512], BF16) for k in range(3)]
                w2t = [SB(st, f"w2t{k}", [128, 4, 512], BF16) for k in range(3)]
                r1 = [SB(st, f"mr1{k}", [128, 512], F32) for k in range(2)]
                w1v = W_1[i].rearrange("(kc p) n -> p kc n", p=128)
                w2v = W_2[i].rearrange("(fc p) n -> p fc n", p=128)
                n1 = 0
                n2 = 0
                for tb in range(8):
                    lb = tb // 4
                    t0 = tb * 512
                    xs = tb % 2
                    load_xT(xT[xs], f"mxT{xs}", t0, f"mxld{xs}")
                    norm_mod(xT[xs], f"mxT{xs}", xn, "mxn", sq, "msq", rs, "mrs", tmp, "mtmp", A2, i, 3, lb)
                    for g in range(8):
                        ws = n1 % 3
                        n1 += 1
                        dma("sp", w1t[ws][:], w1v[:, :, g * 512:(g + 1) * 512], [f"W_1_{i}"], [f"w1t{ws}"], f"w1q{ws}")
                        for m in range(4):
                            b_, bk = nb()
                            for kc in range(8):
                                mm(b_[:], w1t[ws][:, kc, m * 128:(m + 1) * 128], xn[:, kc, :], kc == 0, kc == 7, [f"w1t{ws}", "mxn"], [bk])
                            fc = g * 4 + m
                            rk = fc % 2
                            act(r1[rk][:], b_[:], AF.Relu, [bk], [f"mr1{rk}"])
                            tt("dve" if fc % 4 else "pool", aT[:, fc, :], r1[rk][:], r1[rk][:], ALU.mult, [f"mr1{rk}"], ["maT"])
                    for half in range(2):
                        bks = [nb() for _ in range(4)]
                        for g in range(8):
                            ws = n2 % 3
                            n2 += 1
                            dma("sp", w2t[ws][:], w2v[:, g * 4:(g + 1) * 4, half * 512:(half + 1) * 512], [f"W_2_{i}"], [f"w2t{ws}"], f"w2q{ws}")
                            for f4 in range(4):
                                fc = g * 4 + f4
                                for m in range(4):
                                    mm(bks[m][0][:], w2t[ws][:, f4, m * 128:(m + 1) * 128], aT[:, fc, :], fc == 0, fc == 31,
                                       [f"w2t{ws}", "maT"], [bks[m][1]])
                        for m in range(4):
                            dc = half * 4 + m
                            stt("dve", xT[xs][:, dc, :], bks[m][0][:], MODV(i, 5, dc, lb), xT[xs][:, dc, :], ALU.mult, ALU.add,
                                [bks[m][1], "modT", f"mxT{xs}"], [f"mxT{xs}"])
                    dma("sp", XTv[:, :, t0:t0 + 512], xT[xs][:], [f"mxT{xs}"], ["XT"], f"mxst{xs}")
                P.flush()

        def hyena_phase():
            S2 = 2.0 / 4096.0
            MAGIC = 12582912.0
            TWO_PI = 2.0 * math.pi
            with ExitStack() as st:
                feats = SB(st, "feats", [17, L], F32)
                fw1 = SB(st, "fw1", [17, 64], F32)
                fw2 = SB(st, "fw2", [64, 64], F32)
                fw3 = SB(st, "fw3", [64, 64], F32)
                fw4 = SB(st, "fw4", [64, 4096], F32)
                fb = SB(st, "fb", [64, 3], F32)
                ffr = SB(st, "ffr", [64, 3], F32)
                fbf = SB(st, "fbf", [64, 3], F32)
                ones_f = SB(st, "ones_f", [128, 128], F32)
                hbuf = [SB(st, f"hbuf{k}", [64, L], F32) for k in range(2)]
                fa = SB(st, "fa", [64, L], F32)
                fk = SB(st, "fk", [64, L], F32)
                wint = [SB(st, f"wint{k}", [128, 1024], F32) for k in range(2)]
                hw = [SB(st, f"hw{k}", [128, 2048], F32) for k in range(2)]
                sqt = SB(st, "sqt", [128, 2048], F32)
                acc = SB(st, "acc", [128, 2048], F32)
                Gs = SB(st, "Gs", [128, 16, 1024], BF16)
                Hs = SB(st, "Hs", [128, 16, 1024], BF16)
                rs2 = SB(st, "frs2", [128, 1024], F32)
                sk2 = SB(st, "fsk2", [128, 1024], F32)
                ktmp = [SB(st, f"ktmp{k}", [128, 1024], F32) for k in range(3)]
                fct = [SB(st, f"fct{k}", [128, 16, 128], BF16) for k in range(2)]
                fst = [SB(st, f"fst{k}", [128, 16, 128], BF16) for k in range(2)]
                dma("sp", feats[:], IN["featsT"], [], ["feats"], "f")
                dma("sp", fw1[:], IN["h_f_w1"], [], ["fw1"], "f")
                dma("sp", fw2[:], IN["h_f_w2"], [], ["fw2"], "f")
                dma("sp", fw3[:], IN["h_f_w3"], [], ["fw3"], "f")
                dma("sp", fw4[:], IN["h_f_w4"], [], ["fw4"], "f")
                dma("sp", fb[:], IN["h_f_b"], [], ["fb"], "f")
                dma("sp", ffr[:], IN["h_f_fr"], [], ["ffr"], "f")
                memset("dve", ones_f[:], 1.0, ["ones_f"])
                tt("dve", fbf[:], fb[:], ffr[:], ALU.mult, ["fb", "ffr"], ["fbf"])
                src, srck, kdim = feats, "feats", 17
                wl = [(fw1, "fw1"), (fw2, "fw2"), (fw3, "fw3")]
                for l in range(3):
                    dst, dstk = hbuf[l % 2], f"hbuf{l % 2}"
                    for q in range(4):
                        b_, bk = nb()
                        mm(b_[0:64, :], wl[l][0][0:kdim, :], src[0:kdim, q * 512:(q + 1) * 512], True, True, [wl[l][1], srck], [bk])
                        act(fa[:, q * 512:(q + 1) * 512], b_[0:64, :], AF.Identity, [bk, "ffr", "fbf"], ["fa"],
                            bias=fbf[:, l:l + 1], scale=ffr[:, l:l + 1])
                    ts("dve", fk[:], fa[:], 1.0 / TWO_PI, 0.0, ALU.mult, ALU.add, ["fa"], ["fk"])
                    ts("dve", fk[:], fk[:], MAGIC, 0.0, ALU.add, ALU.add, ["fk"], ["fk"])
                    ts("dve", fk[:], fk[:], -MAGIC, 0.0, ALU.add, ALU.add, ["fk"], ["fk"])
                    stt("dve", fa[:], fk[:], -TWO_PI, fa[:], ALU.mult, ALU.add, ["fk", "fa"], ["fa"])
                    act(dst[:], fa[:], AF.Sin, ["fa"], [dstk], scale=0.9999995)
                    src, srck, kdim = dst, dstk, 64
                h3, h3k = src, srck
                for o in range(2):
                    memset("pool", acc[:], 0.0, ["acc"])
                    for tti in range(16):
                        ws = tti % 2
                        dma("sp", wint[ws][:], IN["window"][tti * 128:(tti + 1) * 128, :], [], [f"wint{ws}"], "f")
                        hs = tti % 2
                        for d_ in range(2):
                            for hf_ in range(2):
                                b_, bk = nb()
                                c0 = d_ * 2048 + o * 1024 + hf_ * 512
                                mm(b_[:], h3[:, tti * 128:(tti + 1) * 128], fw4[:, c0:c0 + 512], True, True, [h3k, "fw4"], [bk])
                                tt("dve", hw[hs][:, d_ * 1024 + hf_ * 512:d_ * 1024 + (hf_ + 1) * 512], b_[:],
                                   wint[ws][:, hf_ * 512:(hf_ + 1) * 512], ALU.mult, [bk, f"wint{ws}"], [f"hw{hs}"])
                        tt("pool", sqt[:], hw[hs][:], hw[hs][:], ALU.mult, [f"hw{hs}"], ["sqt"])
                        tt("pool", acc[:], acc[:], sqt[:], ALU.add, ["acc", "sqt"], ["acc"])
                        if tti == 0:
                            memset("dve", hw[hs][0:1, 1024:2048], 0.0, [f"hw{hs}"])
                        tt("pool", Gs[:, tti, :], hw[hs][:, 0:1024], hw[hs][:, 1024:2048], ALU.add, [f"hw{hs}"], ["Gs"])
                        tt("dve", Hs[:, tti, :], hw[hs][:, 1024:2048], hw[hs][:, 0:1024], ALU.subtract, [f"hw{hs}"], ["Hs"])
                    for hf_ in range(2):
                        b0, b0k = nb()
                        b1, b1k = nb()
                        mm(b0[:], ones_f[:], acc[:, hf_ * 512:(hf_ + 1) * 512], True, True, ["ones_f", "acc"], [b0k])
                        mm(b1[:], ones_f[:], acc[:, 1024 + hf_ * 512:1024 + (hf_ + 1) * 512], True, True, ["ones_f", "acc"], [b1k])
                        cp("dve", ktmp[0][:, 0:512], b0[:], [b0k], ["ktmp0"])
                        tt("dve", ktmp[0][:, 0:512], ktmp[0][:, 0:512], b1[:], ALU.add, ["ktmp0", b1k], ["ktmp0"])
                        act(rs2[:, hf_ * 512:(hf_ + 1) * 512], ktmp[0][:, 0:512], AF.Sqrt, ["ktmp0", "eps_t"], ["frs2"], bias=eps_t[:], scale=1.0)
                    P.op("dve", lambda e, o_=rs2[:]: e.reciprocal(o_, o_), reads=["frs2"], writes=["frs2"])
                    ts("dve", rs2[:], rs2[:], S2, 0.0, ALU.mult, ALU.add, ["frs2"], ["frs2"])
                    dma("sp", sk2[:], IN["h_skip_b"][:, o * 1024:(o + 1) * 1024], [], ["fsk2"], "f")
                    ts("dve", sk2[:], sk2[:], S2, 0.0, ALU.mult, ALU.add, ["fsk2"], ["fsk2"])
                    for fc in range(16):
                        s_ = fc % 2
                        dma("sp", fct[s_][:], IN["FCt"][fc], [], [f"fct{s_}"], "f")
                        dma("sp", fst[s_][:], IN["FSt"][fc], [], [f"fst{s_}"], "f")
                        jobs = [(fct[s_], f"fct{s_}", Gs, "Gs", 0), (fst[s_], f"fst{s_}", Hs, "Hs", 1)]
                        if fc == 0:
                            jobs.append((fst[s_], f"fst{s_}", Gs, "Gs", 2))
                        for (mt, mk, dat, dk_, kind) in jobs:
                            kt, ktk = ktmp[kind], f"ktmp{kind}"
                            for hf_ in range(2):
                                b_, bk = nb()
                                for tti in range(16):
                                    mm(b_[:], mt[:, tti, :], dat[:, tti, hf_ * 512:(hf_ + 1) * 512], tti == 0, tti == 15, [mk, dk_], [bk])
                                tt("dve", kt[:, hf_ * 512:(hf_ + 1) * 512], b_[:], rs2[:, hf_ * 512:(hf_ + 1) * 512], ALU.mult,
                                   [bk, "frs2"], [ktk])
                            if kind != 1:
                                tt("pool", kt[:], kt[:], sk2[:], ALU.add, [ktk, "fsk2"], [ktk])
                            if fc == 0:
                                if kind == 1:
                                    memset("dve", kt[0:1, :], 0.0, [ktk])
                                else:
                                    ts("dve", kt[0:1, :], kt[0:1, :], 0.5, 0.0, ALU.mult, ALU.add, [ktk], [ktk])
                            if kind == 0:
                                dma("sp", KA[fc * 128:(fc + 1) * 128, o * 1024:(o + 1) * 1024], kt[:], [ktk], ["KA"], "f")
                            elif kind == 1:
                                dma("sp", KB[fc * 128:(fc + 1) * 128, o * 1024:(o + 1) * 1024], kt[:], [ktk], ["KB"], "f")
                            else:
                                dma("sp", KD0[0:1, o * 1024:(o + 1) * 1024], kt[0:1, :], [ktk], ["KD0"], "f")
                P.flush()

            for lb in range(2):
                with ExitStack() as bst:
                    u = SB(bst, "u", [128, 16, 1024], BF16)
                    with ExitStack() as st:
                        hlT = SB(st, "hlT", [128, 8, L], BF16)
                        xT = SB(st, "hxT", [128, 8, 512], F32)
                        sq = SB(st, "hsq", [128, 8, 512], BF16)
                        rs = SB(st, "hrs", [128, 512], F32)
                        tmp = [SB(st, f"htmp{k}", [128, 512], F32) for k in range(2)]
                        zbuf = [SB(st, f"zbuf{k}", [128, L + 2], F32) for k in range(2)]
                        ztmp = SB(st, "ztmp", [128, L], F32)
                        zc = [SB(st, f"zc{k}", [128, L], BF16) for k in range(2)]
                        whi = [SB(st, f"whi{k}", [128, 8, 128], BF16) for k in range(2)]
                        cw = SB(st, "cw", [128, 24, 3], F32)
                        cb = SB(st, "cb", [128, 24], F32)
                        dma("sp", cw[:], IN["h_conv_wT"], [], ["cw"], "h")
                        dma("sp", cb[:], IN["h_conv_bT"], [], ["cb"], "h")
                        for k in range(2):
                            memset("pool", zbuf[k][:], 0.0, [f"zbuf{k}"])
                        for tb in range(4):
                            t0 = lb * L + tb * 512
                            load_xT(xT, "hxT", t0, "h")
                            norm_mod(xT, "hxT", None, "hlT", sq, "hsq", rs, "hrs", tmp, "htmp", A1, 1, 0, lb,
                                     hl_of=lambda dc, tb=tb: hlT[:, dc, tb * 512:(tb + 1) * 512])
                        whv = W_hi.rearrange("(dc p) n -> p dc n", p=128)
                        for cc in range(24):
                            s_ = cc % 2
                            dma("sp", whi[s_][:], whv[:, :, cc * 128:(cc + 1) * 128], ["W_hi"], [f"whi{s_}"], "h")
                            for tb in range(4):
                                b_, bk = nb()
                                for dc in range(8):
                                    mm(b_[:], whi[s_][:, dc, :], hlT[:, dc, tb * 512:(tb + 1) * 512], dc == 0, dc == 7, [f"whi{s_}", "hlT"], [bk])
                                cp("act", zbuf[s_][:, 1 + tb * 512:1 + (tb + 1) * 512], b_[:], [bk], [f"zbuf{s_}"])
                            ts("dve", ztmp[:], zbuf[s_][:, 0:L], cw[:, cc, 0:1], cb[:, cc:cc + 1], ALU.mult, ALU.add,
                               [f"zbuf{s_}", "cw", "cb"], ["ztmp"])
                            stt("dve", ztmp[:], zbuf[s_][:, 1:L + 1], cw[:, cc, 1:2], ztmp[:], ALU.mult, ALU.add,
                                [f"zbuf{s_}", "cw", "ztmp"], ["ztmp"])
                            stt("dve", zc[s_][:], zbuf[s_][:, 2:L + 2], cw[:, cc, 2:3], ztmp[:], ALU.mult, ALU.add,
                                [f"zbuf{s_}", "cw", "ztmp"], [f"zc{s_}"])
                            if cc < 16:
                                dma("sp", X12T[cc * 128:(cc + 1) * 128, lb * L:(lb + 1) * L], zc[s_][:], [f"zc{s_}"], ["X12T"], "h")
                            else:
                                vc = cc - 16
                                for g8 in range(2):
                                    pb, pbk = psb[g8], f"psb{g8}"
                                    for j in range(8):
                                        blk = g8 * 8 + j
                                        P.op("pe", lambda e, o_=pb[:, j * 128:(j + 1) * 128], i_=zc[s_][:, blk * 128:(blk + 1) * 128]:
                                             e.transpose(o_, i_, ident_b[:]), reads=[f"zc{s_}", "ident_b"], writes=[pbk])
                                    cp("act" if g8 else "dve", u[:, g8 * 8:(g8 + 1) * 8, vc * 128:(vc + 1) * 128],
                                       pb[:].rearrange("p (a b) -> p a b", a=8), [pbk], ["u"])
                        P.flush()
                    with ExitStack() as cst:
                        Yre = SB(cst, "Yre", [128, 16, 1024], BF16)
                        Yz = SB(cst, "Yz", [128, 16, 1024], BF16)
                        for o in range(2):
                            with ExitStack() as st:
                                fct = [SB(st, f"cfct{k}", [128, 16, 128], BF16) for k in range(2)]
                                fst = [SB(st, f"cfst{k}", [128, 16, 128], BF16) for k in range(2)]
                                kat = [SB(st, f"kat{k}", [128, 1024], F32) for k in range(2)]
                                kbt = [SB(st, f"kbt{k}", [128, 1024], F32) for k in range(2)]
                                kdt = SB(st, "kdt", [128, 1024], F32)
                                t1 = [SB(st, f"ct{k}", [128, 512], F32) for k in range(4)]
                                for fc in range(16):
                                    s_ = fc % 2
                                    dma("sp", fct[s_][:], IN["FCt"][fc], [], [f"cfct{s_}"], "h")
                                    dma("sp", fst[s_][:], IN["FSt"][fc], [], [f"cfst{s_}"], "h")
                                    dma("sp", kat[s_][:], KA[fc * 128:(fc + 1) * 128, o * 1024:(o + 1) * 1024], ["KA"], [f"kat{s_}"], "h")
                                    dma("sp", kbt[s_][:], KB[fc * 128:(fc + 1) * 128, o * 1024:(o + 1) * 1024], ["KB"], [f"kbt{s_}"], "h")
                                    if fc == 0:
                                        cp("pool", kdt[:], kat[s_][:], [f"kat{s_}"], ["kdt"])
                                        dma("sp", kdt[0:1, :], KD0[0:1, o * 1024:(o + 1) * 1024], ["KD0"], ["kdt"], "h")
                                        dd, ddk = kdt, "kdt"
                                    else:
                                        dd, ddk = kat[s_], f"kat{s_}"
                                    for hf_ in range(2):
                                        bc, bck = nb()
                                        bs, bsk = nb()
                                        cs = slice(hf_ * 512, (hf_ + 1) * 512)
                                        for tti in range(16):
                                            mm(bc[:], fct[s_][:, tti, :], u[:, tti, cs], tti == 0, tti == 15, [f"cfct{s_}", "u"], [bck])
                                        for tti in range(16):
                                            mm(bs[:], fst[s_][:, tti, :], u[:, tti, cs], tti == 0, tti == 15, [f"cfst{s_}", "u"], [bsk])
                                        tt("dve", t1[0][:], bc[:], kat[s_][:, cs], ALU.mult, [bck, f"kat{s_}"], ["ct0"])
                                        tt("dve", t1[1][:], bs[:], kbt[s_][:, cs], ALU.mult, [bsk, f"kbt{s_}"], ["ct1"])
                                        tt("pool", Yre[:, fc, cs], t1[0][:], t1[1][:], ALU.add, ["ct0", "ct1"], ["Yre"])
                                        tt("dve", t1[2][:], bs[:], dd[:, cs], ALU.mult, [bsk, ddk], ["ct2"])
                                        tt("dve", t1[3][:], bc[:], kbt[s_][:, cs], ALU.mult, [bck, f"kbt{s_}"], ["ct3"])
                                        tt("pool", Yz[:, fc, cs], t1[2][:], t1[3][:], ALU.subtract, ["ct2", "ct3"], ["Yz"])
                                P.flush()
                            with ExitStack() as st:
                                fcw = SB(st, "fcw", [128, 16, 512], BF16)
                                fsw = SB(st, "fsw", [128, 16, 512], BF16)
                                gate = SB(st, "gate", [128, 8, 512], BF16)
                                y2T = SB(st, "y2T", [128, 8, 512], BF16)
                                if o == 1:
                                    who = SB(st, "who", [128, 8, D], BF16)
                                    xo = [SB(st, f"hxo{k}", [128, 512], F32) for k in range(2)]
                                    dma("sp", who[:], W_ho.rearrange("(c p) n -> p c n", p=128), ["W_ho"], ["who"], "h")
                                gv_ = X12T[o * 1024:(o + 1) * 1024, :].rearrange("(cc p) t -> p cc t", p=128)
                                for nbk in range(4):
                                    t0 = lb * L + nbk * 512
                                    dma("sp", fcw[:], IN["FCw"][nbk], [], ["fcw"], "h")
                                    dma("sp", fsw[:], IN["FSTw"][nbk], [], ["fsw"], "h")
                                    dma("sp", gate[:], gv_[:, :, t0:t0 + 512], ["X12T"], ["gate"], "h")
                                    for cc in range(8):
                                        b_, bk = nb()
                                        for fc in range(16):
                                            mm(b_[:], Yre[:, fc, cc * 128:(cc + 1) * 128], fcw[:, fc, :], fc == 0, False, ["Yre", "fcw"], [bk])
                                        for fc in range(16):
                                            mm(b_[:], Yz[:, fc, cc * 128:(cc + 1) * 128], fsw[:, fc, :], False, fc == 15, ["Yz", "fsw"], [bk])
                                        tt("dve", y2T[:, cc, :], b_[:], gate[:, cc, :], ALU.mult, [bk, "gate"], ["y2T"])
                                    if o == 0:
                                        for j in range(4):
                                            pb, pbk = psb[j % 2], f"psb{j % 2}"
                                            for cc in range(8):
                                                P.op("pe", lambda e, o_=pb[:, cc * 128:(cc + 1) * 128], i_=y2T[:, cc, j * 128:(j + 1) * 128]:
                                                     e.transpose(o_, i_, ident_b[:]), reads=["y2T", "ident_b"], writes=[pbk])
                                            cp("act" if j % 2 else "dve", u[:, nbk * 4 + j, :], pb[:], [pbk], ["u"])
                                    else:
                                        for dc in range(8):
                                            b_, bk = nb()
                                            for cc in range(8):
                                                mm(b_[:], who[:, cc, dc * 128:(dc + 1) * 128], y2T[:, cc, :], cc == 0, cc == 7, ["who", "y2T"], [bk])
                                            xs = dc % 2
                                            dma("sp", xo[xs][:], XT[dc * 128:(dc + 1) * 128, t0:t0 + 512], ["XT"], [f"hxo{xs}"], "h")
                                            stt("dve", xo[xs][:], b_[:], MODV(1, 2, dc, lb), xo[xs][:], ALU.mult, ALU.add,
                                                [bk, "modT", f"hxo{xs}"], [f"hxo{xs}"])
                                            dma("sp", XT[dc * 128:(dc + 1) * 128, t0:t0 + 512], xo[xs][:], [f"hxo{xs}"], ["XT"], "h")
                                P.flush()

        P.skip = not want('M')
        mlp_phase(0)
        P.skip = False
        if dbg:
            dma("sp", DBG["xm0"], XT, ["XT"], [], "dbg")
        if (phases is None and stage >= 2) or (phases is not None and 'H' in phases):
            hyena_phase()
        if dbg:
            dma("sp", DBG["xa1"], XT, ["XT"], [], "dbg")
        if (phases is None and stage >= 2) or (phases is not None and 'N' in phases):
            mlp_phase(1)

        with ExitStack() as st:
            xT = [SB(st, f"fxT{k}", [128, 8, 512], F32) for k in range(2)]
            sq = SB(st, "fsq", [128, 8, 512], BF16)
            rs = SB(st, "frs", [128, 512], F32)
            yT = SB(st, "fyT", [128, 8, 512], F32)
            orow = [SB(st, f"forow{k}", [128, D], F32) for k in range(2)]
            for tb in range(8):
                t0 = tb * 512
                xs = tb % 2
                load_xT(xT[xs], f"fxT{xs}", t0, f"fxld{xs}")
                for dc in range(8):
                    act(sq[:, dc, :], xT[xs][:, dc, :], AF.Square, [f"fxT{xs}"], ["fsq"])
                b_, bk = nb()
                for dc in range(8):
                    mm(b_[:], ones_b[:], sq[:, dc, :], dc == 0, dc == 7, ["ones_b", "fsq"], [bk])
                rstd_from(b_[:], bk, rs[:], "frs", 1024.0)
                for dc in range(8):
                    stt("dve", yT[:, dc, :], xT[xs][:, dc, :], gfin[:, dc:dc + 1], rs[:], ALU.mult, ALU.mult,
                        [f"fxT{xs}", "gfin", "frs"], ["fyT"])
                for t4 in range(4):
                    os_ = t4 % 2
                    for dc in range(8):
                        b_, bk = nb()
                        P.op("pe", lambda e, o=b_[:, 0:128], i=yT[:, dc, t4 * 128:(t4 + 1) * 128]: e.transpose(o, i, ident_f[:]),
                             reads=["fyT", "ident_f"], writes=[bk])
                        cp("act" if dc % 2 else "dve", orow[os_][:, dc * 128:(dc + 1) * 128], b_[:, 0:128], [bk], [f"forow{os_}"])
                    dma("sp", OUT[t0 + t4 * 128:t0 + (t4 + 1) * 128, :], orow[os_][:], [f"forow{os_}"], [], f"fost{os_}")
            P.flush()
        P.wait_all_dma("sp")
        P.flush(barrier=False)
    return nc


_NC = {}


def _run(inputs, dbg=False, stage=2, phases=None):
    per = _prep(inputs)
    shapes = {k: (v.shape, "bf16" if v.dtype == NPBF else "f32") for k, v in per[0].items()}
    key = (dbg, stage, phases)
    if key not in _NC:
        _NC[key] = build(shapes, dbg=dbg, stage=stage, phases=phases)
    res = run_bass_kernel_spmd(_NC[key], per, core_ids=list(range(8)))
    return res


def kernel(**inputs):
    res = _run(inputs)
    out = np.concatenate([np.asarray(r["out"], np.float32).reshape(2, L, D) for r in res.results], axis=0)
    return out
```

```python
import math
import numpy as np
import ml_dtypes
from contextlib import ExitStack
import concourse.bass as bass
import concourse.mybir as mybir
from concourse.bass_utils import run_bass_kernel_spmd

F32 = mybir.dt.float32
BF16 = mybir.dt.bfloat16
ALU = mybir.AluOpType
AF = mybir.ActivationFunctionType
NPBF = ml_dtypes.bfloat16

SAME_ENGINE_SYNC = True
D = 1024
NT = 4096
L = 2048
NK = 2304
EPS = 1e-6


class _Op:
    __slots__ = ("eng", "fn", "waits", "signal", "val", "key", "idx")


class Prog:
    ENGS = ("pe", "dve", "act", "pool", "sp")

    def __init__(self, nc, stack):
        self.nc = nc
        self.stack = stack
        self.streams = {e: [] for e in self.ENGS}
        self.sems = {}
        self.count = {}
        self.known = {e: {} for e in self.ENGS}
        self.res = {}
        self.base_idx = {e: 0 for e in self.ENGS}
        self.dma_n = {e: 0 for e in self.ENGS}
        for e in ("pe", "dve", "act", "pool"):
            self._sem(e)

    def _sem(self, tl):
        if tl not in self.sems:
            self.sems[tl] = self.stack.enter_context(self.nc.semaphore("s_" + str(tl)))
            self.count[tl] = 0
        return self.sems[tl]

    def _r(self, key):
        r = self.res.get(key)
        if r is None:
            r = [{}, {}]
            self.res[key] = r
        return r

    skip = False
    NPOOL = {"sp": 24, "pool": 8, "act": 4}

    def op(self, eng, fn, reads=(), writes=(), dma_key=None):
        if self.skip:
            return None
        o = _Op()
        o.eng = eng
        o.fn = fn
        o.signal = False
        o.val = None
        o.key = dma_key
        o.idx = self.base_idx[eng] + len(self.streams[eng])
        deps = {}

        def add(src):
            for tl, ev in src.items():
                if deps.get(tl, (-1,))[0] < ev[0]:
                    deps[tl] = ev
        for k in reads:
            add(self._r(k)[0])
        for k in writes:
            w, r = self._r(k)
            add(w)
            add(r)
        waits = []
        kn = self.known[eng]
        for tl, ev in deps.items():
            if tl == eng and (eng == "pe" or not SAME_ENGINE_SYNC):
                continue
            if kn.get(tl, -1) >= ev[0]:
                continue
            kn[tl] = ev[0]
            waits.append((tl, ev))
            if ev[1] is not None:
                ev[1].signal = True
        o.waits = waits
        if dma_key is not None:
            npool = self.NPOOL[eng]
            n = self.dma_n[eng]
            self.dma_n[eng] = n + 1
            mytl = "dq_%s_%d" % (eng, n % npool)
            self._sem(mytl)
            prev = self.count[mytl]
            if prev > 0 and kn.get(mytl, -1) < prev:
                kn[mytl] = prev
                waits.append((mytl, (prev, None)))
            self.count[mytl] = prev + 1
            myev = (prev + 1, None)
            o.key = mytl
        else:
            myev = (o.idx, o)
            mytl = eng
        for k in reads:
            self._r(k)[1][mytl] = myev
        for k in writes:
            r = self._r(k)
            r[0] = {mytl: myev}
            r[1] = {}
        self.streams[eng].append(o)
        return o

    def flush(self, barrier=True):
        nc = self.nc
        if barrier:
            self.wait_all_dma("sp", prefixes=("dq_sp_", "dq_act_"))
        for e in self.ENGS:
            for o in self.streams[e]:
                if o.key is None and o.signal:
                    self.count[e] += 1
                    o.val = self.count[e]
        streams = self.streams
        sems = self.sems

        def emit(eng_name, engine):
            for o in streams[eng_name]:
                for tl, ev in o.waits:
                    if ev[1] is None:
                        engine.wait_ge(sems[tl], 16 * ev[0])
                    else:
                        engine.wait_ge(sems[tl], ev[1].val)
                ins = o.fn(engine)
                if o.key is not None:
                    ins.then_inc(sems[o.key], 16)
                elif o.signal:
                    ins.then_inc(sems[eng_name], 1)

        with nc.Block() as block:
            if streams["pe"]:
                @block.tensor
                def _(t):
                    emit("pe", t)
            if streams["dve"]:
                @block.vector
                def _(v):
                    emit("dve", v)
            if streams["act"]:
                @block.scalar
                def _(s):
                    emit("act", s)
            if streams["pool"]:
                @block.gpsimd
                def _(g):
                    emit("pool", g)
            if streams["sp"]:
                @block.sync
                def _(s):
                    emit("sp", s)
        for e in self.ENGS:
            self.base_idx[e] += len(self.streams[e])
            self.streams[e] = []
        if barrier:
            nc.all_engine_barrier()
            for k, r in self.res.items():
                for d in (0, 1):
                    r[d] = {tl: ev for tl, ev in r[d].items() if ev[1] is None}
            for e in self.ENGS:
                self.known[e] = {tl: v for tl, v in self.known[e].items() if tl not in self.ENGS}

    def wait_all_dma(self, eng="sp", prefixes=("dq_",)):
        cnts = {tl: c for tl, c in self.count.items()
                if tl not in self.ENGS and c > 0 and str(tl).startswith(tuple(prefixes))}
        sems = self.sems

        def fn(engine, cnts=cnts):
            for tl, c in cnts.items():
                engine.wait_ge(sems[tl], 16 * c)
            return engine.nop()
        self.op(eng, fn)


def _fm(v, nch):
    return np.ascontiguousarray(np.asarray(v, np.float32).reshape(nch, 128).T)


def _rope_tables(hd):
    half = hd // 2
    nf = half // 2
    inv = (np.float32(10000.0) ** (-np.arange(nf, dtype=np.float32) / np.float32(nf))).astype(np.float32)
    pos = np.arange(L)
    row = (pos // 64).astype(np.float32)
    col = (pos % 64).astype(np.float32)
    cos = np.zeros((hd, L), np.float32)
    sin = np.zeros((hd, L), np.float32)
    partner = np.zeros(hd, np.int64)
    for j in range(hd):
        comp = row if j < half else col
        jj = j % half
        fi = jj % nf
        ang = (comp * inv[fi]).astype(np.float32)
        cos[j] = np.cos(ang)
        s = np.sin(ang)
        if jj < nf:
            sin[j] = -s
            partner[j] = j + nf
        else:
            sin[j] = s
            partner[j] = j - nf
    return cos, sin, partner


_CONST = None


def _constants():
    global _CONST
    if _CONST is not None:
        return _CONST
    c = {}
    cosG, sinG, partG = _rope_tables(64)
    cosM, sinM, partM = _rope_tables(32)
    c["ropeG_cos"] = np.ascontiguousarray(np.tile(cosG, (2, 1)))
    c["ropeG_sin"] = np.ascontiguousarray(np.tile(sinG, (2, 1)))
    c["ropeM_cos"] = cosM
    c["ropeM_sin"] = sinM
    c["_partG"] = partG
    c["_partM"] = partM
    c["ident_f"] = np.eye(128, dtype=np.float32)
    c["ident_b"] = np.eye(128, dtype=np.float32).astype(NPBF)
    bd = np.zeros((128, 128), np.float32)
    bd[:64, :64] = 1
    bd[64:, 64:] = 1
    c["bdones"] = bd.astype(NPBF)
    sel = np.zeros((32, 96), np.float32)
    sel[np.arange(32), np.arange(32)] = 1
    c["sel32"] = sel.astype(NPBF)
    t = np.arange(L, dtype=np.float32)
    t_norm = t / np.float32(L)
    w = (np.float32(2.0 * math.pi) * t / np.float32(L)).astype(np.float32)
    bands = np.linspace(1e-4, 7, 8, dtype=np.float32)
    fw = (w[:, None] * bands[None]).astype(np.float32)
    feats = np.concatenate([t_norm[:, None], np.cos(fw), -np.sin(fw)], axis=-1).astype(np.float32)
    c["featsT"] = np.ascontiguousarray(feats.T)
    max_decay = math.log(1e-2) / 0.3
    min_decay = math.log(1e-2) / 1.5
    deltas = np.abs(np.linspace(min_decay, max_decay, D, dtype=np.float32))
    c["window"] = np.exp(-t_norm[:, None] * deltas[None]).astype(np.float32)
    idx = np.arange(L, dtype=np.int64)
    ph = (np.outer(idx, idx) % 4096).astype(np.float64) * (2.0 * math.pi / 4096.0)
    FC = np.cos(ph)
    FS = np.sin(ph)
    FS[:, 0] = (-1.0) ** idx
    FST = FS.T.copy()

    def tile_t(M):
        return np.ascontiguousarray(M.reshape(16, 128, 16, 128).transpose(2, 1, 0, 3)).astype(NPBF)

    def tile_w(M):
        return np.ascontiguousarray(M.reshape(16, 128, 4, 512).transpose(2, 1, 0, 3)).astype(NPBF)
    c["FCt"] = tile_t(FC)
    c["FSt"] = tile_t(FS)
    c["FCw"] = tile_w(FC)
    c["FSTw"] = tile_w(FST)
    _CONST = c
    return c


def _prep(inp):
    c = _constants()
    partG, partM = c["_partG"], c["_partM"]
    sh = {k: v for k, v in c.items() if not k.startswith("_")}
    f32 = lambda a: np.ascontiguousarray(np.asarray(a, np.float32))
    sh["w_mod"] = f32(inp["w_mod"])
    sh["b_modT"] = np.ascontiguousarray(f32(inp["b_mod"]).reshape(2, 48, 128).transpose(2, 0, 1))
    sh["g_n1"] = np.ascontiguousarray(f32(inp["norm1_g"]).reshape(2, 8, 128).transpose(2, 0, 1))
    sh["g_n2"] = np.ascontiguousarray(f32(inp["norm2_g"]).reshape(2, 8, 128).transpose(2, 0, 1))
    sh["g_fin"] = _fm(inp["final_g"], 8)
    sh["mlp_w1"] = f32(inp["mlp_w1"])
    sh["mlp_w2"] = f32(inp["mlp_w2"])
    win = f32(inp["a_w_in"][0])
    cq, ckv, kr, gq, gk, gv = np.split(win, [384, 640, 672, 1184, 1312], axis=1)
    gq_sw = gq.reshape(D, 8, 64)[:, :, partG].reshape(D, 512)
    gk_sw = gk.reshape(D, 2, 64)[:, :, partG].reshape(D, 128)
    kr_sw = kr[:, partM]
    sh["a_w_in_x"] = np.ascontiguousarray(np.concatenate([cq, ckv, kr, kr_sw, gq, gq_sw, gk, gk_sw, gv], axis=1))
    wqb = f32(inp["a_w_q_b"][0]).reshape(384, 8, 96)
    nope, rope = wqb[:, :, :64], wqb[:, :, 64:]
    main = np.concatenate([rope, nope], axis=2)
    swp = rope[:, :, partM]
    sh["a_w_qb_x"] = np.ascontiguousarray(np.concatenate([main.reshape(384, 768), swp.reshape(384, 256)], axis=1))
    wkv = f32(inp["a_w_kv_b"][0]).reshape(256, 8, 128)
    knp = np.zeros((256, 8, 96), np.float32)
    knp[:, :, 32:] = wkv[:, :, :64]
    sh["a_w_kn"] = np.ascontiguousarray(knp.reshape(256, 768))
    sh["a_w_v"] = np.ascontiguousarray(wkv[:, :, 64:].reshape(256, 512))
    sh["a_w_out"] = f32(inp["a_w_out"][0])
    sh["g_qa"] = _fm(inp["a_q_a_g"][0], 3)
    sh["g_kva"] = _fm(inp["a_kv_a_g"][0], 2)
    qn = f32(inp["a_q_norm_g"][0])
    kn = f32(inp["a_k_norm_g"][0])
    sh["g_hn"] = np.ascontiguousarray(np.stack([np.tile(qn, 2), np.tile(qn[partG], 2),
                                                np.tile(kn, 2), np.tile(kn[partG], 2)], axis=1))
    sh["h_w_in"] = f32(inp["h_w_in"][0])
    sh["h_conv_wT"] = np.ascontiguousarray(f32(inp["h_conv_w"][0]).reshape(3, 24, 128).transpose(2, 1, 0))
    sh["h_conv_bT"] = _fm(inp["h_conv_b"][0], 24)
    sh["h_f_w1"] = f32(inp["h_f_w1"][0])
    sh["h_f_w2"] = f32(inp["h_f_w2"][0])
    sh["h_f_w3"] = f32(inp["h_f_w3"][0])
    sh["h_f_w4"] = f32(inp["h_f_w4"][0])
    sh["h_f_b"] = np.ascontiguousarray(np.stack([f32(inp["h_f_b1"][0]), f32(inp["h_f_b2"][0]), f32(inp["h_f_b3"][0])], axis=1))
    sh["h_f_fr"] = np.ascontiguousarray(f32(inp["h_f_freq"][0]).T)
    sh["h_skip_b"] = np.ascontiguousarray(np.broadcast_to(f32(inp["h_skip"][0]).reshape(1, 2048), (128, 2048)))
    sh["h_w_out"] = f32(inp["h_w_out"][0])
    x = f32(inp["x"])
    ctx = f32(inp["ctx"])
    cc = f32(inp["c"])
    cctx = f32(inp["c_ctx"])
    per = []
    for k in range(8):
        d = dict(sh)
        d["x"] = np.ascontiguousarray(x[2 * k:2 * k + 2].reshape(NT, D))
        d["ctx"] = np.ascontiguousarray(ctx[2 * k:2 * k + 2].reshape(512, D))
        cT = np.stack([cc[2 * k], cc[2 * k + 1], cctx], axis=1)
        d["cT"] = np.ascontiguousarray(cT.reshape(8, 128, 3).transpose(1, 0, 2))
        per.append(d)
    return per


def build(shapes, dbg=False, stage=2, phases=None):
    nc = bass.Bass("TRN2", target_bir_lowering=False)
    IN = {}
    for k, (shp, dt) in shapes.items():
        IN[k] = nc.dram_tensor(k, list(shp), BF16 if dt == "bf16" else F32, kind="ExternalInput").ap()
    OUT = nc.dram_tensor("out", [NT, D], F32, kind="ExternalOutput").ap()
    DBG = {}
    if dbg:
        for nm in ("xa0", "xm0", "xa1"):
            DBG[nm] = nc.dram_tensor("dbg_" + nm, [D, NT], F32, kind="ExternalOutput").ap()

    def scratch(name, shape, dt):
        return nc.dram_tensor(name, list(shape), dt).ap()
    XT = scratch("XT", [D, NT], F32)
    XTv = XT.rearrange("(c p) t -> p c t", p=128)
    W_in = scratch("W_in", [D, 2112], BF16)
    W_qb = scratch("W_qb", [384, 1024], BF16)
    W_kn = scratch("W_kn", [256, 768], BF16)
    W_v = scratch("W_v", [256, 512], BF16)
    W_ao = scratch("W_ao", [D, D], BF16)
    W_1 = [scratch(f"W_1_{i}", [D, 4096], BF16) for i in range(2)]
    W_2 = [scratch(f"W_2_{i}", [4096, D], BF16) for i in range(2)]
    W_hi = scratch("W_hi", [D, 3072], BF16)
    W_ho = scratch("W_ho", [D, D], BF16)
    QM = scratch("QM", [8, 96, NT], BF16)
    QG = scratch("QG", [8, 64, NT], BF16)
    KM = scratch("KM", [2, 8, 96, NK], BF16)
    KG = scratch("KG", [2, 2, 64, NK], BF16)
    VM = scratch("VM", [2, NK, 8, 128], BF16)
    VG = scratch("VG", [2, NK, 2, 128], BF16)
    X12T = scratch("X12T", [2048, NT], BF16)
    KA = scratch("KA", [2048, 2048], F32)
    KB = scratch("KB", [2048, 2048], F32)
    KD0 = scratch("KD0", [1, 2048], F32)

    with ExitStack() as top:
        P = Prog(nc, top)
        ps = [top.enter_context(nc.psum_tensor(f"ps{i}", [128, 512], F32)) for i in range(6)]
        psb = [top.enter_context(nc.psum_tensor(f"psb{i}", [128, 1024], BF16)) for i in range(2)]
        bank = [0]

        def nb():
            i = bank[0] % 6
            bank[0] += 1
            return ps[i], f"ps{i}"

        sbn = [0]

        def SB(st, name, shape, dt):
            sbn[0] += 1
            return st.enter_context(nc.sbuf_tensor("sb%d_%s" % (sbn[0], name), list(shape), dt))

        def dma(eng, out, in_, reads, writes, key):
            P.op(eng, lambda e, o=out, i=in_: e.dma_start(out=o, in_=i), reads=reads, writes=writes, dma_key=key)

        def mm(out, lhsT, rhs, start, stop, reads, writes):
            P.op("pe", lambda e, o=out, a=lhsT, b=rhs, s=start, t=stop: e.matmul(o, a, b, start=s, stop=t),
                 reads=reads, writes=writes)

        def tt(eng, out, a, b, op, reads, writes):
            P.op(eng, lambda e, o=out, x=a, y=b, p=op: e.tensor_tensor(o, x, y, p), reads=reads, writes=writes)

        def ts(eng, out, a, s1, s2, op0, op1, reads, writes):
            P.op(eng, lambda e, o=out, x=a, u=s1, v=s2, p=op0, q=op1: e.tensor_scalar(o, x, u, v, p, q),
                 reads=reads, writes=writes)

        def stt(eng, out, a, s, b, op0, op1, reads, writes):
            P.op(eng, lambda e, o=out, x=a, u=s, y=b, p=op0, q=op1: e.scalar_tensor_tensor(o, x, u, y, p, q),
                 reads=reads, writes=writes)

        def act(out, in_, func, reads, writes, bias=None, scale=None):
            kw = {}
            if bias is not None:
                kw["bias"] = bias
            if scale is not None:
                kw["scale"] = scale
            P.op("act", lambda e, o=out, i=in_, f=func, kw=kw: e.activation(o, i, f, **kw), reads=reads, writes=writes)

        def cp(eng, out, in_, reads, writes):
            if eng == "act":
                P.op("act", lambda e, o=out, i=in_: e.copy(o, i), reads=reads, writes=writes)
            else:
                P.op(eng, lambda e, o=out, i=in_: e.tensor_copy(o, i), reads=reads, writes=writes)

        def memset(eng, ap, val, writes):
            P.op(eng, lambda e, a=ap, v=val: e.memset(a, v), writes=writes)

        modT = SB(top, "modT", [128, 2, 48, 3], F32)
        A1 = SB(top, "A1", [128, 2, 8, 3], F32)
        A2 = SB(top, "A2", [128, 2, 8, 3], F32)
        gn1 = SB(top, "gn1", [128, 2, 8], F32)
        gn2 = SB(top, "gn2", [128, 2, 8], F32)
        gfin = SB(top, "gfin", [128, 8], F32)
        eps_t = SB(top, "eps_t", [128, 1], F32)
        ones_b = SB(top, "ones_b", [128, 128], BF16)
        ident_f = SB(top, "ident_f", [128, 128], F32)
        ident_b = SB(top, "ident_b", [128, 128], BF16)
        memset("dve", eps_t[:], EPS, ["eps_t"])
        memset("dve", ones_b[:], 1.0, ["ones_b"])
        dma("sp", ident_f[:], IN["ident_f"], [], ["ident_f"], "c0")
        dma("sp", ident_b[:], IN["ident_b"], [], ["ident_b"], "c0")
        dma("sp", gn1[:], IN["g_n1"], [], ["gn1"], "c0")
        dma("sp", gn2[:], IN["g_n2"], [], ["gn2"], "c0")
        dma("sp", gfin[:], IN["g_fin"], [], ["gfin"], "c0")

        def cast(dst, src, key, nsplit=1):
            n = src.shape[0]
            step = n // nsplit
            for i in range(nsplit):
                dma("pool", dst[i * step:(i + 1) * step, :], src[i * step:(i + 1) * step, :], [], [key], "cast_" + key)
        cast(W_in, IN["a_w_in_x"], "W_in", 2)
        cast(W_qb, IN["a_w_qb_x"], "W_qb")
        cast(W_kn, IN["a_w_kn"], "W_kn")
        cast(W_v, IN["a_w_v"], "W_v")
        cast(W_ao, IN["a_w_out"], "W_ao")
        cast(W_1[0], IN["mlp_w1"][0], "W_1_0", 4)
        cast(W_2[0], IN["mlp_w2"][0], "W_2_0", 4)
        cast(W_hi, IN["h_w_in"], "W_hi", 4)
        cast(W_ho, IN["h_w_out"], "W_ho")
        cast(W_1[1], IN["mlp_w1"][1], "W_1_1", 4)
        cast(W_2[1], IN["mlp_w2"][1], "W_2_1", 4)

        with ExitStack() as st:
            cT = SB(st, "cT", [128, 8, 3], F32)
            bm = SB(st, "bm", [128, 2, 48], F32)
            wm = [SB(st, f"wm{i}", [128, 8, 768], F32) for i in range(2)]
            dma("sp", cT[:], IN["cT"], [], ["cT"], "a0")
            dma("sp", bm[:], IN["b_modT"], [], ["bm"], "a0")
            act(cT[:], cT[:], AF.Silu, ["cT"], ["cT"])
            for i in range(2):
                wv = IN["w_mod"][i].rearrange("(kc p) n -> p kc n", p=128)
                for g in range(8):
                    s = (i * 8 + g) % 2
                    dma("sp", wm[s][:], wv[:, :, g * 768:(g + 1) * 768], [], [f"wm{s}"], f"wm{s}")
                    b_, bk = nb()
                    for m in range(6):
                        for kc in range(8):
                            mm(b_[:, m * 3:m * 3 + 3], wm[s][:, kc, m * 128:(m + 1) * 128], cT[:, kc, :],
                               kc == 0, kc == 7, [f"wm{s}", "cT"], [bk])
                    for m in range(6):
                        ch = g * 6 + m
                        ts("dve", modT[:, i, ch, :], b_[:, m * 3:m * 3 + 3], bm[:, i, ch:ch + 1], 0.0, ALU.add, ALU.add,
                           [bk, "bm"], ["modT"])
            for i in range(2):
                for (Ax, gx, which, nm) in ((A1, gn1, 1, "A1"), (A2, gn2, 4, "A2")):
                    for dc in range(8):
                        ts("dve", Ax[:, i, dc, :], modT[:, i, which * 8 + dc, :], 1.0, gx[:, i, dc:dc + 1], ALU.add, ALU.mult,
                           ["modT", "gn1", "gn2"], [nm])
            P.flush()

        def MODV(i, which, dc, j):
            return modT[:, i, which * 8 + dc, j:j + 1]

        def rstd_from(psap, pk, out, ok, n):
            act(out, psap, AF.Sqrt, [pk, "eps_t"], [ok], bias=eps_t[:], scale=1.0 / n)
            P.op("dve", lambda e, o=out: e.reciprocal(o, o), reads=[ok], writes=[ok])

        def load_xT(dst, dk, t0, key):
            dma("sp", dst[:], XTv[:, :, t0:t0 + 512], ["XT"], [dk], key)

        def norm_mod(xT, xk, hl, hk, sq, sqk, rs, rsk, tmp, tmpk, Ax, i, shw, j, hl_of=None):
            for dc in range(8):
                act(sq[:, dc, :], xT[:, dc, :], AF.Square, [xk], [sqk])
            b_, bk = nb()
            for dc in range(8):
                mm(b_[:], ones_b[:], sq[:, dc, :], dc == 0, dc == 7, ["ones_b", sqk], [bk])
            rstd_from(b_[:], bk, rs[:], rsk, 1024.0)
            for dc in range(8):
                s = dc % 2
                tt("dve", tmp[s][:], xT[:, dc, :], rs[:], ALU.mult, [xk, rsk], [tmpk + str(s)])
                act(hl_of(dc) if hl_of is not None else hl[:, dc, :], tmp[s][:], AF.Identity, [tmpk + str(s), "modT", "A1", "A2"], [hk],
                    bias=MODV(i, shw, dc, j), scale=Ax[:, i, dc, j:j + 1])

        want = (lambda ph: (phases is None and stage >= 1) or (phases is not None and ph in phases))
        P.skip = not want('C')
        with ExitStack() as st:
            win = SB(st, "win", [128, 8, 2112], BF16)
            wqb = SB(st, "wqb", [128, 3, 1024], BF16)
            wkn = SB(st, "wkn", [128, 2, 768], BF16)
            wv = SB(st, "wv", [128, 2, 512], BF16)
            sel32 = SB(st, "sel32", [32, 96], BF16)
            bdo = SB(st, "bdo", [128, 128], BF16)
            gqa = SB(st, "gqa", [128, 3], F32)
            gkva = SB(st, "gkva", [128, 2], F32)
            ghn = SB(st, "ghn", [128, 4], F32)
            tabG = SB(st, "tabG", [128, 4, L], F32)
            tabM = SB(st, "tabM", [32, 2, L], F32)
            xrow = [SB(st, f"xrow{i}", [128, D], F32) for i in range(2)]
            xT = SB(st, "xT", [128, 8, 512], F32)
            sq = SB(st, "sq", [128, 8, 512], BF16)
            hl = SB(st, "hl", [128, 8, 512], BF16)
            rs = SB(st, "rs", [128, 512], F32)
            rs2 = SB(st, "rs2", [128, 512], F32)
            tmp = [SB(st, f"tmp{i}", [128, 512], F32) for i in range(4)]
            c32 = SB(st, "c32", [128, 3, 512], F32)
            cqn = SB(st, "cqn", [128, 3, 512], BF16)
            ckvn = SB(st, "ckvn", [128, 2, 512], BF16)
            krr = SB(st, "krr", [32, 512], BF16)
            qo = [SB(st, f"qo{i}", [128, 512], BF16) for i in range(2)]
            vst = [SB(st, f"vst{i}", [128, 8, 128], BF16) for i in range(2)]
            vgs = [SB(st, f"vgs{i}", [128, 2, 128], BF16) for i in range(2)]
            dma("sp", win[:], W_in.rearrange("(c p) n -> p c n", p=128), ["W_in"], ["win"], "c1")
            dma("sp", wqb[:], W_qb.rearrange("(c p) n -> p c n", p=128), ["W_qb"], ["wqb"], "c1")
            dma("sp", wkn[:], W_kn.rearrange("(c p) n -> p c n", p=128), ["W_kn"], ["wkn"], "c1")
            dma("sp", wv[:], W_v.rearrange("(c p) n -> p c n", p=128), ["W_v"], ["wv"], "c1")
            dma("sp", sel32[:], IN["sel32"], [], ["sel32"], "c1")
            dma("sp", bdo[:], IN["bdones"], [], ["bdo"], "c1")
            dma("sp", gqa[:], IN["g_qa"], [], ["gqa"], "c1")
            dma("sp", gkva[:], IN["g_kva"], [], ["gkva"], "c1")
            dma("sp", ghn[:], IN["g_hn"], [], ["ghn"], "c1")
            dma("sp", tabG[:, 0, :], IN["ropeG_cos"], [], ["tabG"], "c1")
            dma("sp", tabG[:, 1, :], IN["ropeG_sin"], [], ["tabG"], "c1")
            dma("sp", tabG[:, 2, :], IN["ropeG_cos"], [], ["tabG"], "c1")
            dma("sp", tabG[:, 3, :], IN["ropeG_sin"], [], ["tabG"], "c1")
            dma("sp", tabM[:, 0, :], IN["ropeM_cos"], [], ["tabM"], "c1")
            dma("sp", tabM[:, 1, :], IN["ropeM_sin"], [], ["tabM"], "c1")
            for q in range(4):
                ts("pool" if q % 2 else "dve", tabG[:, q, :], tabG[:, q, :], ghn[:, q:q + 1], 0.0, ALU.mult, ALU.add,
                   ["tabG", "ghn"], ["tabG"])
            for i in range(2):
                memset("dve", vst[i][:], 1.0, [f"vst{i}"])
                memset("pool", vgs[i][:], 1.0, [f"vgs{i}"])

            def evac_copy(k, out, in_, reads, writes):
                cp("act" if k % 2 else "dve", out, in_, reads, writes)

            OFF_CKV, OFF_KR, OFF_KRS, OFF_GQ, OFF_GQS, OFF_GK, OFF_GKS, OFF_GV = 384, 640, 672, 704, 1216, 1728, 1856, 1984
            for blk in range(9):
                is_ctx = blk == 0
                src = IN["ctx"] if is_ctx else IN["x"]
                r0 = 0 if is_ctx else (blk - 1) * 512
                for dc in range(8):
                    pass
                for t4 in range(4):
                    s = t4 % 2
                    dma("sp", xrow[s][:], src[r0 + t4 * 128:r0 + (t4 + 1) * 128, :], [], [f"xrow{s}"], f"xrow{s}")
                    for dc in range(8):
                        b_, bk = nb()
                        P.op("pe", lambda e, o=b_[:, 0:128], i=xrow[s][:, dc * 128:(dc + 1) * 128]: e.transpose(o, i, ident_f[:]),
                             reads=[f"xrow{s}", "ident_f"], writes=[bk])
                        evac_copy(dc, xT[:, dc, t4 * 128:(t4 + 1) * 128], b_[:, 0:128], [bk], ["xT"])
                if not is_ctx:
                    dma("sp", XTv[:, :, r0:r0 + 512], xT[:], ["xT"], ["XT"], "xtst")
                if is_ctx:
                    norm_mod(xT, "xT", hl, "hl", sq, "sq", rs, "rs", tmp, "tmp", A1, 0, 0, 2)
                else:
                    lb = (blk - 1) // 4
                    norm_mod(xT, "xT", hl, "hl", sq, "sq", rs, "rs", tmp, "tmp", A1, 0, 0, lb)
                p0 = 0 if is_ctx else ((blk - 1) % 4) * 512

                def proj(col0, ncols, b_, bk, rows=128):
                    for dc in range(8):
                        mm(b_[0:ncols, :], win[:, dc, col0:col0 + ncols], hl[:, dc, :], dc == 0, dc == 7, ["win", "hl"], [bk])

                for (nm, col0, nch, gain, dst, dstk, do) in (("cq", 0, 3, gqa, cqn, "cqn", not is_ctx),
                                                             ("ckv", OFF_CKV, 2, gkva, ckvn, "ckvn", True)):
                    if not do:
                        continue
                    for m in range(nch):
                        b_, bk = nb()
                        proj(col0 + m * 128, 128, b_, bk)
                        cp("dve", c32[:, m, :], b_[:], [bk], ["c32"])
                        act(sq[:, m, :], c32[:, m, :], AF.Square, ["c32"], ["sq"])
                    b_, bk = nb()
                    for m in range(nch):
                        mm(b_[:], ones_b[:], sq[:, m, :], m == 0, m == nch - 1, ["ones_b", "sq"], [bk])
                    rstd_from(b_[:], bk, rs2[:], "rs2", float(nch * 128))
                    for m in range(nch):
                        stt("dve", dst[:, m, :], c32[:, m, :], gain[:, m:m + 1], rs2[:], ALU.mult, ALU.mult,
                            ["c32", "rs2", "gqa", "gkva"], [dstk])
                bm_, bmk = nb()
                proj(OFF_KR, 32, bm_, bmk)
                if is_ctx:
                    cp("dve", krr[:], bm_[0:32, :], [bmk], ["krr"])
                else:
                    bs_, bsk = nb()
                    proj(OFF_KRS, 32, bs_, bsk)
                    tt("dve", tmp[0][0:32, :], bm_[0:32, :], tabM[:, 0, p0:p0 + 512], ALU.mult, [bmk, "tabM"], ["tmp0"])
                    tt("dve", tmp[1][0:32, :], bs_[0:32, :], tabM[:, 1, p0:p0 + 512], ALU.mult, [bsk, "tabM"], ["tmp1"])
                    tt("pool", krr[:], tmp[0][0:32, :], tmp[1][0:32, :], ALU.add, ["tmp0", "tmp1"], ["krr"])
                if not is_ctx:
                    for h in range(8):
                        bm_, bmk = nb()
                        bs_, bsk = nb()
                        for kc in range(3):
                            mm(bm_[0:96, :], wqb[:, kc, h * 96:(h + 1) * 96], cqn[:, kc, :], kc == 0, kc == 2, ["wqb", "cqn"], [bmk])
                        for kc in range(3):
                            mm(bs_[0:32, :], wqb[:, kc, 768 + h * 32:768 + (h + 1) * 32], cqn[:, kc, :], kc == 0, kc == 2,
                               ["wqb", "cqn"], [bsk])
                        s = h % 2
                        tt("dve", tmp[0][0:32, :], bm_[0:32, :], tabM[:, 0, p0:p0 + 512], ALU.mult, [bmk, "tabM"], ["tmp0"])
                        tt("dve", tmp[1][0:32, :], bs_[0:32, :], tabM[:, 1, p0:p0 + 512], ALU.mult, [bsk, "tabM"], ["tmp1"])
                        tt("pool", qo[s][0:32, :], tmp[0][0:32, :], tmp[1][0:32, :], ALU.add, ["tmp0", "tmp1"], [f"qo{s}"])
                        cp("act", qo[s][32:64, :], bm_[32:64, :], [bmk], [f"qo{s}b", bmk])
                        cp("act", qo[s][64:96, :], bm_[64:96, :], [bmk], [f"qo{s}c", bmk])
                        dma("sp", QM[h, :, r0:r0 + 512], qo[s][0:96, :], [f"qo{s}", f"qo{s}b", f"qo{s}c"], ["QM"], f"qst{s}")
                for h in range(8):
                    b_, bk = nb()
                    for kc in range(2):
                        mm(b_[0:96, :], wkn[:, kc, h * 96:(h + 1) * 96], ckvn[:, kc, :], kc == 0, False, ["wkn", "ckvn"], [bk])
                    mm(b_[0:96, :], sel32[:], krr[:], False, True, ["sel32", "krr"], [bk])
                    s = h % 2
                    evac_copy(h, qo[s][0:96, :], b_[0:96, :], [bk], [f"qo{s}", f"qo{s}b", f"qo{s}c"])
                    if is_ctx:
                        for lb2 in range(2):
                            dma("sp", KM[lb2, h, :, 0:256], qo[s][0:96, lb2 * 256:(lb2 + 1) * 256], [f"qo{s}", f"qo{s}b"], ["KM"], f"qst{s}")
                    else:
                        dma("sp", KM[lb, h, :, 256 + p0:256 + p0 + 512], qo[s][0:96, :], [f"qo{s}", f"qo{s}b"], ["KM"], f"qst{s}")
                for t4 in range(4):
                    b_, bk = nb()
                    for kc in range(2):
                        mm(b_[:], ckvn[:, kc, t4 * 128:(t4 + 1) * 128], wv[:, kc, :], kc == 0, kc == 1, ["ckvn", "wv"], [bk])
                    s = t4 % 2
                    evac_copy(t4, vst[s][:, :, 0:64], b_[:].rearrange("p (h e) -> p h e", h=8), [bk], [f"vst{s}"])
                    if is_ctx:
                        lb2, kk = t4 // 2, (t4 % 2) * 128
                    else:
                        lb2, kk = lb, 256 + p0 + t4 * 128
                    dma("sp", VM[lb2, kk:kk + 128, :, :], vst[s][:], [f"vst{s}"], ["VM"], f"vstq{s}")
                for (cm, cs, tq, is_q) in [(OFF_GQ + m * 128, OFF_GQS + m * 128, 0, True) for m in range(4)] + [(OFF_GK, OFF_GKS, 2, False)]:
                    if is_q and is_ctx:
                        continue
                    bm_, bmk = nb()
                    proj(cm, 128, bm_, bmk)
                    act(sq[:, 0, :], bm_[:], AF.Square, [bmk], ["sq", bmk])
                    bq_, bqk = nb()
                    mm(bq_[:], bdo[:], sq[:, 0, :], True, True, ["bdo", "sq"], [bqk])
                    rstd_from(bq_[:], bqk, rs2[:], "rs2", 64.0)
                    m = (cm - OFF_GQ) // 128 if is_q else 0
                    s = m % 2
                    if is_ctx:
                        stt("dve", qo[s][:], bm_[:], ghn[:, 2:3], rs2[:], ALU.mult, ALU.mult, [bmk, "rs2", "ghn"], [f"qo{s}", f"qo{s}b", f"qo{s}c"])
                    else:
                        bs_, bsk = nb()
                        proj(cs, 128, bs_, bsk)
                        tt("dve", tmp[0][:], bm_[:], tabG[:, tq, p0:p0 + 512], ALU.mult, [bmk, "tabG"], ["tmp0"])
                        tt("dve", tmp[1][:], bs_[:], tabG[:, tq + 1, p0:p0 + 512], ALU.mult, [bsk, "tabG"], ["tmp1"])
                        tt("pool", tmp[2][:], tmp[0][:], tmp[1][:], ALU.add, ["tmp0", "tmp1"], ["tmp2"])
                        tt("dve", qo[s][:], tmp[2][:], rs2[:], ALU.mult, ["tmp2", "rs2"], [f"qo{s}", f"qo{s}b", f"qo{s}c"])
                    if is_q:
                        dma("sp", QG[2 * m:2 * m + 2, :, r0:r0 + 512].rearrange("h d t -> (h d) t"), qo[s][:],
                            [f"qo{s}", f"qo{s}b"], ["QG"], f"qst{s}")
                    elif is_ctx:
                        for lb2 in range(2):
                            dma("sp", KG[lb2, :, :, 0:256].rearrange("h d t -> (h d) t"), qo[s][:, lb2 * 256:(lb2 + 1) * 256],
                                [f"qo{s}", f"qo{s}b"], ["KG"], f"qst{s}")
                    else:
                        dma("sp", KG[lb, :, :, 256 + p0:256 + p0 + 512].rearrange("h d t -> (h d) t"), qo[s][:],
                            [f"qo{s}", f"qo{s}b"], ["KG"], f"qst{s}")
                for t4 in range(4):
                    b_, bk = nb()
                    for dc in range(8):
                        mm(b_[:, 0:128], hl[:, dc, t4 * 128:(t4 + 1) * 128], win[:, dc, OFF_GV:OFF_GV + 128], dc == 0, dc == 7,
                           ["hl", "win"], [bk])
                    s = t4 % 2
                    evac_copy(t4 + 1, vgs[s][:, :, 0:64], b_[:, 0:128].rearrange("p (h e) -> p h e", h=2), [bk], [f"vgs{s}"])
                    if is_ctx:
                        lb2, kk = t4 // 2, (t4 % 2) * 128
                    else:
                        lb2, kk = lb, 256 + p0 + t4 * 128
                    dma("sp", VG[lb2, kk:kk + 128, :, :], vgs[s][:], [f"vgs{s}"], ["VG"], f"vgsq{s}")
            P.flush()

        P.skip = not want('D')
        with ExitStack() as st:
            kms = SB(st, "kms", [96, 8, NK], BF16)
            kgs = SB(st, "kgs", [64, 2, NK], BF16)
            vms = SB(st, "vms", [128, 18, 8, 128], BF16)
            vgs2 = SB(st, "vgs2", [128, 18, 2, 128], BF16)
            wo = SB(st, "wo", [64, 16, D], BF16)
            qms = [SB(st, f"qms{i}", [96, 8, 512], BF16) for i in range(2)]
            qgs = [SB(st, f"qgs{i}", [64, 8, 512], BF16) for i in range(2)]
            pT = [SB(st, f"pT{i}", [128, 512], BF16) for i in range(5)]
            pcount = [0]
            rec = [SB(st, f"rec{i}", [128, 512], F32) for i in range(2)]
            on = SB(st, "on", [64, 16, 512], BF16)
            xo = [SB(st, f"xo{i}", [128, 512], F32) for i in range(2)]
            dma("sp", wo[:], W_ao.rearrange("(h d) n -> d h n", d=64), ["W_ao"], ["wo"], "d0")
            psb_f = [ps[4], ps[5]]
            b4 = [0]

            def nb4():
                i = b4[0] % 4
                b4[0] += 1
                return ps[i], f"ps{i}"
            SC_M = 1.0 / math.sqrt(96.0)
            SC_G = 0.125
            pi = 0
            for lb in range(2):
                dma("sp", kms[:], KM[lb].rearrange("h d t -> d h t"), ["KM"], ["kms"], "d1")
                dma("sp", kgs[:], KG[lb].rearrange("h d t -> d h t"), ["KG"], ["kgs"], "d1")
                dma("sp", vms[:], VM[lb].rearrange("(c p) h e -> p c h e", p=128), ["VM"], ["vms"], "d1")
                dma("sp", vgs2[:], VG[lb].rearrange("(c p) h e -> p c h e", p=128), ["VG"], ["vgs2"], "d1")
                for qb in range(4):
                    t0 = lb * L + qb * 512
                    s = qb % 2
                    dma("sp", qms[s][:], QM[:, :, t0:t0 + 512].rearrange("h d t -> d h t"), ["QM"], [f"qms{s}"], f"qld{s}")
                    dma("sp", qgs[s][:], QG[:, :, t0:t0 + 512].rearrange("h d t -> d h t"), ["QG"], [f"qgs{s}"], f"qld{s}")
                    items = [(h, kc) for h in range(16) for kc in range(18)]
                    sbank = {}

                    def S_E(i):
                        h, kc = items[i]
                        b_, bk = nb4()
                        if h < 8:
                            mm(b_[:], kms[:, h, kc * 128:(kc + 1) * 128], qms[s][:, h, :], True, True, ["kms", f"qms{s}"], [bk])
                            sc = SC_M
                        else:
                            g = (h - 8) // 4
                            mm(b_[:], kgs[:, g, kc * 128:(kc + 1) * 128], qgs[s][:, h - 8, :], True, True, ["kgs", f"qgs{s}"], [bk])
                            sc = SC_G
                        pp = pcount[0] % 5
                        pcount[0] += 1
                        act(pT[pp][:], b_[:], AF.Exp, [bk], [f"pT{pp}"], scale=sc)
                        sbank[i] = pp

                    def PV(i):
                        h, kc = items[i]
                        oa, oak = psb_f[h % 2], f"ps{4 + h % 2}"
                        pp = sbank.pop(i)
                        if h < 8:
                            va, vk = vms[:, kc, h, :], "vms"
                        else:
                            va, vk = vgs2[:, kc, (h - 8) // 4, :], "vgs2"
                        mm(oa[:], va, pT[pp][:], kc == 0, kc == 17, [vk, f"pT{pp}"], [oak])
                        if kc == 17:
                            rr = h % 2
                            P.op("dve", lambda e, o=rec[rr][64:128, :], i_=oa[64:128, :]: e.reciprocal(o, i_), reads=[oak], writes=[f"rec{rr}"])
                            tt("dve", on[:, h, :], oa[0:64, :], rec[rr][64:128, :], ALU.mult, [oak, f"rec{rr}"], ["on"])
                    LA = 3
                    for i in range(LA):
                        S_E(i)
                    for i in range(len(items)):
                        if i + LA < len(items):
                            S_E(i + LA)
                        PV(i)
                    for dc in range(8):
                        b_, bk = nb4()
                        for h in range(16):
                            mm(b_[:], wo[:, h, dc * 128:(dc + 1) * 128], on[:, h, :], h == 0, h == 15, ["wo", "on"], [bk])
                        xs = dc % 2
                        dma("sp", xo[xs][:], XT[dc * 128:(dc + 1) * 128, t0:t0 + 512], ["XT"], [f"xo{xs}"], f"xold{xs}")
                        stt("dve", xo[xs][:], b_[:], MODV(0, 2, dc, lb), xo[xs][:], ALU.mult, ALU.add, [bk, "modT", f"xo{xs}"], [f"xo{xs}"])
                        dma("sp", XT[dc * 128:(dc + 1) * 128, t0:t0 + 512], xo[xs][:], [f"xo{xs}"], ["XT"], f"xost{xs}")
            P.flush()
        P.skip = False
        if dbg:
            dma("sp", DBG["xa0"], XT, ["XT"], [], "dbg")

        def mlp_phase(i, final=False):
            with ExitStack() as st:
                xT = [SB(st, f"mxT{k}", [128, 8, 512], F32) for k in range(2)]
                if final:
                    fsq = SB(st, "mfsq", [128, 8, 512], BF16)
                    frs = SB(st, "mfrs", [128, 512], F32)
                    yT = SB(st, "myT", [128, 8, 512], F32)
                    orow = [SB(st, f"morow{k}", [128, D], F32) for k in range(2)]
                pend = []

                def fin_a(xs_, t0_):
                    b_, bk = nb()
                    for dc in range(8):
                        mm(b_[:], ones_b[:], fsq[:, dc, :], dc == 0, dc == 7, ["ones_b", "mfsq"], [bk])
                    rstd_from(b_[:], bk, frs[:], "mfrs", 1024.0)
                    for dc in range(8):
                        stt("dve", yT[:, dc, :], xT[xs_][:, dc, :], gfin[:, dc:dc + 1], frs[:], ALU.mult, ALU.mult,
                            [f"mxT{xs_}", "gfin", "mfrs"], ["myT"])

                def fin_b(xs_, t0_):
                    for t4 in range(4):
                        os_ = t4 % 2
                        for dc in range(8):
                            b_, bk = nb()
                            P.op("pe", lambda e, o=b_[:, 0:128], i_=yT[:, dc, t4 * 128:(t4 + 1) * 128]: e.transpose(o, i_, ident_f[:]),
                                 reads=["myT", "ident_f"], writes=[bk])
                            cp("act" if dc % 2 else "dve", orow[os_][:, dc * 128:(dc + 1) * 128], b_[:, 0:128], [bk], [f"morow{os_}"])
                        dma("sp", OUT[t0_ + t4 * 128:t0_ + (t4 + 1) * 128, :], orow[os_][:], [f"morow{os_}"], [], "mfo")

                sq = SB(st, "msq", [128, 8, 512], BF16)
                xn = SB(st, "mxn", [128, 8, 512], BF16)
                rs = SB(st, "mrs", [128, 512], F32)
                tmp = [SB(st, f"mtmp{k}", [128, 512], F32) for k in range(2)]
                aT = SB(st, "maT", [128, 32, 512], BF16)
                w1t = [SB(st, f"w1t{k}", [128, 8, 512], BF16) for k in range(3)]
                w2t = [SB(st, f"w2t{k}", [128, 4, 512], BF16) for k in range(3)]
                r1 = [SB(st, f"mr1{k}", [128, 512], F32) for k in range(2)]
                w1v = W_1[i].rearrange("(kc p) n -> p kc n", p=128)
                w2v = W_2[i].rearrange("(fc p) n -> p fc n", p=128)
                n1 = 0
                n2 = 0
                for tb in range(8):
                    lb = tb // 4
                    t0 = tb * 512
                    xs = tb % 2
                    load_xT(xT[xs], f"mxT{xs}", t0, f"mxld{xs}")
                    norm_mod(xT[xs], f"mxT{xs}", xn, "mxn", sq, "msq", rs, "mrs", tmp, "mtmp", A2, i, 3, lb)
                    if final and pend:
                        fin_a(*pend[0])
                    for g in range(8):
                        ws = n1 % 3
                        n1 += 1
                        dma("sp", w1t[ws][:], w1v[:, :, g * 512:(g + 1) * 512], [f"W_1_{i}"], [f"w1t{ws}"], f"w1q{ws}")
                        for m in range(4):
                            b_, bk = nb()
                            for kc in range(8):
                                mm(b_[:], w1t[ws][:, kc, m * 128:(m + 1) * 128], xn[:, kc, :], kc == 0, kc == 7, [f"w1t{ws}", "mxn"], [bk])
                            fc = g * 4 + m
                            rk = fc % 2
                            act(r1[rk][:], b_[:], AF.Relu, [bk], [f"mr1{rk}"])
                            tt("dve" if fc % 4 else "pool", aT[:, fc, :], r1[rk][:], r1[rk][:], ALU.mult, [f"mr1{rk}"], ["maT"])
                    if final and pend:
                        fin_b(*pend.pop(0))
                    for half in range(2):
                        bks = [nb() for _ in range(4)]
                        for g in range(8):
                            ws = n2 % 3
                            n2 += 1
                            dma("sp", w2t[ws][:], w2v[:, g * 4:(g + 1) * 4, half * 512:(half + 1) * 512], [f"W_2_{i}"], [f"w2t{ws}"], f"w2q{ws}")
                            for f4 in range(4):
                                fc = g * 4 + f4
                                for m in range(4):
                                    mm(bks[m][0][:], w2t[ws][:, f4, m * 128:(m + 1) * 128], aT[:, fc, :], fc == 0, fc == 31,
                                       [f"w2t{ws}", "maT"], [bks[m][1]])
                        for m in range(4):
                            dc = half * 4 + m
                            stt("dve", xT[xs][:, dc, :], bks[m][0][:], MODV(i, 5, dc, lb), xT[xs][:, dc, :], ALU.mult, ALU.add,
                                [bks[m][1], "modT", f"mxT{xs}"], [f"mxT{xs}"])
                    if final:
                        for dc in range(8):
                            act(fsq[:, dc, :], xT[xs][:, dc, :], AF.Square, [f"mxT{xs}"], ["mfsq"])
                        pend.append((xs, t0))
                    else:
                        dma("sp", XTv[:, :, t0:t0 + 512], xT[xs][:], [f"mxT{xs}"], ["XT"], f"mxst{xs}")
                if final:
                    fin_a(*pend[0])
                    fin_b(*pend.pop(0))
                P.flush()

        def hyena_phase():
            S2 = 2.0 / 4096.0
            MAGIC = 12582912.0
            TWO_PI = 2.0 * math.pi
            with ExitStack() as st:
                feats = SB(st, "feats", [17, L], F32)
                fw1 = SB(st, "fw1", [17, 64], F32)
                fw2 = SB(st, "fw2", [64, 64], F32)
                fw3 = SB(st, "fw3", [64, 64], F32)
                fw4 = SB(st, "fw4", [64, 4096], F32)
                fb = SB(st, "fb", [64, 3], F32)
                ffr = SB(st, "ffr", [64, 3], F32)
                fbf = SB(st, "fbf", [64, 3], F32)
                ones_f = SB(st, "ones_f", [128, 128], F32)
                hbuf = [SB(st, f"hbuf{k}", [64, L], F32) for k in range(2)]
                fa = SB(st, "fa", [64, L], F32)
                fk = SB(st, "fk", [64, L], F32)
                wint = [SB(st, f"wint{k}", [128, 1024], F32) for k in range(2)]
                hw = [SB(st, f"hw{k}", [128, 2048], F32) for k in range(2)]
                sqt = SB(st, "sqt", [128, 2048], F32)
                acc = SB(st, "acc", [128, 2048], F32)
                Gs = SB(st, "Gs", [128, 16, 1024], BF16)
                Hs = SB(st, "Hs", [128, 16, 1024], BF16)
                rs2 = SB(st, "frs2", [128, 1024], F32)
                sk2 = SB(st, "fsk2", [128, 1024], F32)
                ktmp = [SB(st, f"ktmp{k}", [128, 1024], F32) for k in range(3)]
                fct = [SB(st, f"fct{k}", [128, 16, 128], BF16) for k in range(2)]
                fst = [SB(st, f"fst{k}", [128, 16, 128], BF16) for k in range(2)]
                dma("sp", feats[:], IN["featsT"], [], ["feats"], "f")
                dma("sp", fw1[:], IN["h_f_w1"], [], ["fw1"], "f")
                dma("sp", fw2[:], IN["h_f_w2"], [], ["fw2"], "f")
                dma("sp", fw3[:], IN["h_f_w3"], [], ["fw3"], "f")
                dma("sp", fw4[:], IN["h_f_w4"], [], ["fw4"], "f")
                dma("sp", fb[:], IN["h_f_b"], [], ["fb"], "f")
                dma("sp", ffr[:], IN["h_f_fr"], [], ["ffr"], "f")
                memset("dve", ones_f[:], 1.0, ["ones_f"])
                tt("dve", fbf[:], fb[:], ffr[:], ALU.mult, ["fb", "ffr"], ["fbf"])
                src, srck, kdim = feats, "feats", 17
                wl = [(fw1, "fw1"), (fw2, "fw2"), (fw3, "fw3")]
                for l in range(3):
                    dst, dstk = hbuf[l % 2], f"hbuf{l % 2}"
                    for q in range(4):
                        b_, bk = nb()
                        mm(b_[0:64, :], wl[l][0][0:kdim, :], src[0:kdim, q * 512:(q + 1) * 512], True, True, [wl[l][1], srck], [bk])
                        act(fa[:, q * 512:(q + 1) * 512], b_[0:64, :], AF.Identity, [bk, "ffr", "fbf"], ["fa"],
                            bias=fbf[:, l:l + 1], scale=ffr[:, l:l + 1])
                    ts("dve", fk[:], fa[:], 1.0 / TWO_PI, 0.0, ALU.mult, ALU.add, ["fa"], ["fk"])
                    ts("dve", fk[:], fk[:], MAGIC, 0.0, ALU.add, ALU.add, ["fk"], ["fk"])
                    ts("dve", fk[:], fk[:], -MAGIC, 0.0, ALU.add, ALU.add, ["fk"], ["fk"])
                    stt("dve", fa[:], fk[:], -TWO_PI, fa[:], ALU.mult, ALU.add, ["fk", "fa"], ["fa"])
                    act(dst[:], fa[:], AF.Sin, ["fa"], [dstk], scale=0.9999995)
                    src, srck, kdim = dst, dstk, 64
                h3, h3k = src, srck
                for o in range(2):
                    memset("pool", acc[:], 0.0, ["acc"])
                    for tti in range(16):
                        ws = tti % 2
                        dma("sp", wint[ws][:], IN["window"][tti * 128:(tti + 1) * 128, :], [], [f"wint{ws}"], "f")
                        hs = tti % 2
                        for d_ in range(2):
                            for hf_ in range(2):
                                b_, bk = nb()
                                c0 = d_ * 2048 + o * 1024 + hf_ * 512
                                mm(b_[:], h3[:, tti * 128:(tti + 1) * 128], fw4[:, c0:c0 + 512], True, True, [h3k, "fw4"], [bk])
                                tt("dve", hw[hs][:, d_ * 1024 + hf_ * 512:d_ * 1024 + (hf_ + 1) * 512], b_[:],
                                   wint[ws][:, hf_ * 512:(hf_ + 1) * 512], ALU.mult, [bk, f"wint{ws}"], [f"hw{hs}"])
                        tt("pool", sqt[:], hw[hs][:], hw[hs][:], ALU.mult, [f"hw{hs}"], ["sqt"])
                        tt("pool", acc[:], acc[:], sqt[:], ALU.add, ["acc", "sqt"], ["acc"])
                        if tti == 0:
                            memset("dve", hw[hs][0:1, 1024:2048], 0.0, [f"hw{hs}"])
                        tt("pool", Gs[:, tti, :], hw[hs][:, 0:1024], hw[hs][:, 1024:2048], ALU.add, [f"hw{hs}"], ["Gs"])
                        tt("dve", Hs[:, tti, :], hw[hs][:, 1024:2048], hw[hs][:, 0:1024], ALU.subtract, [f"hw{hs}"], ["Hs"])
                    for hf_ in range(2):
                        b0, b0k = nb()
                        b1, b1k = nb()
                        mm(b0[:], ones_f[:], acc[:, hf_ * 512:(hf_ + 1) * 512], True, True, ["ones_f", "acc"], [b0k])
                        mm(b1[:], ones_f[:], acc[:, 1024 + hf_ * 512:1024 + (hf_ + 1) * 512], True, True, ["ones_f", "acc"], [b1k])
                        cp("dve", ktmp[0][:, 0:512], b0[:], [b0k], ["ktmp0"])
                        tt("dve", ktmp[0][:, 0:512], ktmp[0][:, 0:512], b1[:], ALU.add, ["ktmp0", b1k], ["ktmp0"])
                        act(rs2[:, hf_ * 512:(hf_ + 1) * 512], ktmp[0][:, 0:512], AF.Sqrt, ["ktmp0", "eps_t"], ["frs2"], bias=eps_t[:], scale=1.0)
                    P.op("dve", lambda e, o_=rs2[:]: e.reciprocal(o_, o_), reads=["frs2"], writes=["frs2"])
                    ts("dve", rs2[:], rs2[:], S2, 0.0, ALU.mult, ALU.add, ["frs2"], ["frs2"])
                    dma("sp", sk2[:], IN["h_skip_b"][:, o * 1024:(o + 1) * 1024], [], ["fsk2"], "f")
                    ts("dve", sk2[:], sk2[:], S2, 0.0, ALU.mult, ALU.add, ["fsk2"], ["fsk2"])
                    for fc in range(16):
                        s_ = fc % 2
                        dma("sp", fct[s_][:], IN["FCt"][fc], [], [f"fct{s_}"], "f")
                        dma("sp", fst[s_][:], IN["FSt"][fc], [], [f"fst{s_}"], "f")
                        jobs = [(fct[s_], f"fct{s_}", Gs, "Gs", 0), (fst[s_], f"fst{s_}", Hs, "Hs", 1)]
                        if fc == 0:
                            jobs.append((fst[s_], f"fst{s_}", Gs, "Gs", 2))
                        for (mt, mk, dat, dk_, kind) in jobs:
                            kt, ktk = ktmp[kind], f"ktmp{kind}"
                            for hf_ in range(2):
                                b_, bk = nb()
                                for tti in range(16):
                                    mm(b_[:], mt[:, tti, :], dat[:, tti, hf_ * 512:(hf_ + 1) * 512], tti == 0, tti == 15, [mk, dk_], [bk])
                                tt("dve", kt[:, hf_ * 512:(hf_ + 1) * 512], b_[:], rs2[:, hf_ * 512:(hf_ + 1) * 512], ALU.mult,
                                   [bk, "frs2"], [ktk])
                            if kind != 1:
                                tt("pool", kt[:], kt[:], sk2[:], ALU.add, [ktk, "fsk2"], [ktk])
                            if fc == 0:
                                if kind == 1:
                                    memset("dve", kt[0:1, :], 0.0, [ktk])
                                else:
                                    ts("dve", kt[0:1, :], kt[0:1, :], 0.5, 0.0, ALU.mult, ALU.add, [ktk], [ktk])
                            if kind == 0:
                                dma("sp", KA[fc * 128:(fc + 1) * 128, o * 1024:(o + 1) * 1024], kt[:], [ktk], ["KA"], "f")
                            elif kind == 1:
                                dma("sp", KB[fc * 128:(fc + 1) * 128, o * 1024:(o + 1) * 1024], kt[:], [ktk], ["KB"], "f")
                            else:
                                dma("sp", KD0[0:1, o * 1024:(o + 1) * 1024], kt[0:1, :], [ktk], ["KD0"], "f")
                P.flush()

            for lb in range(2):
                with ExitStack() as bst:
                    u = SB(bst, "u", [128, 16, 1024], BF16)
                    with ExitStack() as st:
                        hlT = SB(st, "hlT", [128, 8, L], BF16)
                        xT = SB(st, "hxT", [128, 8, 512], F32)
                        sq = SB(st, "hsq", [128, 8, 512], BF16)
                        rs = SB(st, "hrs", [128, 512], F32)
                        tmp = [SB(st, f"htmp{k}", [128, 512], F32) for k in range(2)]
                        zbuf = [SB(st, f"zbuf{k}", [128, L + 2], F32) for k in range(2)]
                        ztmp = SB(st, "ztmp", [128, L], F32)
                        zc = [SB(st, f"zc{k}", [128, L], BF16) for k in range(2)]
                        whi = [SB(st, f"whi{k}", [128, 8, 128], BF16) for k in range(2)]
                        cw = SB(st, "cw", [128, 24, 3], F32)
                        cb = SB(st, "cb", [128, 24], F32)
                        dma("sp", cw[:], IN["h_conv_wT"], [], ["cw"], "h")
                        dma("sp", cb[:], IN["h_conv_bT"], [], ["cb"], "h")
                        for k in range(2):
                            memset("pool", zbuf[k][:], 0.0, [f"zbuf{k}"])
                        for tb in range(4):
                            t0 = lb * L + tb * 512
                            load_xT(xT, "hxT", t0, "h")
                            norm_mod(xT, "hxT", None, "hlT", sq, "hsq", rs, "hrs", tmp, "htmp", A1, 1, 0, lb,
                                     hl_of=lambda dc, tb=tb: hlT[:, dc, tb * 512:(tb + 1) * 512])
                        whv = W_hi.rearrange("(dc p) n -> p dc n", p=128)
                        for cc in range(24):
                            s_ = cc % 2
                            dma("sp", whi[s_][:], whv[:, :, cc * 128:(cc + 1) * 128], ["W_hi"], [f"whi{s_}"], "h")
                            for tb in range(4):
                                b_, bk = nb()
                                for dc in range(8):
                                    mm(b_[:], whi[s_][:, dc, :], hlT[:, dc, tb * 512:(tb + 1) * 512], dc == 0, dc == 7, [f"whi{s_}", "hlT"], [bk])
                                cp("act", zbuf[s_][:, 1 + tb * 512:1 + (tb + 1) * 512], b_[:], [bk], [f"zbuf{s_}"])
                            ts("dve", ztmp[:], zbuf[s_][:, 0:L], cw[:, cc, 0:1], cb[:, cc:cc + 1], ALU.mult, ALU.add,
                               [f"zbuf{s_}", "cw", "cb"], ["ztmp"])
                            stt("dve", ztmp[:], zbuf[s_][:, 1:L + 1], cw[:, cc, 1:2], ztmp[:], ALU.mult, ALU.add,
                                [f"zbuf{s_}", "cw", "ztmp"], ["ztmp"])
                            stt("dve", zc[s_][:], zbuf[s_][:, 2:L + 2], cw[:, cc, 2:3], ztmp[:], ALU.mult, ALU.add,
                                [f"zbuf{s_}", "cw", "ztmp"], [f"zc{s_}"])
                            if cc < 16:
                                dma("sp", X12T[cc * 128:(cc + 1) * 128, lb * L:(lb + 1) * L], zc[s_][:], [f"zc{s_}"], ["X12T"], "h")
                            else:
                                vc = cc - 16
                                for g8 in range(2):
                                    pb, pbk = psb[g8], f"psb{g8}"
                                    for j in range(8):
                                        blk = g8 * 8 + j
                                        P.op("pe", lambda e, o_=pb[:, j * 128:(j + 1) * 128], i_=zc[s_][:, blk * 128:(blk + 1) * 128]:
                                             e.transpose(o_, i_, ident_b[:]), reads=[f"zc{s_}", "ident_b"], writes=[pbk])
                                    cp("act" if g8 else "dve", u[:, g8 * 8:(g8 + 1) * 8, vc * 128:(vc + 1) * 128],
                                       pb[:].rearrange("p (a b) -> p a b", a=8), [pbk], ["u"])
                        P.flush()
                    with ExitStack() as cst:
                        Yre = SB(cst, "Yre", [128, 16, 1024], BF16)
                        Yz = SB(cst, "Yz", [128, 16, 1024], BF16)
                        for o in range(2):
                            with ExitStack() as st:
                                fct = [SB(st, f"cfct{k}", [128, 16, 128], BF16) for k in range(2)]
                                fst = [SB(st, f"cfst{k}", [128, 16, 128], BF16) for k in range(2)]
                                kat = [SB(st, f"kat{k}", [128, 1024], F32) for k in range(2)]
                                kbt = [SB(st, f"kbt{k}", [128, 1024], F32) for k in range(2)]
                                kdt = SB(st, "kdt", [128, 1024], F32)
                                t1 = [SB(st, f"ct{k}", [128, 512], F32) for k in range(4)]
                                for fc in range(16):
                                    s_ = fc % 2
                                    dma("sp", fct[s_][:], IN["FCt"][fc], [], [f"cfct{s_}"], "h")
                                    dma("sp", fst[s_][:], IN["FSt"][fc], [], [f"cfst{s_}"], "h")
                                    dma("sp", kat[s_][:], KA[fc * 128:(fc + 1) * 128, o * 1024:(o + 1) * 1024], ["KA"], [f"kat{s_}"], "h")
                                    dma("sp", kbt[s_][:], KB[fc * 128:(fc + 1) * 128, o * 1024:(o + 1) * 1024], ["KB"], [f"kbt{s_}"], "h")
                                    if fc == 0:
                                        cp("pool", kdt[:], kat[s_][:], [f"kat{s_}"], ["kdt"])
                                        dma("sp", kdt[0:1, :], KD0[0:1, o * 1024:(o + 1) * 1024], ["KD0"], ["kdt"], "h")
                                        dd, ddk = kdt, "kdt"
                                    else:
                                        dd, ddk = kat[s_], f"kat{s_}"
                                    for hf_ in range(2):
                                        bc, bck = nb()
                                        bs, bsk = nb()
                                        cs = slice(hf_ * 512, (hf_ + 1) * 512)
                                        for tti in range(16):
                                            mm(bc[:], fct[s_][:, tti, :], u[:, tti, cs], tti == 0, tti == 15, [f"cfct{s_}", "u"], [bck])
                                        for tti in range(16):
                                            mm(bs[:], fst[s_][:, tti, :], u[:, tti, cs], tti == 0, tti == 15, [f"cfst{s_}", "u"], [bsk])
                                        tt("dve", t1[0][:], bc[:], kat[s_][:, cs], ALU.mult, [bck, f"kat{s_}"], ["ct0"])
                                        tt("dve", t1[1][:], bs[:], kbt[s_][:, cs], ALU.mult, [bsk, f"kbt{s_}"], ["ct1"])
                                        tt("pool", Yre[:, fc, cs], t1[0][:], t1[1][:], ALU.add, ["ct0", "ct1"], ["Yre"])
                                        tt("dve", t1[2][:], bs[:], dd[:, cs], ALU.mult, [bsk, ddk], ["ct2"])
                                        tt("dve", t1[3][:], bc[:], kbt[s_][:, cs], ALU.mult, [bck, f"kbt{s_}"], ["ct3"])
                                        tt("pool", Yz[:, fc, cs], t1[2][:], t1[3][:], ALU.subtract, ["ct2", "ct3"], ["Yz"])
                                P.flush()
                            with ExitStack() as st:
                                fcw = SB(st, "fcw", [128, 16, 512], BF16)
                                fsw = SB(st, "fsw", [128, 16, 512], BF16)
                                gate = SB(st, "gate", [128, 8, 512], BF16)
                                y2T = SB(st, "y2T", [128, 8, 512], BF16)
                                if o == 1:
                                    who = SB(st, "who", [128, 8, D], BF16)
                                    xo = [SB(st, f"hxo{k}", [128, 512], F32) for k in range(2)]
                                    dma("sp", who[:], W_ho.rearrange("(c p) n -> p c n", p=128), ["W_ho"], ["who"], "h")
                                gv_ = X12T[o * 1024:(o + 1) * 1024, :].rearrange("(cc p) t -> p cc t", p=128)
                                for nbk in range(4):
                                    t0 = lb * L + nbk * 512
                                    dma("sp", fcw[:], IN["FCw"][nbk], [], ["fcw"], "h")
                                    dma("sp", fsw[:], IN["FSTw"][nbk], [], ["fsw"], "h")
                                    dma("sp", gate[:], gv_[:, :, t0:t0 + 512], ["X12T"], ["gate"], "h")
                                    for cc in range(8):
                                        b_, bk = nb()
                                        for fc in range(16):
                                            mm(b_[:], Yre[:, fc, cc * 128:(cc + 1) * 128], fcw[:, fc, :], fc == 0, False, ["Yre", "fcw"], [bk])
                                        for fc in range(16):
                                            mm(b_[:], Yz[:, fc, cc * 128:(cc + 1) * 128], fsw[:, fc, :], False, fc == 15, ["Yz", "fsw"], [bk])
                                        tt("dve", y2T[:, cc, :], b_[:], gate[:, cc, :], ALU.mult, [bk, "gate"], ["y2T"])
                                    if o == 0:
                                        for j in range(4):
                                            pb, pbk = psb[j % 2], f"psb{j % 2}"
                                            for cc in range(8):
                                                P.op("pe", lambda e, o_=pb[:, cc * 128:(cc + 1) * 128], i_=y2T[:, cc, j * 128:(j + 1) * 128]:
                                                     e.transpose(o_, i_, ident_b[:]), reads=["y2T", "ident_b"], writes=[pbk])
                                            cp("act" if j % 2 else "dve", u[:, nbk * 4 + j, :], pb[:], [pbk], ["u"])
                                    else:
                                        for dc in range(8):
                                            b_, bk = nb()
                                            for cc in range(8):
                                                mm(b_[:], who[:, cc, dc * 128:(dc + 1) * 128], y2T[:, cc, :], cc == 0, cc == 7, ["who", "y2T"], [bk])
                                            xs = dc % 2
                                            dma("sp", xo[xs][:], XT[dc * 128:(dc + 1) * 128, t0:t0 + 512], ["XT"], [f"hxo{xs}"], "h")
                                            stt("dve", xo[xs][:], b_[:], MODV(1, 2, dc, lb), xo[xs][:], ALU.mult, ALU.add,
                                                [bk, "modT", f"hxo{xs}"], [f"hxo{xs}"])
                                            dma("sp", XT[dc * 128:(dc + 1) * 128, t0:t0 + 512], xo[xs][:], [f"hxo{xs}"], ["XT"], "h")
                                P.flush()

        P.skip = not want('M')
        mlp_phase(0)
        P.skip = False
        if dbg:
            dma("sp", DBG["xm0"], XT, ["XT"], [], "dbg")
        if (phases is None and stage >= 2) or (phases is not None and 'H' in phases):
            hyena_phase()
        if dbg:
            dma("sp", DBG["xa1"], XT, ["XT"], [], "dbg")
        fused_final = False
        if (phases is None and stage >= 2) or (phases is not None and 'N' in phases):
            mlp_phase(1, final=True)
            fused_final = True

        P.skip = fused_final
        with ExitStack() as st:
            xT = [SB(st, f"fxT{k}", [128, 8, 512], F32) for k in range(2)]
            sq = SB(st, "fsq", [128, 8, 512], BF16)
            rs = SB(st, "frs", [128, 512], F32)
            yT = SB(st, "fyT", [128, 8, 512], F32)
            orow = [SB(st, f"forow{k}", [128, D], F32) for k in range(2)]
            for tb in range(8):
                t0 = tb * 512
                xs = tb % 2
                load_xT(xT[xs], f"fxT{xs}", t0, f"fxld{xs}")
                for dc in range(8):
                    act(sq[:, dc, :], xT[xs][:, dc, :], AF.Square, [f"fxT{xs}"], ["fsq"])
                b_, bk = nb()
                for dc in range(8):
                    mm(b_[:], ones_b[:], sq[:, dc, :], dc == 0, dc == 7, ["ones_b", "fsq"], [bk])
                rstd_from(b_[:], bk, rs[:], "frs", 1024.0)
                for dc in range(8):
                    stt("dve", yT[:, dc, :], xT[xs][:, dc, :], gfin[:, dc:dc + 1], rs[:], ALU.mult, ALU.mult,
                        [f"fxT{xs}", "gfin", "frs"], ["fyT"])
                for t4 in range(4):
                    os_ = t4 % 2
                    for dc in range(8):
                        b_, bk = nb()
                        P.op("pe", lambda e, o=b_[:, 0:128], i=yT[:, dc, t4 * 128:(t4 + 1) * 128]: e.transpose(o, i, ident_f[:]),
                             reads=["fyT", "ident_f"], writes=[bk])
                        cp("act" if dc % 2 else "dve", orow[os_][:, dc * 128:(dc + 1) * 128], b_[:, 0:128], [bk], [f"forow{os_}"])
                    dma("sp", OUT[t0 + t4 * 128:t0 + (t4 + 1) * 128, :], orow[os_][:], [f"forow{os_}"], [], f"fost{os_}")
            P.flush()
        P.skip = False
        P.wait_all_dma("sp")
        P.flush(barrier=False)
    return nc


_NC = {}


def _run(inputs, dbg=False, stage=2, phases=None):
    per = _prep(inputs)
    shapes = {k: (v.shape, "bf16" if v.dtype == NPBF else "f32") for k, v in per[0].items()}
    key = (dbg, stage, phases)
    if key not in _NC:
        _NC[key] = build(shapes, dbg=dbg, stage=stage, phases=phases)
    res = run_bass_kernel_spmd(_NC[key], per, core_ids=list(range(8)))
    return res


def kernel(**inputs):
    res = _run(inputs)
    out = np.concatenate([np.asarray(r["out"], np.float32).reshape(2, L, D) for r in res.results], axis=0)
    return out
```
